# Optimizing a Trainium2 kernel written in Bass

```python
import math
import jax, jax.numpy as jnp
from jax import lax
import numpy as np

D_MODEL = 2048
BATCH = 4
SEQ = 2048
DEPTH = 2
DEC_BATCH = 8
DEC_SEQ = 64
PAST_LEN = 1024

CHUNK = 64
BLOCK = 16
N_EVEN = (DEPTH + 1) // 2
N_ODD = DEPTH // 2

H_A = 4
DK_A = 128
DV_A = 256
ROPE_BASE = 10000.0
H_B = 8
DK_B = 128
DV_B = 128
H_C = 4
DK_C = 256
DV_C = 512
GK_RANK = 16
GATE_NORMALIZER = 16.0
D_FF = 5504
CONV_W = 3
EPS = 1e-6

QA = H_A * DK_A
VA = H_A * DV_A
QB = H_B * DK_B
VB = H_B * DV_B
D_IN_EVEN = 2 * QA + 2 * VA + 2 * QB + 2 * VB
D_MIX_EVEN = VA + VB
SPLITS_EVEN = (QA, 2 * QA, 2 * QA + VA, 2 * QA + 2 * VA,
               2 * QA + 2 * VA + QB, 2 * QA + 2 * VA + 2 * QB, 2 * QA + 2 * VA + 2 * QB + VB)
QC = H_C * DK_C
VC = H_C * DV_C
D_IN_ODD = 2 * QC + 2 * VC + GK_RANK
SPLITS_ODD = (QC, 2 * QC, 2 * QC + VC, 2 * QC + 2 * VC)

kernel_name = "hybrid_retention_hgrn2_gla_convglu_stream_step"


def rmsnorm(x, w):
    xf = x.astype(jnp.float32)
    y = xf * lax.rsqrt(jnp.mean(xf * xf, axis=-1, keepdims=True) + EPS)
    return (y * w.astype(jnp.float32)).astype(x.dtype)


def head_groupnorm(x):
    xf = x.astype(jnp.float32)
    mu = jnp.mean(xf, axis=-1, keepdims=True)
    var = jnp.mean(jnp.square(xf - mu), axis=-1, keepdims=True)
    return ((xf - mu) * lax.rsqrt(var + EPS)).astype(x.dtype)


def rotary(x, pos):
    half = x.shape[-1] // 2
    inv = ROPE_BASE ** (-jnp.arange(half, dtype=jnp.float32) / half)
    ang = pos.astype(jnp.float32)[:, None] * inv[None, :]
    cos = jnp.cos(ang)[None, :, None, :]
    sin = jnp.sin(ang)[None, :, None, :]
    xf = x.astype(jnp.float32)
    x1, x2 = xf[..., :half], xf[..., half:]
    return jnp.concatenate([x1 * cos - x2 * sin, x1 * sin + x2 * cos], axis=-1).astype(x.dtype)


def gated_linear_recurrence(q, k, v, log_g, s0):
    B, T, H, K = q.shape
    V = v.shape[-1]
    n = -(-T // BLOCK)
    pad = n * BLOCK - T

    def prep(a):
        a = jnp.pad(a.astype(jnp.float32), ((0, 0), (0, pad), (0, 0), (0, 0)))
        return a.reshape(B, n, BLOCK, H, a.shape[-1]).transpose(1, 0, 3, 2, 4)

    qb, kb, vb, gb = prep(q), prep(k), prep(v), prep(log_g)
    b = jnp.cumsum(gb, axis=3)
    b_last = b[:, :, :, -1:, :]
    q_in = qb * jnp.exp(b)
    k_in = kb * jnp.exp(-b)
    k_out = kb * jnp.exp(b_last - b)
    causal = jnp.tril(jnp.ones((BLOCK, BLOCK), dtype=bool))
    scores = jnp.where(causal, jnp.einsum('nbhlk,nbhsk->nbhls', q_in, k_in), 0.0)
    o_intra = jnp.einsum('nbhls,nbhsv->nbhlv', scores, vb)

    def step(S, xs):
        q_i, k_o, v_i, dec = xs
        o_inter = jnp.einsum('bhlk,bhkv->bhlv', q_i, S)
        S = jnp.exp(dec)[..., 0, :, None] * S + jnp.einsum('bhlk,bhlv->bhkv', k_o, v_i)
        return S, o_inter

    s_final, o_inter = lax.scan(step, s0.astype(jnp.float32), (q_in, k_out, vb, b_last))
    o = (o_intra + o_inter).transpose(1, 0, 3, 2, 4).reshape(B, n * BLOCK, H, V)[:, :T]
    return o.astype(v.dtype), s_final.astype(s0.dtype)


def even_mixer(h, pos, s_ret, s_hgrn, w_in, w_out, hgrn_gnorm, lb):
    B, T, _ = h.shape
    q_a, k_a, v_a, g_a, q_b, f_b, i_b, g_b = jnp.split(h @ w_in, SPLITS_EVEN, axis=-1)
    q_a = rotary(q_a.reshape(B, T, H_A, DK_A), pos)
    k_a = rotary(k_a.reshape(B, T, H_A, DK_A), pos) * (DK_A ** -0.5)
    log_gamma = jnp.log1p(-jnp.exp2(-5.0 - jnp.arange(H_A, dtype=jnp.float32)))
    log_g_a = jnp.broadcast_to(log_gamma[None, None, :, None], (B, T, H_A, DK_A))
    o_a, s_ret_new = gated_linear_recurrence(q_a, k_a, v_a.reshape(B, T, H_A, DV_A), log_g_a, s_ret)
    o_a = head_groupnorm(o_a).reshape(B, T, VA) * jax.nn.silu(g_a)
    q_b = jax.nn.silu(q_b).reshape(B, T, H_B, DK_B) * (DK_B ** -0.5)
    f = lb + (1.0 - lb) * jax.nn.sigmoid(f_b.astype(jnp.float32))
    k_b = (1.0 - f).reshape(B, T, H_B, DK_B)
    log_f = jnp.log(f).reshape(B, T, H_B, DK_B)
    o_b, s_hgrn_new = gated_linear_recurrence(q_b, k_b, i_b.reshape(B, T, H_B, DV_B), log_f, s_hgrn)
    o_b = rmsnorm(o_b, hgrn_gnorm).reshape(B, T, VB) * jax.nn.silu(g_b)
    y = jnp.concatenate([o_a, o_b], axis=-1) @ w_out
    return y, s_ret_new, s_hgrn_new


def odd_mixer(h, s_gla, w_in, w_gk2, b_gk2, gla_gnorm, w_out):
    B, T, _ = h.shape
    q, k, v, g, gk_low = jnp.split(h @ w_in, SPLITS_ODD, axis=-1)
    q = q.reshape(B, T, H_C, DK_C) * (DK_C ** -0.5)
    k = k.reshape(B, T, H_C, DK_C)
    log_g = jax.nn.log_sigmoid((gk_low @ w_gk2 + b_gk2).astype(jnp.float32)) / GATE_NORMALIZER
    o, s_new = gated_linear_recurrence(q, k, v.reshape(B, T, H_C, DV_C), log_g.reshape(B, T, H_C, DK_C), s_gla)
    o = rmsnorm(o, gla_gnorm).reshape(B, T, VC) * jax.nn.silu(g)
    return o @ w_out, s_new


def conv_ffn(h, conv_buf, w_up, conv_w, conv_b, w_down):
    a, u = jnp.split(h @ w_up, 2, axis=-1)
    T = a.shape[1]
    a_ext = jnp.concatenate([conv_buf.astype(a.dtype), a], axis=1)
    a_conv = conv_b + sum(a_ext[:, j:j + T] * conv_w[j] for j in range(CONV_W))
    y = (jax.nn.silu(a_conv) * u) @ w_down
    return y, a_ext[:, -(CONV_W - 1):]


def run_group(x, pos, s_ret, s_hgrn, s_gla, s_conv, norm_mix, norm_ffn, norm_final,
              w_in_even, w_out_even, hgrn_lb, hgrn_gnorm, w_in_odd, w_gk2, b_gk2, gla_gnorm,
              w_out_odd, ffn_w_up, ffn_conv_w, ffn_conv_b, ffn_w_down):
    lb_all = jnp.cumsum(jax.nn.softmax(hgrn_lb.astype(jnp.float32), axis=0), axis=0)
    new_ret, new_hgrn, new_gla, new_conv = [], [], [], []
    h = x
    for l in range(DEPTH):
        i = l // 2
        hn = rmsnorm(h, norm_mix[l])
        if l % 2 == 0:
            y, sr, sh = even_mixer(hn, pos, s_ret[i], s_hgrn[i], w_in_even[i], w_out_even[i],
                                   hgrn_gnorm[i], lb_all[l])
            new_ret.append(sr)
            new_hgrn.append(sh)
        else:
            y, sg = odd_mixer(hn, s_gla[i], w_in_odd[i], w_gk2[i], b_gk2[i], gla_gnorm[i], w_out_odd[i])
            new_gla.append(sg)
        h = h + y
        y, sc = conv_ffn(rmsnorm(h, norm_ffn[l]), s_conv[l], ffn_w_up[l], ffn_conv_w[l],
                         ffn_conv_b[l], ffn_w_down[l])
        new_conv.append(sc)
        h = h + y
    return (rmsnorm(h, norm_final), jnp.stack(new_ret), jnp.stack(new_hgrn),
            jnp.stack(new_gla), jnp.stack(new_conv))


def setup_inputs(seed: int = 0) -> dict:
    key = jax.random.key(seed)
    ks = jax.random.split(key, 24)

    def nrm(k, shape, scale):
        return jax.random.normal(k, shape, dtype=jnp.float32) * scale

    return {
        "x_prompt": nrm(ks[0], (BATCH, SEQ, D_MODEL), 1.0),
        "x_sample": nrm(ks[1], (DEC_BATCH, DEC_SEQ, D_MODEL), 1.0),
        "state_ret": nrm(ks[2], (N_EVEN, DEC_BATCH, H_A, DK_A, DV_A), 0.5),
        "state_hgrn": nrm(ks[3], (N_EVEN, DEC_BATCH, H_B, DK_B, DV_B), 0.5),
        "state_gla": nrm(ks[4], (N_ODD, DEC_BATCH, H_C, DK_C, DV_C), 0.5),
        "cache_ffn_conv": nrm(ks[5], (DEPTH, DEC_BATCH, CONV_W - 1, D_FF), 1.0),
        "norm_mix": 1.0 + nrm(ks[6], (DEPTH, D_MODEL), 0.02),
        "norm_ffn": 1.0 + nrm(ks[7], (DEPTH, D_MODEL), 0.02),
        "norm_final": 1.0 + nrm(ks[8], (D_MODEL,), 0.02),
        "w_in_even": nrm(ks[9], (N_EVEN, D_MODEL, D_IN_EVEN), D_MODEL ** -0.5),
        "w_out_even": nrm(ks[10], (N_EVEN, D_MIX_EVEN, D_MODEL), D_MIX_EVEN ** -0.5),
        "hgrn_lb": nrm(ks[11], (DEPTH + 1, QB), 0.1),
        "hgrn_gnorm": 1.0 + nrm(ks[12], (N_EVEN, DV_B), 0.02),
        "w_in_odd": nrm(ks[13], (N_ODD, D_MODEL, D_IN_ODD), D_MODEL ** -0.5),
        "w_gk2": nrm(ks[14], (N_ODD, GK_RANK, QC), GK_RANK ** -0.5),
        "b_gk2": nrm(ks[15], (N_ODD, QC), 0.02),
        "gla_gnorm": 1.0 + nrm(ks[16], (N_ODD, DV_C), 0.02),
        "w_out_odd": nrm(ks[17], (N_ODD, VC, D_MODEL), VC ** -0.5),
        "ffn_w_up": nrm(ks[18], (DEPTH, D_MODEL, 2 * D_FF), D_MODEL ** -0.5),
        "ffn_conv_w": nrm(ks[19], (DEPTH, CONV_W, D_FF), CONV_W ** -0.5),
        "ffn_conv_b": nrm(ks[20], (DEPTH, D_FF), 0.02),
        "ffn_w_down": nrm(ks[21], (DEPTH, D_FF, D_MODEL), D_FF ** -0.5),
    }


def reference(x_prompt, x_sample, state_ret, state_hgrn, state_gla, cache_ffn_conv,
              norm_mix, norm_ffn, norm_final, w_in_even, w_out_even, hgrn_lb, hgrn_gnorm,
              w_in_odd, w_gk2, b_gk2, gla_gnorm, w_out_odd, ffn_w_up, ffn_conv_w, ffn_conv_b,
              ffn_w_down):
    dt = x_prompt.dtype
    Bp = x_prompt.shape[0]
    y_prompt, ret_p, hgrn_p, gla_p, conv_p = run_group(
        x_prompt, jnp.arange(SEQ, dtype=jnp.int32),
        jnp.zeros((N_EVEN, Bp, H_A, DK_A, DV_A), dt),
        jnp.zeros((N_EVEN, Bp, H_B, DK_B, DV_B), dt),
        jnp.zeros((N_ODD, Bp, H_C, DK_C, DV_C), dt),
        jnp.zeros((DEPTH, Bp, CONV_W - 1, D_FF), dt),
        norm_mix, norm_ffn, norm_final, w_in_even, w_out_even, hgrn_lb, hgrn_gnorm,
        w_in_odd, w_gk2, b_gk2, gla_gnorm, w_out_odd, ffn_w_up, ffn_conv_w, ffn_conv_b, ffn_w_down)
    y_sample, ret_s, hgrn_s, gla_s, conv_s = run_group(
        x_sample, PAST_LEN + jnp.arange(x_sample.shape[1], dtype=jnp.int32),
        state_ret, state_hgrn, state_gla, cache_ffn_conv,
        norm_mix, norm_ffn, norm_final, w_in_even, w_out_even, hgrn_lb, hgrn_gnorm,
        w_in_odd, w_gk2, b_gk2, gla_gnorm, w_out_odd, ffn_w_up, ffn_conv_w, ffn_conv_b, ffn_w_down)
    return (y_prompt, y_sample, ret_p, ret_s, hgrn_p, hgrn_s, gla_p, gla_s, conv_p, conv_s)
```

```python
import math
from contextlib import ExitStack

import numpy as np
import concourse.bass as bass
import concourse.mybir as mybir
from concourse.bass_utils import run_bass_kernel_spmd

F32 = mybir.dt.float32
BF16 = mybir.dt.bfloat16
ALU = mybir.AluOpType
AF = mybir.ActivationFunctionType

T = 1088
TP = 1024
NTT = [(0, 512), (512, 512), (1024, 64)]
CH = [(i * 128, 128) for i in range(8)] + [(1024, 64)]
NPASS = 2
DFF = 5504
NFT = 43
EPS = 1e-6


_GUARD = [None]


class Tile:
    __slots__ = ("name", "w", "r", "scoped")

    def __init__(self, name=""):
        self.name = name
        self.w = None
        self.scoped = _GUARD[0] is not None
        self.r = dict(_GUARD[0]) if _GUARD[0] else {}


class Eng:
    def __init__(self, name, h, sem, is_pe=False):
        self.name = name
        self.h = h
        self.sem = sem
        self.is_pe = is_pe
        self.n = 0
        self.nsig = 0
        self.sigs = []
        self.last = None
        self.waited = {}


class DmaSlot:
    def __init__(self, sem):
        self.sem = sem
        self.total = 0


class FW:
    def __init__(self, nc, stack, n_dma_sems=32):
        self.nc = nc
        es = stack.enter_context
        self.pe = Eng("pe", nc.tensor, es(nc.semaphore("s_pe")), True)
        self.act = Eng("act", nc.scalar, es(nc.semaphore("s_act")))
        self.dve = Eng("dve", nc.vector, es(nc.semaphore("s_dve")))
        self.pool = Eng("pool", nc.gpsimd, es(nc.semaphore("s_pool")))
        self.sp = Eng("sp", nc.sync, es(nc.semaphore("s_sp")))
        self.engs = [self.pe, self.act, self.dve, self.pool, self.sp]
        self.slots = [DmaSlot(es(nc.semaphore("s_dma%d" % i))) for i in range(n_dma_sems)]
        self.slots_sw = self.slots[:n_dma_sems // 2]
        self.slots_hw = self.slots[n_dma_sems // 2:]
        self.slot_i = {True: 0, False: 0}
        self.out_events = []
        self.scoped_dma = []

    def _resolve(self, ev):
        if ev[0] == "dma":
            return ev[1].sem, ev[2]
        e, idx = ev
        lo, hi = 0, len(e.sigs)
        while lo < hi:
            mid = (lo + hi) // 2
            if e.sigs[mid][0] >= idx:
                hi = mid
            else:
                lo = mid + 1
        if lo < len(e.sigs):
            return e.sem, e.sigs[lo][1]
        assert e.last is not None and e.n - 1 >= idx
        e.last.then_inc(e.sem, 1)
        e.nsig += 1
        e.sigs.append((e.n - 1, e.nsig))
        return e.sem, e.nsig

    def _wait(self, eng, evs):
        need = {}
        for ev in evs:
            if ev is None:
                continue
            sem, val = self._resolve(ev)
            k = id(sem)
            if eng.waited.get(k, 0) >= val:
                continue
            if k not in need or need[k][1] < val:
                need[k] = (sem, val)
        for k, (sem, val) in need.items():
            eng.h.wait_ge(sem, val)
            eng.waited[k] = val

    def _deps(self, eng, reads, writes, is_dma=False):
        evs = []
        for t in reads:
            if t.w is not None:
                if (not is_dma) and t.w[0] is eng and eng.is_pe:
                    continue
                evs.append(t.w)
        for t in writes:
            if t.w is not None:
                if not ((not is_dma) and t.w[0] is eng and eng.is_pe):
                    evs.append(t.w)
            for ev in t.r.values():
                if (not is_dma) and ev[0] is eng and eng.is_pe:
                    continue
                evs.append(ev)
        return evs

    def _record(self, ev, reads, writes):
        key = id(ev[0]) if ev[0] != "dma" else ("d", id(ev[1]))
        for t in reads:
            t.r[key] = ev
        for t in writes:
            t.w = ev
            t.r = {}

    def op(self, eng, fn, reads=(), writes=(), sig=None):
        self._wait(eng, self._deps(eng, reads, writes))
        inst = fn()
        eng.last = inst
        ev = (eng, eng.n)
        eng.n += 1
        if sig is None:
            sig = not eng.is_pe
        if sig:
            inst.then_inc(eng.sem, 1)
            eng.nsig += 1
            eng.sigs.append((eng.n - 1, eng.nsig))
        self._record(ev, reads, writes)
        return inst

    def dma(self, eng, out, in_, reads=(), writes=(), is_output=False):
        sw = eng is self.pool
        pool_ = self.slots_sw if sw else self.slots_hw
        slot = pool_[self.slot_i[sw]]
        self.slot_i[sw] = (self.slot_i[sw] + 1) % len(pool_)
        evs = self._deps(eng, reads, writes, is_dma=True)
        if slot.total:
            evs.append(("dma", slot, slot.total))
        self._wait(eng, evs)
        eng.h.dma_start(out=out, in_=in_).then_inc(slot.sem, 16)
        slot.total += 16
        ev = ("dma", slot, slot.total)
        self._record(ev, reads, writes)
        if any(t.scoped for t in reads) or any(t.scoped for t in writes):
            self.scoped_dma.append(ev)
        if is_output:
            self.out_events.append(ev)
        return ev

    def release(self):
        guard = dict(_GUARD[0] or {})
        for e in self.engs:
            if e.n:
                ev = (e, e.n - 1)
                if e.is_pe:
                    self._resolve(ev)
                guard[id(e)] = ev
        for ev in self.scoped_dma:
            guard[("d", id(ev[1]))] = ev
        self.scoped_dma = []
        _GUARD[0] = guard

    def barrier(self):
        evs = []
        for e in self.engs:
            if e.n:
                evs.append((e, e.n - 1))
        for s in self.slots:
            if s.total:
                evs.append(("dma", s, s.total))
        for e in self.engs:
            self._wait(e, evs)

    def finish(self):
        self.barrier()


class Prog:
    pass


def build_program(plan=None):
    _GUARD[0] = None
    nc = bass.Bass("TRN2", target_bir_lowering=False)

    def D(name, shape, kind="ExternalInput", dt=F32):
        return nc.dram_tensor(name, list(shape), dt, kind=kind).ap()

    g = Prog()
    g.nc = nc
    g.recording = plan is None
    g.wa_seq, g.wb_seq = ([], []) if plan is None else plan
    g.wa_pos = g.wb_pos = 0
    g.wa_issued = g.wb_issued = 0
    g.xin = D("xin", [NPASS, 128, 16, T])
    g.st_ret = D("st_ret", [NPASS, 4, 128, 256])
    g.st_hg = D("st_hg", [NPASS, 8, 128, 128])
    g.st_gla = D("st_gla", [NPASS, 4, 256, 512])
    g.cv_in = D("cv_in", [NPASS, 2, 128, NFT, 2])
    g.gam = D("gam", [128, 5, 16])
    g.convw = D("convw", [128, 2, NFT, 4])
    g.lbraw = D("lbraw", [128, 3, 8])
    g.hgn = D("hgn", [128, 1])
    g.wgk2 = D("wgk2", [16, 1024])
    g.bgk = D("bgk", [128, 8])
    g.glan = D("glan", [128, 4])
    g.w_in_e = D("w_in_e", [2048, 7168])
    g.w_out_e = D("w_out_e", [2048, 2048])
    g.w_in_o = D("w_in_o", [2048, 6160])
    g.w_out_o = D("w_out_o", [2048, 2048])
    g.w_up = D("w_up", [2, 2048, 2 * DFF])
    g.w_dn = D("w_dn", [2, DFF, 2048])
    g.c_rope = D("c_rope", [NPASS, 2, 128, T])
    g.c_ret = D("c_ret", [128, 4, 2, 128])
    g.c_ko = D("c_ko", [128, 4, 2])
    g.c_dec = D("c_dec", [128, 4, 2])
    g.c_mat = D("c_mat", [4, 128, 128])
    g.c_smask = D("c_smask", [128, T])
    EO = "ExternalOutput"
    g.yout = D("yout", [NPASS, 128, 16, T], EO)
    g.o_ret_p = D("o_ret_p", [4, 128, 256], EO)
    g.o_ret_s = D("o_ret_s", [NPASS, 4, 128, 256], EO)
    g.o_hg_p = D("o_hg_p", [8, 128, 128], EO)
    g.o_hg_s = D("o_hg_s", [NPASS, 8, 128, 128], EO)
    g.o_gla_p = D("o_gla_p", [4, 256, 512], EO)
    g.o_gla_s = D("o_gla_s", [NPASS, 4, 256, 512], EO)
    g.o_cv = D("o_cv", [NPASS, 2, 128, NFT, 4], EO)
    g.cr_ret = D("cr_ret", [4, 128, 256], "Internal")
    g.cr_hg = D("cr_hg", [8, 128, 128], "Internal")
    g.cr_gla = D("cr_gla", [4, 256, 512], "Internal")
    g.t_cr = {"ret": [Tile() for _ in range(4)], "hg": [Tile() for _ in range(8)],
              "gla": [Tile() for _ in range(4)]}

    with ExitStack() as st:
        es = st.enter_context
        fw = FW(nc, st)
        g.fw = fw

        g.uid = 0

        def SB(name, shape, dt=F32, stack=None):
            g.uid += 1
            return (stack or st).enter_context(nc.sbuf_tensor("%s_%d" % (name, g.uid), list(shape), dt))
        g.SB = SB

        g.H = SB("H", [128, 16, T]); g.tH = [Tile("H%d" % i) for i in range(16)]
        g.HN = SB("HN", [128, 16, T], BF16); g.tHN = [Tile("HN%d" % i) for i in range(16)]
        g.ZB = SB("ZB", [128, 8, T], BF16); g.tZB = [Tile("ZB%d" % i) for i in range(8)]
        g.WA = [SB("WA%d" % i, [128, 16, 256], BF16) for i in range(2)]; g.tWA = [Tile(), Tile()]
        g.WB = [SB("WB%d" % i, [128, 8, 256], BF16) for i in range(2)]; g.tWB = [Tile(), Tile()]
        g.wmap = {"w_in_e": g.w_in_e, "w_out_e": g.w_out_e, "w_in_o": g.w_in_o, "w_out_o": g.w_out_o,
                  ("w_up", 0): g.w_up[0], ("w_up", 1): g.w_up[1], ("w_dn", 0): g.w_dn[0], ("w_dn", 1): g.w_dn[1]}
        g.gamS = SB("gamS", [128, 5, 16]); g.tgam = Tile()
        g.convwS = SB("convwS", [128, 2, NFT, 4]); g.tconvw = Tile()
        g.cmatF = SB("cmatF", [128, 4, 128]); g.tcmat = Tile()
        g.identB = SB("identB", [128, 128], BF16)
        g.onesB = SB("onesB", [128, 128], BF16)
        g.cvcar = SB("cvcar", [128, 2, NFT, 2]); g.tcvcar = [Tile(), Tile()]
        g.lbS = SB("lbS", [128, 3, 8]); g.tlb = Tile()
        g.lb1 = SB("lb1", [128, 8]); g.lb2 = SB("lb2", [128, 8])
        g.hgnS = SB("hgnS", [128, 1])
        g.bgkS = SB("bgkS", [128, 8]); g.nbgk = SB("nbgk", [128, 8])
        g.glanS = SB("glanS", [128, 4])
        g.tmisc = Tile()
        g.kodec = SB("kodec", [128, 4, 4])
        g.PA = [es(nc.psum_tensor("PA%d" % i, [128, 1536], F32)) for i in range(2)]
        g.PB = [es(nc.psum_tensor("PB%d" % i, [128, 512], F32)) for i in range(2)]
        g.bank = [g.PA[0][:, 0:512], g.PA[0][:, 512:1024], g.PA[0][:, 1024:1536],
                  g.PA[1][:, 0:512], g.PA[1][:, 512:1024], g.PA[1][:, 1024:1536],
                  g.PB[0][:, :], g.PB[1][:, :]]
        g.tbank = [Tile("bank%d" % i) for i in range(8)]
        g.pa_i = 0

        load_consts(g)
        _GUARD[0] = {}
        for p in range(NPASS):
            run_pass(g, p)
        fw.finish()
    if plan is None:
        return build_program((g.wa_seq, g.wb_seq))
    return nc


def load_consts(g):
    fw, nc = g.fw, g.nc
    fw.dma(fw.sp, g.gamS[:], g.gam, writes=[g.tgam])
    fw.dma(fw.sp, g.convwS[:], g.convw, writes=[g.tconvw])
    fw.dma(fw.sp, g.cmatF[:], g.c_mat.rearrange("m p n -> p m n"), writes=[g.tcmat])
    fw.dma(fw.sp, g.lbS[:], g.lbraw, writes=[g.tlb])
    fw.dma(fw.sp, g.hgnS[:], g.hgn, writes=[g.tmisc])
    fw.dma(fw.sp, g.bgkS[:], g.bgk, writes=[g.tmisc])
    fw.dma(fw.sp, g.glanS[:], g.glan, writes=[g.tmisc])
    fw.dma(fw.sp, g.kodec[:, :, 0:2], g.c_ko, writes=[g.tmisc])
    fw.dma(fw.sp, g.kodec[:, :, 2:4], g.c_dec, writes=[g.tmisc])
    fw.op(fw.dve, lambda: nc.vector.tensor_copy(out=g.identB[:], in_=g.cmatF[:, 0, :]), reads=[g.tcmat], writes=[g.tmisc])
    fw.op(fw.dve, lambda: nc.vector.tensor_copy(out=g.onesB[:], in_=g.cmatF[:, 1, :]), reads=[g.tcmat], writes=[g.tmisc])
    fw.op(fw.act, lambda: nc.scalar.activation(out=g.lbS[:], in_=g.lbS[:], func=AF.Exp), reads=[g.tlb], writes=[g.tlb])
    fw.op(fw.dve, lambda: nc.vector.tensor_tensor(out=g.lb2[:], in0=g.lbS[:, 0, :], in1=g.lbS[:, 1, :], op=ALU.add), reads=[g.tlb], writes=[g.tmisc])
    fw.op(fw.dve, lambda: nc.vector.tensor_tensor(out=g.lb2[:], in0=g.lb2[:], in1=g.lbS[:, 2, :], op=ALU.add), reads=[g.tlb, g.tmisc], writes=[g.tmisc])
    fw.op(fw.dve, lambda: nc.vector.reciprocal(out=g.lb2[:], in_=g.lb2[:]), reads=[g.tmisc], writes=[g.tmisc])
    fw.op(fw.dve, lambda: nc.vector.tensor_tensor(out=g.lb1[:], in0=g.lbS[:, 0, :], in1=g.lb2[:], op=ALU.mult), reads=[g.tlb, g.tmisc], writes=[g.tmisc])
    fw.op(fw.dve, lambda: nc.vector.tensor_scalar(out=g.lb2[:], in0=g.lb1[:], scalar1=-1.0, scalar2=1.0, op0=ALU.mult, op1=ALU.add), reads=[g.tmisc], writes=[g.tmisc])
    fw.op(fw.dve, lambda: nc.vector.tensor_scalar(out=g.nbgk[:], in0=g.bgkS[:], scalar1=-1.0, scalar2=None, op0=ALU.mult), reads=[g.tmisc], writes=[g.tmisc])
    fw.op(fw.pool, lambda: nc.gpsimd.memset(g.cvcar[:], 0.0), writes=g.tcvcar)
    fw.barrier()


def pa_set(g):
    i = g.pa_i
    g.pa_i ^= 1
    return g.PA[i], [g.tbank[3 * i], g.tbank[3 * i + 1], g.tbank[3 * i + 2]]


def _wa_fetch(g, i):
    fw = g.fw
    wname, segs = g.wa_seq[i]
    Wv = g.wmap[wname].rearrange("(kc p) n -> p kc n", p=128)
    b = i % 2
    off = 0
    for (c0, ncol) in segs:
        fw.dma(fw.pool, g.WA[b][:, :, off:off + ncol], Wv[:, :, c0:c0 + ncol], writes=[g.tWA[b]])
        off += ncol
    g.wa_issued = i + 1


def gemm_fm(g, wname, blocks, on_tile):
    fw, nc = g.fw, g.nc
    for segs in blocks:
        i = g.wa_pos
        g.wa_pos += 1
        if g.recording:
            g.wa_seq.append((wname, segs))
        assert g.wa_seq[i] == (wname, segs)
        if g.wa_issued <= i:
            _wa_fetch(g, i)
        if i + 1 < len(g.wa_seq) and g.wa_issued <= i + 1 and not g.recording:
            _wa_fetch(g, i + 1)
        b = i % 2
        off = 0
        mts = []
        for (c0, ncol) in segs:
            m0 = 0
            while m0 < ncol:
                n = min(128, ncol - m0)
                mts.append((c0 + m0, off + m0, n))
                m0 += n
            off += ncol
        for (cg, m0, n) in mts:
            ps, tps = pa_set(g)
            for kc in range(16):
                for ti, (t0, tn) in enumerate(NTT):
                    fw.op(fw.pe, lambda kc=kc, t0=t0, tn=tn, m0=m0, n=n, b=b, ps=ps:
                          nc.tensor.matmul(ps[0:n, t0:t0 + tn], lhsT=g.WA[b][:, kc, m0:m0 + n],
                                           rhs=g.HN[:, kc, t0:t0 + tn], start=(kc == 0), stop=(kc == 15)),
                          reads=[g.tWA[b], g.tHN[kc]], writes=[tps[ti]], sig=(kc == 15 and ti == 2))
            on_tile(cg, n, ps, tps)


def _wb_fetch(g, i):
    fw = g.fw
    wname, row0, nk, c0 = g.wb_seq[i]
    Wv = g.wmap[wname][row0:row0 + nk * 128, :].rearrange("(kc p) n -> p kc n", p=128)
    b = i % 2
    fw.dma(fw.pool, g.WB[b][:, 0:nk, :], Wv[:, :, c0:c0 + 256], writes=[g.tWB[b]])
    g.wb_issued = i + 1


def gemm_acc(g, wname, row0, nk):
    fw, nc = g.fw, g.nc
    for blk in range(8):
        c0 = blk * 256
        i = g.wb_pos
        g.wb_pos += 1
        if g.recording:
            g.wb_seq.append((wname, row0, nk, c0))
        assert g.wb_seq[i] == (wname, row0, nk, c0)
        if g.wb_issued <= i:
            _wb_fetch(g, i)
        if i + 1 < len(g.wb_seq) and g.wb_issued <= i + 1 and not g.recording:
            _wb_fetch(g, i + 1)
        b = i % 2
        for mi in range(2):
            m = blk * 2 + mi
            ps, tps = pa_set(g)
            for kc in range(nk):
                for ti, (t0, tn) in enumerate(NTT):
                    fw.op(fw.pe, lambda kc=kc, t0=t0, tn=tn, mi=mi, b=b, ps=ps:
                          nc.tensor.matmul(ps[:, t0:t0 + tn], lhsT=g.WB[b][:, kc, mi * 128:(mi + 1) * 128],
                                           rhs=g.ZB[:, kc, t0:t0 + tn], start=(kc == 0), stop=(kc == nk - 1)),
                          reads=[g.tWB[b], g.tZB[kc]], writes=[tps[ti]], sig=(kc == nk - 1 and ti == 2))
            fw.op(fw.dve, lambda m=m, ps=ps: nc.vector.tensor_tensor(out=g.H[:, m, :], in0=g.H[:, m, :], in1=ps[:, 0:T], op=ALU.add),
                  reads=tps + [g.tH[m]], writes=[g.tH[m]])


def rmsnorm(g, gi, final_out=None):
    fw, nc = g.fw, g.nc
    with ExitStack() as ph:
        sq = [g.SB("rn_sq%d" % i, [128, T], BF16, ph) for i in range(3)]
        tsq = [Tile(), Tile(), Tile()]
        rstd = g.SB("rn_rstd", [128, T], F32, ph)
        trs = Tile()
        tmp = g.SB("rn_tmp", [128, T], F32, ph)
        ttmp = Tile()
        ps, tps = pa_set(g)
        for kc in range(16):
            i = kc % 3
            if i == 0:
                fw.op(fw.act, lambda kc=kc, i=i: nc.scalar.activation(out=sq[i][:], in_=g.H[:, kc, :], func=AF.Square),
                      reads=[g.tH[kc]], writes=[tsq[i]])
            else:
                eng = fw.dve if i == 1 else fw.pool
                fw.op(eng, lambda kc=kc, i=i, eng=eng: eng.h.tensor_tensor(out=sq[i][:], in0=g.H[:, kc, :], in1=g.H[:, kc, :], op=ALU.mult),
                      reads=[g.tH[kc]], writes=[tsq[i]])
            for ti, (t0, tn) in enumerate(NTT):
                fw.op(fw.pe, lambda kc=kc, i=i, t0=t0, tn=tn: nc.tensor.matmul(ps[:, t0:t0 + tn], lhsT=g.onesB[:], rhs=sq[i][:, t0:t0 + tn],
                                                                          start=(kc == 0), stop=(kc == 15)),
                      reads=[tsq[i], g.tmisc], writes=[tps[ti]], sig=(ti == 2))
        fw.op(fw.dve, lambda: nc.vector.tensor_scalar(out=rstd[:], in0=ps[:, 0:T], scalar1=1.0 / 2048.0, scalar2=EPS, op0=ALU.mult, op1=ALU.add),
              reads=tps, writes=[trs])
        fw.op(fw.act, lambda: nc.scalar.activation(out=rstd[:], in_=rstd[:], func=AF.Ln), reads=[trs], writes=[trs])
        fw.op(fw.act, lambda: nc.scalar.activation(out=rstd[:], in_=rstd[:], func=AF.Exp, scale=-0.5), reads=[trs], writes=[trs])
        for kc in range(16):
            dst, tdst = (g.HN, g.tHN) if final_out is None else (g.H, g.tH)
            if kc % 2 == 0:
                fw.op(fw.dve, lambda kc=kc, dst=dst: nc.vector.scalar_tensor_tensor(out=dst[:, kc, :], in0=g.H[:, kc, :], scalar=g.gamS[:, gi, kc:kc + 1],
                                                                                    in1=rstd[:], op0=ALU.mult, op1=ALU.mult),
                      reads=[g.tH[kc], trs, g.tgam], writes=[tdst[kc]])
            else:
                fw.op(fw.pool, lambda kc=kc: nc.gpsimd.tensor_tensor(out=tmp[:], in0=g.H[:, kc, :], in1=rstd[:], op=ALU.mult),
                      reads=[g.tH[kc], trs], writes=[ttmp])
                fw.op(fw.act, lambda kc=kc, dst=dst: nc.scalar.activation(out=dst[:, kc, :], in_=tmp[:], func=AF.Copy, scale=g.gamS[:, gi, kc:kc + 1]),
                      reads=[ttmp, g.tgam], writes=[tdst[kc]])
        fw.release()


def transposes_to_tok(g, src, tsrc, nf, dst, tdst, scale_ap=None):
    fw, nc = g.fw, g.nc
    for c, (t0, cc) in enumerate(CH):
        for f0 in range(0, nf, 4):
            fn = min(4, nf - f0)
            bi = 4 + (g.tr_i % 4)
            use_act = (g.tr_i % 2 == 0)
            g.tr_i += 1
            pb = g.bank[bi].bitcast(BF16)
            for f in range(fn):
                fw.op(fw.pe, lambda f=f, f0=f0, t0=t0, cc=cc, pb=pb: nc.tensor.transpose(pb[0:cc, f * 128:(f + 1) * 128], src[:, f0 + f, t0:t0 + cc], g.identB[:]),
                      reads=[tsrc, g.tmisc], writes=[g.tbank[bi]], sig=(f == fn - 1))
            o_ap = dst[0:cc, c, f0 * 128:(f0 + fn) * 128]
            i_ap = pb[0:cc, 0:fn * 128]
            if scale_ap is None:
                if use_act:
                    fw.op(fw.act, lambda o_ap=o_ap, i_ap=i_ap: nc.scalar.copy(out=o_ap, in_=i_ap), reads=[g.tbank[bi]], writes=[tdst])
                else:
                    fw.op(fw.dve, lambda o_ap=o_ap, i_ap=i_ap: nc.vector.tensor_copy(out=o_ap, in_=i_ap), reads=[g.tbank[bi]], writes=[tdst])
            else:
                sc = scale_ap(cc)
                if use_act:
                    fw.op(fw.act, lambda o_ap=o_ap, i_ap=i_ap, sc=sc: nc.scalar.activation(out=o_ap, in_=i_ap, func=AF.Copy, scale=sc),
                          reads=[g.tbank[bi], g.tmisc], writes=[tdst])
                else:
                    fw.op(fw.dve, lambda o_ap=o_ap, i_ap=i_ap, sc=sc: nc.vector.tensor_scalar(out=o_ap, in0=i_ap, scalar1=sc, scalar2=None, op0=ALU.mult),
                          reads=[g.tbank[bi], g.tmisc], writes=[tdst])


def recurrence(g, p, nk, nv, QIN, tQ, KIN, tK, KOT, tKO, VT, tV, OT, tOT, dec_ap, st_in, st_out_p, st_out_s, carry, tcarry, ph):
    fw, nc = g.fw, g.nc
    V = nv * 128
    LA = 4 if nk == 1 else 2
    NR = LA + 1
    S = g.SB("S", [128, nk, V], F32, ph); tS = Tile()
    Sb = [g.SB("Sb%d" % i, [128, nk, V], BF16, ph) for i in range(NR)]; tSb = [Tile() for _ in range(NR)]
    scm = g.SB("scm", [128, 9, 128], BF16, ph); tscm = [Tile() for _ in range(9)]

    def view(d):
        return d.rearrange("(kt p) v -> p kt v", p=128)

    if p == 0:
        fw.op(fw.pool, lambda: nc.gpsimd.memset(S[:], 0.0), writes=[tS])
        fw.op(fw.pool, lambda: nc.gpsimd.memset(Sb[0][:], 0.0), writes=[tSb[0]])
    else:
        fw.dma(fw.sp, S[:], view(carry), reads=[tcarry], writes=[tS])
        fw.op(fw.act, lambda: nc.scalar.copy(out=Sb[0][:], in_=S[:]), reads=[tS], writes=[tSb[0]])
    for c, (t0, cc) in enumerate(CH):
        bsc = c % 2
        psc = g.bank[bsc]
        for kt in range(nk):
            fw.op(fw.pe, lambda kt=kt, t0=t0, cc=cc, psc=psc: nc.tensor.matmul(psc[0:cc, 0:cc], lhsT=KIN[:, kt, t0:t0 + cc], rhs=QIN[:, kt, t0:t0 + cc],
                                                                         start=(kt == 0), stop=(kt == nk - 1)),
                  reads=[tK, tQ], writes=[g.tbank[bsc]], sig=(kt == nk - 1))
        fw.op(fw.dve, lambda c=c, cc=cc, psc=psc: nc.vector.tensor_tensor(out=scm[0:cc, c, 0:cc], in0=psc[0:cc, 0:cc], in1=g.cmatF[0:cc, 2, 0:cc], op=ALU.mult),
              reads=[g.tbank[bsc], g.tcmat], writes=[tscm[c]])

    def scan_step(c):
        t0, cc = CH[c]
        if c == 8:
            if p == 0:
                fw.dma(fw.sp, view(carry), S[:], reads=[tS], writes=[tcarry])
            else:
                fw.dma(fw.sp, view(st_out_p), S[:], reads=[tS], is_output=True)
            fw.dma(fw.sp, S[:], view(st_in), writes=[tS])
            fw.op(fw.act, lambda: nc.scalar.copy(out=Sb[8 % NR][:], in_=S[:]), reads=[tS], writes=[tSb[8 % NR]])
        for kt in range(nk):
            bs = 4 + ((nk * c + kt) % 4)
            pS = g.bank[bs]
            fw.op(fw.pe, lambda kt=kt, c=c, cc=cc, pS=pS: nc.tensor.matmul(pS[:, 0:V], lhsT=KOT[0:cc, c, kt * 128:(kt + 1) * 128], rhs=VT[0:cc, c, 0:V],
                                                                     start=True, stop=True),
                  reads=[tKO, tV], writes=[g.tbank[bs]], sig=True)
            fw.op(fw.dve, lambda kt=kt, c=c, pS=pS: nc.vector.scalar_tensor_tensor(out=S[:, kt, :], in0=S[:, kt, :], scalar=dec_ap(kt, c), in1=pS[:, 0:V],
                                                                               op0=ALU.mult, op1=ALU.add),
                  reads=[tS, g.tbank[bs], g.tmisc, g.tdec], writes=[tS])
        if c < 7:
            r = (c + 1) % NR
            fw.op(fw.pool, lambda r=r: nc.gpsimd.tensor_copy(out=Sb[r][:], in_=S[:]), reads=[tS], writes=[tSb[r]])
        if c == 8:
            fw.dma(fw.sp, view(st_out_s), S[:], reads=[tS], is_output=True)

    def out_step(c):
        t0, cc = CH[c]
        bo = c % 4
        po = g.bank[bo]
        r = c % NR
        for vt in range(nv):
            fw.op(fw.pe, lambda vt=vt, c=c, cc=cc, po=po: nc.tensor.matmul(po[:, vt * cc:(vt + 1) * cc], lhsT=VT[0:cc, c, vt * 128:(vt + 1) * 128],
                                                                     rhs=scm[0:cc, c, 0:cc], start=True, stop=False),
                  reads=[tV, tscm[c]], writes=[g.tbank[bo]])
            for kt in range(nk):
                fw.op(fw.pe, lambda vt=vt, kt=kt, t0=t0, cc=cc, po=po, r=r: nc.tensor.matmul(po[:, vt * cc:(vt + 1) * cc], lhsT=Sb[r][:, kt, vt * 128:(vt + 1) * 128],
                                                                                      rhs=QIN[:, kt, t0:t0 + cc], start=False, stop=(kt == nk - 1)),
                      reads=[tSb[r], tQ], writes=[g.tbank[bo]], sig=(kt == nk - 1 and vt == nv - 1))
        fw.op(fw.act, lambda t0=t0, cc=cc, po=po: nc.scalar.copy(out=OT[:, :, t0:t0 + cc], in_=po[:, 0:nv * cc].rearrange("p (v c) -> p v c", v=nv)),
              reads=[g.tbank[bo]], writes=[tOT])

    for c in range(LA):
        scan_step(c)
    for c in range(9):
        out_step(c)
        if c + LA < 9:
            scan_step(c + LA)


def head_norm_gate(g, OT, tOT, nv, SG, tSG, slot0, center, gn_ap, ph):
    fw, nc = g.fw, g.nc
    V = nv * 128
    sq = [g.SB("hn_sq%d" % i, [128, T], BF16, ph) for i in range(2)]; tsq = [Tile(), Tile()]
    rstd = g.SB("hn_rstd", [128, T], F32, ph); trs = Tile()
    mean = g.SB("hn_mean", [128, T], F32, ph) if center else None; tmn = Tile()
    tmp = g.SB("hn_tmp", [128, T], F32, ph); ttmp = Tile()
    ps2, tps2 = pa_set(g)
    for vt in range(nv):
        i = vt % 2
        fw.op(fw.act, lambda vt=vt, i=i: nc.scalar.activation(out=sq[i][:], in_=OT[:, vt, :], func=AF.Square), reads=[tOT], writes=[tsq[i]])
        for ti, (t0, tn) in enumerate(NTT):
            fw.op(fw.pe, lambda vt=vt, i=i, t0=t0, tn=tn: nc.tensor.matmul(ps2[:, t0:t0 + tn], lhsT=g.onesB[:], rhs=sq[i][:, t0:t0 + tn],
                                                                      start=(vt == 0), stop=(vt == nv - 1)),
                  reads=[tsq[i], g.tmisc], writes=[tps2[ti]], sig=(ti == 2))
    if center:
        ps1, tps1 = pa_set(g)
        for vt in range(nv):
            for ti, (t0, tn) in enumerate(NTT):
                fw.op(fw.pe, lambda vt=vt, t0=t0, tn=tn: nc.tensor.matmul(ps1[:, t0:t0 + tn], lhsT=g.onesB[:], rhs=OT[:, vt, t0:t0 + tn],
                                                                     start=(vt == 0), stop=(vt == nv - 1)),
                      reads=[tOT, g.tmisc], writes=[tps1[ti]], sig=(ti == 2 and vt == nv - 1))
        fw.op(fw.dve, lambda: nc.vector.tensor_scalar(out=mean[:], in0=ps1[:, 0:T], scalar1=1.0 / V, scalar2=None, op0=ALU.mult), reads=tps1, writes=[tmn])
        fw.op(fw.dve, lambda: nc.vector.tensor_tensor(out=tmp[:], in0=mean[:], in1=mean[:], op=ALU.mult), reads=[tmn], writes=[ttmp])
        fw.op(fw.dve, lambda: nc.vector.scalar_tensor_tensor(out=rstd[:], in0=ps2[:, 0:T], scalar=1.0 / V, in1=tmp[:], op0=ALU.mult, op1=ALU.subtract),
              reads=tps2 + [ttmp], writes=[trs])
        fw.op(fw.dve, lambda: nc.vector.tensor_scalar(out=rstd[:], in0=rstd[:], scalar1=EPS, scalar2=None, op0=ALU.add), reads=[trs], writes=[trs])
    else:
        fw.op(fw.dve, lambda: nc.vector.tensor_scalar(out=rstd[:], in0=ps2[:, 0:T], scalar1=1.0 / V, scalar2=EPS, op0=ALU.mult, op1=ALU.add), reads=tps2, writes=[trs])
    fw.op(fw.act, lambda: nc.scalar.activation(out=rstd[:], in_=rstd[:], func=AF.Ln), reads=[trs], writes=[trs])
    fw.op(fw.act, lambda: nc.scalar.activation(out=rstd[:], in_=rstd[:], func=AF.Exp, scale=-0.5), reads=[trs], writes=[trs])
    for vt in range(nv):
        if center:
            fw.op(fw.dve, lambda vt=vt: nc.vector.tensor_tensor(out=tmp[:], in0=OT[:, vt, :], in1=mean[:], op=ALU.subtract), reads=[tOT, tmn], writes=[ttmp])
            fw.op(fw.dve, lambda: nc.vector.tensor_tensor(out=tmp[:], in0=tmp[:], in1=rstd[:], op=ALU.mult), reads=[ttmp, trs], writes=[ttmp])
        else:
            fw.op(fw.dve, lambda vt=vt: nc.vector.scalar_tensor_tensor(out=tmp[:], in0=OT[:, vt, :], scalar=gn_ap(vt), in1=rstd[:], op0=ALU.mult, op1=ALU.mult),
                  reads=[tOT, trs, g.tmisc], writes=[ttmp])
        fw.op(fw.dve, lambda vt=vt: nc.vector.tensor_tensor(out=g.ZB[:, slot0 + vt, :], in0=tmp[:], in1=SG[:, vt, :], op=ALU.mult),
              reads=[ttmp, tSG], writes=[g.tZB[slot0 + vt]])


def smask_setup(g, ph):
    fw, nc = g.fw, g.nc
    sm = g.SB("smask", [128, T], F32, ph)
    tsm = Tile()
    fw.dma(fw.sp, sm[:], g.c_smask, writes=[tsm])
    return sm, tsm


def mixer_even(g, p):
    fw, nc = g.fw, g.nc
    WN = "w_in_e"
    g.tdec = Tile()
    for h in range(4):
        with ExitStack() as ph:
            SBp = lambda n, s, d=F32: g.SB(n, s, d, ph)
            QIN = SBp("QIN", [128, 1, T], BF16); tQ = Tile()
            KIN = SBp("KIN", [128, 1, T], BF16); tK = Tile()
            KOT = SBp("KOT", [128, 9, 128], BF16); tKO = Tile()
            VT = SBp("VT", [128, 9, 256], BF16); tV = Tile()
            OT = SBp("OT", [128, 2, T], BF16); tOT = Tile()
            SG = SBp("SG", [128, 2, T], BF16); tSG = Tile()
            with ExitStack() as ph1:
                SB1 = lambda n, s, d=F32: g.SB(n, s, d, ph1)
                cs = SB1("cs", [128, 2, T]); tcs = Tile()
                eqk = SB1("eqk", [128, 2, 128]); teqk = Tile()
                fw.dma(fw.sp, cs[:], g.c_rope[p].rearrange("m p t -> p m t"), writes=[tcs])
                fw.dma(fw.sp, eqk[:], g.c_ret[:, h, :, :], writes=[teqk])
                XF = SB1("XF", [128, T]); tXF = Tile()
                XR = SB1("XR", [128, T]); tXR = Tile()
                KRB = SB1("KRB", [128, 1, T], BF16); tKRB = Tile()
                VF = SB1("VF", [128, 2, T], BF16); tVF = Tile()

                def rot(which):
                    def on_tile(c0, n, ps, tps):
                        fw.op(fw.act, lambda: nc.scalar.copy(out=XF[:], in_=ps[:, 0:T]), reads=tps, writes=[tXF])
                        ps2, tps2 = pa_set(g)
                        for ti, (t0, tn) in enumerate(NTT):
                            fw.op(fw.pe, lambda t0=t0, tn=tn: nc.tensor.matmul(ps2[:, t0:t0 + tn], lhsT=g.cmatF[:, 3, :], rhs=XF[:, t0:t0 + tn], start=True, stop=True),
                                  reads=[tXF, g.tcmat], writes=[tps2[ti]], sig=(ti == 2))
                        fw.op(fw.dve, lambda: nc.vector.tensor_tensor(out=XR[:], in0=ps2[:, 0:T], in1=cs[:, 1, :], op=ALU.mult), reads=tps2 + [tcs], writes=[tXR])
                        fw.op(fw.dve, lambda: nc.vector.tensor_tensor(out=XF[:], in0=XF[:], in1=cs[:, 0, :], op=ALU.mult), reads=[tXF, tcs], writes=[tXF])
                        fw.op(fw.dve, lambda: nc.vector.tensor_tensor(out=XF[:], in0=XF[:], in1=XR[:], op=ALU.add), reads=[tXF, tXR], writes=[tXF])
                        dst, tdst = (QIN, tQ) if which == 0 else (KIN, tK)
                        for c, (t0, cc) in enumerate(CH):
                            fw.op(fw.dve, lambda t0=t0, cc=cc: nc.vector.tensor_tensor(out=dst[:, 0, t0:t0 + cc], in0=XF[:, t0:t0 + cc], in1=eqk[:, which, 0:cc], op=ALU.mult),
                                  reads=[tXF, teqk], writes=[tdst])
                        if which == 1:
                            fw.op(fw.act, lambda: nc.scalar.copy(out=KRB[:, 0, :], in_=XF[:]), reads=[tXF], writes=[tKRB])
                    return on_tile

                gemm_fm(g, WN, [[(h * 128, 128)]], rot(0))
                gemm_fm(g, WN, [[(512 + h * 128, 128)]], rot(1))

                def on_v(c0, n, ps, tps):
                    vt = (c0 - (1024 + h * 256)) // 128
                    fw.op(fw.act, lambda: nc.scalar.copy(out=VF[:, vt, :], in_=ps[:, 0:T]), reads=tps, writes=[tVF])
                gemm_fm(g, WN, [[(1024 + h * 256, 256)]], on_v)
                g.tr_i = 0
                transposes_to_tok(g, KRB, tKRB, 1, KOT, tKO, scale_ap=lambda cc: g.kodec[0:cc, h, (0 if cc == 128 else 1):(1 if cc == 128 else 2)])
                transposes_to_tok(g, VF, tVF, 2, VT, tV)
                fw.release()
            with ExitStack() as ph3:
                dec_ap = lambda kt, c: g.kodec[:, h, (2 if CH[c][1] == 128 else 3):(3 if CH[c][1] == 128 else 4)]
                recurrence(g, p, 1, 2, QIN, tQ, KIN, tK, KOT, tKO, VT, tV, OT, tOT, dec_ap,
                           g.st_ret[p, h], g.o_ret_p[h], g.o_ret_s[p, h], g.cr_ret[h], g.t_cr["ret"][h], ph3)
                fw.release()

            def on_g(c0, n, ps, tps):
                vt = (c0 - (2048 + h * 256)) // 128
                fw.op(fw.act, lambda: nc.scalar.activation(out=SG[:, vt, :], in_=ps[:, 0:T], func=AF.Silu), reads=tps, writes=[tSG])
            gemm_fm(g, WN, [[(2048 + h * 256, 256)]], on_g)
            with ExitStack() as ph4:
                head_norm_gate(g, OT, tOT, 2, SG, tSG, 2 * h, True, None, ph4)
                fw.release()
    gemm_acc(g, "w_out_e", 0, 8)
    fw.release()
    for h in range(8):
        with ExitStack() as ph:
            SBp = lambda n, s, d=F32: g.SB(n, s, d, ph)
            dec = SBp("dec", [128, 9]); g.tdec = Tile()
            QIN = SBp("QIN", [128, 1, T], BF16); tQ = Tile()
            KIN = SBp("KIN", [128, 1, T], BF16); tK = Tile()
            KOT = SBp("KOT", [128, 9, 128], BF16); tKO = Tile()
            VT = SBp("VT", [128, 9, 128], BF16); tV = Tile()
            OT = SBp("OT", [128, 1, T], BF16); tOT = Tile()
            SG = SBp("SG", [128, 1, T], BF16); tSG = Tile()
            with ExitStack() as ph1:
                SB1 = lambda n, s, d=F32: g.SB(n, s, d, ph1)
                sm, tsm = smask_setup(g, ph1)
                QS = SB1("QS", [128, T]); tQS = Tile()
                FF = SB1("FF", [128, T]); tFF = Tile()
                BB = SB1("BB", [128, T]); tBB = Tile()
                EE = SB1("EE", [128, T]); tEE = Tile()
                KOF = SB1("KOF", [128, 1, T], BF16); tKOF = Tile()
                VF = SB1("VF", [128, 1, T], BF16); tVF = Tile()

                def on_q(c0, n, ps, tps):
                    fw.op(fw.act, lambda: nc.scalar.activation(out=QS[:], in_=ps[:, 0:T], func=AF.Silu), reads=tps, writes=[tQS])

                def on_f(c0, n, ps, tps):
                    fw.op(fw.act, lambda: nc.scalar.activation(out=FF[:], in_=ps[:, 0:T], func=AF.Sigmoid), reads=tps, writes=[tFF])
                    fw.op(fw.dve, lambda: nc.vector.tensor_scalar(out=FF[:], in0=FF[:], scalar1=g.lb2[:, h:h + 1], scalar2=g.lb1[:, h:h + 1], op0=ALU.mult, op1=ALU.add),
                          reads=[tFF, g.tmisc], writes=[tFF])
                    fw.op(fw.act, lambda: nc.scalar.activation(out=BB[:], in_=FF[:], func=AF.Ln), reads=[tFF], writes=[tBB])
                    fw.op(fw.dve, lambda: nc.vector.tensor_scalar(out=FF[:], in0=FF[:], scalar1=-1.0, scalar2=1.0, op0=ALU.mult, op1=ALU.add), reads=[tFF, tBB], writes=[tFF])
                    fw.op(fw.dve, lambda: nc.vector.tensor_tensor_scan(out=BB[:], data0=sm[:], data1=BB[:], initial=0.0, op0=ALU.mult, op1=ALU.add),
                          reads=[tsm, tBB], writes=[tBB])
                    fw.op(fw.act, lambda: nc.scalar.activation(out=EE[:], in_=BB[:], func=AF.Exp), reads=[tBB], writes=[tEE])
                    fw.op(fw.dve, lambda: nc.vector.scalar_tensor_tensor(out=QIN[:, 0, :], in0=QS[:], scalar=128.0 ** -0.5, in1=EE[:], op0=ALU.mult, op1=ALU.mult),
                          reads=[tQS, tEE], writes=[tQ])
                    for c, (t0, cc) in enumerate(CH):
                        fw.op(fw.act, lambda c=c, t0=t0, cc=cc: nc.scalar.activation(out=dec[:, c:c + 1], in_=BB[:, t0 + cc - 1:t0 + cc], func=AF.Exp),
                              reads=[tBB], writes=[g.tdec])
                    fw.op(fw.act, lambda: nc.scalar.activation(out=EE[:], in_=BB[:], func=AF.Exp, scale=-1.0), reads=[tBB, tQ], writes=[tEE])
                    fw.op(fw.dve, lambda: nc.vector.tensor_tensor(out=EE[:], in0=EE[:], in1=FF[:], op=ALU.mult), reads=[tEE, tFF], writes=[tEE])
                    fw.op(fw.act, lambda: nc.scalar.copy(out=KIN[:, 0, :], in_=EE[:]), reads=[tEE], writes=[tK])
                    for c, (t0, cc) in enumerate(CH):
                        fw.op(fw.dve, lambda c=c, t0=t0, cc=cc: nc.vector.tensor_scalar(out=KOF[:, 0, t0:t0 + cc], in0=EE[:, t0:t0 + cc], scalar1=dec[:, c:c + 1], scalar2=None, op0=ALU.mult),
                              reads=[tEE, g.tdec], writes=[tKOF])

                def on_v(c0, n, ps, tps):
                    fw.op(fw.act, lambda: nc.scalar.copy(out=VF[:, 0, :], in_=ps[:, 0:T]), reads=tps, writes=[tVF])

                def on_g(c0, n, ps, tps):
                    fw.op(fw.act, lambda: nc.scalar.activation(out=SG[:, 0, :], in_=ps[:, 0:T], func=AF.Silu), reads=tps, writes=[tSG])

                gemm_fm(g, WN, [[(3072 + h * 128, 128)]], on_q)
                gemm_fm(g, WN, [[(4096 + h * 128, 128)]], on_f)
                gemm_fm(g, WN, [[(5120 + h * 128, 128)]], on_v)
                gemm_fm(g, WN, [[(6144 + h * 128, 128)]], on_g)
                g.tr_i = 0
                transposes_to_tok(g, KOF, tKOF, 1, KOT, tKO)
                transposes_to_tok(g, VF, tVF, 1, VT, tV)
                fw.release()
            with ExitStack() as ph3:
                dec_ap = lambda kt, c: dec[:, c:c + 1]
                recurrence(g, p, 1, 1, QIN, tQ, KIN, tK, KOT, tKO, VT, tV, OT, tOT, dec_ap,
                           g.st_hg[p, h], g.o_hg_p[h], g.o_hg_s[p, h], g.cr_hg[h], g.t_cr["hg"][h], ph3)
                fw.release()
            with ExitStack() as ph4:
                head_norm_gate(g, OT, tOT, 1, SG, tSG, h, False, lambda vt: g.hgnS[:, 0:1], ph4)
                fw.release()
    gemm_acc(g, "w_out_e", 1024, 8)
    fw.release()


def mixer_odd(g, p):
    fw, nc = g.fw, g.nc
    WN = "w_in_o"
    with ExitStack() as ph0:
        GK = g.SB("GK", [16, T], F32, ph0); tGK = Tile()
        wgk = g.SB("wgk", [16, 1024], F32, ph0); twgk = Tile()
        fw.dma(fw.sp, wgk[:], g.wgk2, writes=[twgk])

        def on_gk(c0, n, ps, tps):
            fw.op(fw.act, lambda: nc.scalar.copy(out=GK[:], in_=ps[0:16, 0:T]), reads=tps, writes=[tGK])
        gemm_fm(g, WN, [[(6144, 16)]], on_gk)
        for h in range(4):
            with ExitStack() as ph:
                OT = g.SB("OT", [128, 4, T], BF16, ph); tOT = Tile()
                with ExitStack() as phR:
                    SBr = lambda n, s, d=F32: g.SB(n, s, d, phR)
                    dec = SBr("dec", [128, 2, 9]); g.tdec = Tile()
                    QIN = SBr("QIN", [128, 2, T], BF16); tQ = Tile()
                    KIN = SBr("KIN", [128, 2, T], BF16); tK = Tile()
                    KOT = SBr("KOT", [128, 9, 256], BF16); tKO = Tile()
                    with ExitStack() as ph1:
                        SB1 = lambda n, s, d=F32: g.SB(n, s, d, ph1)
                        sm, tsm = smask_setup(g, ph1)
                        CS = SB1("CS", [128, 2, T]); tCS = Tile()
                        EE = SB1("EE", [128, T]); tEE = Tile()
                        KOF = SB1("KOF", [128, 2, T], BF16); tKOF = Tile()
                        for kt in range(2):
                            ps, tps = pa_set(g)
                            for ti, (t0, tn) in enumerate(NTT):
                                fw.op(fw.pe, lambda t0=t0, tn=tn, kt=kt, ps=ps: nc.tensor.matmul(ps[:, t0:t0 + tn], lhsT=wgk[0:16, h * 256 + kt * 128:h * 256 + (kt + 1) * 128], rhs=GK[0:16, t0:t0 + tn],
                                                                                             start=True, stop=True),
                                      reads=[tGK, twgk], writes=[tps[ti]], sig=(ti == 2))
                            j = h * 2 + kt
                            fw.op(fw.act, lambda kt=kt, j=j, ps=ps: nc.scalar.activation(out=CS[:, kt, :], in_=ps[:, 0:T], func=AF.Exp, scale=-1.0, bias=g.nbgk[:, j:j + 1]),
                                  reads=tps + [g.tmisc], writes=[tCS])
                            fw.op(fw.act, lambda kt=kt: nc.scalar.activation(out=CS[:, kt, :], in_=CS[:, kt, :], func=AF.Ln, bias=1.0), reads=[tCS], writes=[tCS])
                            fw.op(fw.dve, lambda kt=kt: nc.vector.tensor_tensor_scan(out=CS[:, kt, :], data0=sm[:], data1=CS[:, kt, :], initial=0.0, op0=ALU.mult, op1=ALU.add),
                                  reads=[tsm, tCS], writes=[tCS])
                            for c, (t0, cc) in enumerate(CH):
                                fw.op(fw.act, lambda kt=kt, c=c, t0=t0, cc=cc: nc.scalar.activation(out=dec[:, kt, c:c + 1], in_=CS[:, kt, t0 + cc - 1:t0 + cc], func=AF.Exp, scale=-1.0 / 16.0),
                                      reads=[tCS], writes=[g.tdec])

                        def on_q(c0, n, ps, tps):
                            kt = (c0 - h * 256) // 128
                            fw.op(fw.act, lambda: nc.scalar.activation(out=EE[:], in_=CS[:, kt, :], func=AF.Exp, scale=-1.0 / 16.0), reads=[tCS], writes=[tEE])
                            fw.op(fw.dve, lambda: nc.vector.scalar_tensor_tensor(out=QIN[:, kt, :], in0=ps[:, 0:T], scalar=1.0 / 16.0, in1=EE[:], op0=ALU.mult, op1=ALU.mult),
                                  reads=tps + [tEE], writes=[tQ])

                        def on_k(c0, n, ps, tps):
                            kt = (c0 - (1024 + h * 256)) // 128
                            fw.op(fw.act, lambda: nc.scalar.activation(out=EE[:], in_=CS[:, kt, :], func=AF.Exp, scale=1.0 / 16.0), reads=[tCS], writes=[tEE])
                            fw.op(fw.dve, lambda: nc.vector.tensor_tensor(out=EE[:], in0=EE[:], in1=ps[:, 0:T], op=ALU.mult), reads=tps + [tEE], writes=[tEE])
                            fw.op(fw.act, lambda: nc.scalar.copy(out=KIN[:, kt, :], in_=EE[:]), reads=[tEE], writes=[tK])
                            for c, (t0, cc) in enumerate(CH):
                                fw.op(fw.dve, lambda c=c, t0=t0, cc=cc: nc.vector.tensor_scalar(out=KOF[:, kt, t0:t0 + cc], in0=EE[:, t0:t0 + cc], scalar1=dec[:, kt, c:c + 1], scalar2=None, op0=ALU.mult),
                                      reads=[tEE, g.tdec], writes=[tKOF])

                        gemm_fm(g, WN, [[(h * 256, 256)]], on_q)
                        gemm_fm(g, WN, [[(1024 + h * 256, 256)]], on_k)
                        g.tr_i = 0
                        transposes_to_tok(g, KOF, tKOF, 2, KOT, tKO)
                        fw.release()
                    VT = SBr("VT", [128, 9, 512], BF16); tV = Tile()
                    with ExitStack() as ph2:
                        VF = g.SB("VF", [128, 4, T], BF16, ph2); tVF = Tile()

                        def on_v(c0, n, ps, tps):
                            vt = (c0 - (2048 + h * 512)) // 128
                            fw.op(fw.act, lambda: nc.scalar.copy(out=VF[:, vt, :], in_=ps[:, 0:T]), reads=tps, writes=[tVF])
                        gemm_fm(g, WN, [[(2048 + h * 512, 256)], [(2048 + h * 512 + 256, 256)]], on_v)
                        transposes_to_tok(g, VF, tVF, 4, VT, tV)
                        fw.release()
                    with ExitStack() as ph3:
                        dec_ap = lambda kt, c: dec[:, kt, c:c + 1]
                        recurrence(g, p, 2, 4, QIN, tQ, KIN, tK, KOT, tKO, VT, tV, OT, tOT, dec_ap,
                                   g.st_gla[p, h], g.o_gla_p[h], g.o_gla_s[p, h], g.cr_gla[h], g.t_cr["gla"][h], ph3)
                        fw.release()
                with ExitStack() as ph4:
                    SG = g.SB("SG", [128, 4, T], BF16, ph4); tSG = Tile()

                    def on_g(c0, n, ps, tps):
                        vt = (c0 - (4096 + h * 512)) // 128
                        fw.op(fw.act, lambda: nc.scalar.activation(out=SG[:, vt, :], in_=ps[:, 0:T], func=AF.Silu), reads=tps, writes=[tSG])
                    gemm_fm(g, WN, [[(4096 + h * 512, 256)], [(4096 + h * 512 + 256, 256)]], on_g)
                    head_norm_gate(g, OT, tOT, 4, SG, tSG, (h % 2) * 4, False, lambda vt: g.glanS[:, vt:vt + 1], ph4)
                    fw.release()
            if h % 2 == 1:
                gemm_acc(g, "w_out_o", (h - 1) * 512, 8)
                fw.release()
        fw.release()


def ffn(g, p, l):
    fw, nc = g.fw, g.nc
    with ExitStack() as ph:
        SBp = lambda n, s, d=F32: g.SB(n, s, d, ph)
        A = [SBp("A%d" % i, [128, 1092]) for i in range(2)]; tA = [Tile(), Tile()]
        C1 = [SBp("C1_%d" % i, [128, 1090]) for i in range(2)]; tC1 = [Tile(), Tile()]
        cvs = SBp("cvs", [128, NFT, 2]); tcvs = Tile()
        cvo = SBp("cvo", [128, NFT, 4]); tcvo = Tile()
        fw.dma(fw.sp, cvs[:], g.cv_in[p, l], writes=[tcvs])
        ft = 0
        while ft < NFT:
            nk = min(8, NFT - ft)
            for j in range(nk):
                f = ft + j
                i = f % 2
                cw = lambda k, f=f: g.convwS[:, l, f, k:k + 1]

                def on_tile(c0, n, ps, tps, f=f, i=i, j=j, cw=cw):
                    if c0 < DFF:
                        fw.op(fw.act, lambda: nc.scalar.copy(out=A[i][:, 2:1026], in_=ps[:, 0:1024]), reads=tps, writes=[tA[i]])
                        fw.op(fw.act, lambda: nc.scalar.copy(out=A[i][:, 1028:1092], in_=ps[:, 1024:1088]), reads=tps, writes=[tA[i]])
                        fw.op(fw.pool, lambda: nc.gpsimd.tensor_copy(out=A[i][:, 0:2], in_=g.cvcar[:, l, f, :]), reads=[g.tcvcar[l]], writes=[tA[i]])
                        fw.op(fw.pool, lambda: nc.gpsimd.tensor_copy(out=A[i][:, 1026:1028], in_=cvs[:, f, :]), reads=[tcvs], writes=[tA[i]])
                        fw.op(fw.pool, lambda: nc.gpsimd.tensor_copy(out=cvo[:, f, 0:2], in_=A[i][:, 1024:1026]), reads=[tA[i]], writes=[tcvo])
                        fw.op(fw.pool, lambda: nc.gpsimd.tensor_copy(out=cvo[:, f, 2:4], in_=A[i][:, 1090:1092]), reads=[tA[i]], writes=[tcvo])
                        fw.op(fw.pool, lambda: nc.gpsimd.tensor_copy(out=g.cvcar[:, l, f, :], in_=A[i][:, 1024:1026]), reads=[tA[i]], writes=[g.tcvcar[l]])
                        fw.op(fw.dve, lambda: nc.vector.tensor_scalar(out=C1[i][:], in0=A[i][:, 2:1092], scalar1=cw(2), scalar2=cw(3), op0=ALU.mult, op1=ALU.add),
                              reads=[tA[i], g.tconvw], writes=[tC1[i]])
                        fw.op(fw.dve, lambda: nc.vector.scalar_tensor_tensor(out=C1[i][:], in0=A[i][:, 1:1091], scalar=cw(1), in1=C1[i][:], op0=ALU.mult, op1=ALU.add),
                              reads=[tA[i], g.tconvw, tC1[i]], writes=[tC1[i]])
                        fw.op(fw.dve, lambda: nc.vector.scalar_tensor_tensor(out=C1[i][:], in0=A[i][:, 0:1090], scalar=cw(0), in1=C1[i][:], op0=ALU.mult, op1=ALU.add),
                              reads=[tA[i], g.tconvw, tC1[i]], writes=[tC1[i]])
                        fw.op(fw.act, lambda: nc.scalar.activation(out=C1[i][:], in_=C1[i][:], func=AF.Silu), reads=[tC1[i]], writes=[tC1[i]])
                    else:
                        fw.op(fw.dve, lambda: nc.vector.tensor_tensor(out=g.ZB[:, j, 0:1024], in0=C1[i][:, 0:1024], in1=ps[:, 0:1024], op=ALU.mult),
                              reads=tps + [tC1[i]], writes=[g.tZB[j]])
                        fw.op(fw.dve, lambda: nc.vector.tensor_tensor(out=g.ZB[:, j, 1024:1088], in0=C1[i][:, 1026:1090], in1=ps[:, 1024:1088], op=ALU.mult),
                              reads=tps + [tC1[i]], writes=[g.tZB[j]])
                gemm_fm(g, ("w_up", l), [[(f * 128, 128), (DFF + f * 128, 128)]], on_tile)
            gemm_acc(g, ("w_dn", l), ft * 128, nk)
            ft += nk
        fw.dma(fw.sp, g.o_cv[p, l], cvo[:], reads=[tcvo], is_output=True)
        fw.release()


def run_pass(g, p):
    fw, nc = g.fw, g.nc
    for kc in range(16):
        fw.dma(fw.sp if kc % 2 == 0 else fw.act, g.H[:, kc, :], g.xin[p, :, kc, :], writes=[g.tH[kc]])
    for l in range(2):
        rmsnorm(g, 2 * l)
        if l == 0:
            mixer_even(g, p)
        else:
            mixer_odd(g, p)
        rmsnorm(g, 2 * l + 1)
        ffn(g, p, l)
    rmsnorm(g, 4, final_out=True)
    for kc in range(16):
        fw.dma(fw.sp, g.yout[p, :, kc, :], g.H[:, kc, :], reads=[g.tH[kc]], is_output=True)
    fw.barrier()


_PROG = None


def _host_consts():
    f32 = np.float32
    half = 64
    inv = (f32(10000.0) ** (-(np.arange(half, dtype=f32) / f32(half)))).astype(f32)
    rope = np.zeros((NPASS, 2, 128, T), f32)
    for p in range(NPASS):
        pos = np.concatenate([p * 1024 + np.arange(1024), 1024 + np.arange(64)]).astype(f32)
        ang = (pos[:, None] * inv[None, :]).astype(f32)
        cs = np.cos(ang).astype(f32).T
        sn = np.sin(ang).astype(f32).T
        rope[p, 0] = np.concatenate([cs, cs], 0)
        rope[p, 1] = np.concatenate([sn, sn], 0)
    lg = np.log1p(-np.exp2(-5.0 - np.arange(4, dtype=np.float64)))
    tl = np.arange(128, dtype=np.float64)
    ret = np.zeros((128, 4, 2, 128), f32)
    ko = np.zeros((128, 4, 2), f32)
    dec = np.zeros((128, 4, 2), f32)
    for h in range(4):
        ret[:, h, 0, :] = np.exp(lg[h] * (tl + 1))[None, :]
        ret[:, h, 1, :] = (np.exp(-lg[h] * (tl + 1)) * 128.0 ** -0.5)[None, :]
        ko[:, h, 0] = np.exp(lg[h] * (127 - tl)) * 128.0 ** -0.5
        ko[:64, h, 1] = np.exp(lg[h] * (63 - tl[:64])) * 128.0 ** -0.5
        dec[:, h, 0] = np.exp(lg[h] * 128)
        dec[:, h, 1] = np.exp(lg[h] * 64)
    mat = np.zeros((4, 128, 128), f32)
    mat[0] = np.eye(128, dtype=f32)
    mat[1] = 1.0
    mat[2] = np.triu(np.ones((128, 128), f32))
    for m in range(64):
        mat[3][m + 64, m] = -1.0
        mat[3][m, m + 64] = 1.0
    sm = np.ones((128, T), f32)
    sm[:, 0:T:128] = 0.0
    return dict(c_rope=rope, c_ret=ret, c_ko=ko, c_dec=dec, c_mat=mat, c_smask=sm)


def kernel(x_prompt, x_sample, state_ret, state_hgrn, state_gla, cache_ffn_conv,
           norm_mix, norm_ffn, norm_final, w_in_even, w_out_even, hgrn_lb, hgrn_gnorm,
           w_in_odd, w_gk2, b_gk2, gla_gnorm, w_out_odd, ffn_w_up, ffn_conv_w, ffn_conv_b,
           ffn_w_down):
    global _PROG
    f32 = np.float32
    A = lambda a: np.ascontiguousarray(np.asarray(a, dtype=f32))
    x_prompt, x_sample = A(x_prompt), A(x_sample)
    state_ret, state_hgrn, state_gla, cache_ffn_conv = A(state_ret), A(state_hgrn), A(state_gla), A(cache_ffn_conv)
    if _PROG is None:
        _PROG = build_program()
    nc = _PROG
    consts = _host_consts()
    gam = np.stack([A(norm_mix)[0], A(norm_ffn)[0], A(norm_mix)[1], A(norm_ffn)[1], A(norm_final)], 0)
    gam = A(gam.reshape(5, 16, 128).transpose(2, 0, 1))
    cw = np.concatenate([A(ffn_conv_w), A(ffn_conv_b)[:, None, :]], 1)
    cw = A(cw.reshape(2, 4, NFT, 128).transpose(3, 0, 2, 1))
    shared = dict(
        gam=gam, convw=cw,
        lbraw=A(A(hgrn_lb).reshape(3, 8, 128).transpose(2, 0, 1)),
        hgn=A(A(hgrn_gnorm)[0][:, None]),
        wgk2=A(A(w_gk2)[0]),
        bgk=A(A(b_gk2)[0].reshape(8, 128).T),
        glan=A(A(gla_gnorm)[0].reshape(4, 128).T),
        w_in_e=A(w_in_even)[0], w_out_e=A(w_out_even)[0], w_in_o=A(w_in_odd)[0], w_out_o=A(w_out_odd)[0],
        w_up=A(ffn_w_up), w_dn=A(ffn_w_down),
    )
    shared.update(consts)
    USE = [0, 1, 4, 5]
    big = ("w_in_e", "w_out_e", "w_in_o", "w_out_o", "w_up", "w_dn")
    zshared = dict(shared)
    for k in big:
        zshared[k] = np.zeros_like(shared[k])
    zx = dict(xin=np.zeros((NPASS, 128, 16, T), f32), st_ret=np.zeros((NPASS, 4, 128, 256), f32),
              st_hg=np.zeros((NPASS, 8, 128, 128), f32), st_gla=np.zeros((NPASS, 4, 256, 512), f32),
              cv_in=np.zeros((NPASS, 2, 128, NFT, 2), f32))
    in_maps = []
    for core in range(8):
        if core not in USE:
            m = dict(zshared)
            m.update(zx)
            in_maps.append(m)
            continue
        b = USE.index(core)
        xin = np.zeros((NPASS, 128, 16, T), f32)
        st_r = np.zeros((NPASS, 4, 128, 256), f32)
        st_h = np.zeros((NPASS, 8, 128, 128), f32)
        st_g = np.zeros((NPASS, 4, 256, 512), f32)
        cv = np.zeros((NPASS, 2, 128, NFT, 2), f32)
        for p in range(NPASS):
            s = 2 * b + p
            X = np.concatenate([x_prompt[b, p * 1024:(p + 1) * 1024], x_sample[s]], 0)
            xin[p] = X.T.reshape(16, 128, T).transpose(1, 0, 2)
            st_r[p] = state_ret[0, s]
            st_h[p] = state_hgrn[0, s]
            st_g[p] = state_gla[0, s]
            cv[p] = cache_ffn_conv[:, s].reshape(2, 2, NFT, 128).transpose(0, 3, 2, 1)
        m = dict(shared)
        m.update(xin=xin, st_ret=st_r, st_hg=st_h, st_gla=st_g, cv_in=cv)
        in_maps.append(m)
    res = run_bass_kernel_spmd(nc, in_maps, core_ids=list(range(8)))
    R = res.results
    y_prompt = np.zeros((4, 2048, 2048), f32)
    y_sample = np.zeros((8, 64, 2048), f32)
    ret_p = np.zeros((1, 4, 4, 128, 256), f32); ret_s = np.zeros((1, 8, 4, 128, 256), f32)
    hg_p = np.zeros((1, 4, 8, 128, 128), f32); hg_s = np.zeros((1, 8, 8, 128, 128), f32)
    gla_p = np.zeros((1, 4, 4, 256, 512), f32); gla_s = np.zeros((1, 8, 4, 256, 512), f32)
    cv_p = np.zeros((2, 4, 2, DFF), f32); cv_s = np.zeros((2, 8, 2, DFF), f32)
    for b in range(4):
        r = R[USE[b]]
        yo = np.asarray(r["yout"])
        ocv = np.asarray(r["o_cv"])
        for p in range(NPASS):
            s = 2 * b + p
            Y = yo[p].transpose(2, 1, 0).reshape(T, 2048)
            y_prompt[b, p * 1024:(p + 1) * 1024] = Y[:1024]
            y_sample[s] = Y[1024:]
            ret_s[0, s] = np.asarray(r["o_ret_s"])[p]
            hg_s[0, s] = np.asarray(r["o_hg_s"])[p]
            gla_s[0, s] = np.asarray(r["o_gla_s"])[p]
            for l in range(2):
                cv_s[l, s] = ocv[p, l, :, :, 2:4].transpose(2, 1, 0).reshape(2, DFF)
        for l in range(2):
            cv_p[l, b] = ocv[1, l, :, :, 0:2].transpose(2, 1, 0).reshape(2, DFF)
        ret_p[0, b] = np.asarray(r["o_ret_p"])
        hg_p[0, b] = np.asarray(r["o_hg_p"])
        gla_p[0, b] = np.asarray(r["o_gla_p"])
    return (y_prompt, y_sample, ret_p, ret_s, hg_p, hg_s, gla_p, gla_s, cv_p, cv_s)
```

```python
import math
from contextlib import ExitStack

import numpy as np
import concourse.bass as bass
import concourse.mybir as mybir
from concourse.bass_utils import run_bass_kernel_spmd

F32 = mybir.dt.float32
BF16 = mybir.dt.bfloat16
ALU = mybir.AluOpType
AF = mybir.ActivationFunctionType

T = 1088
TP = 1024
NTT = [(0, 512), (512, 512), (1024, 64)]
CH = [(i * 128, 128) for i in range(8)] + [(1024, 64)]
NPASS = 2
DFF = 5504
NFT = 43
EPS = 1e-6


_GUARD = [None]


class Tile:
    __slots__ = ("name", "w", "r", "scoped")

    def __init__(self, name=""):
        self.name = name
        self.w = None
        self.scoped = _GUARD[0] is not None
        self.r = dict(_GUARD[0]) if _GUARD[0] else {}


class Eng:
    def __init__(self, name, h, sem, is_pe=False):
        self.name = name
        self.h = h
        self.sem = sem
        self.is_pe = is_pe
        self.n = 0
        self.nsig = 0
        self.sigs = []
        self.last = None
        self.waited = {}


class DmaSlot:
    def __init__(self, sem):
        self.sem = sem
        self.total = 0


class FW:
    def __init__(self, nc, stack, n_dma_sems=32):
        self.nc = nc
        es = stack.enter_context
        self.pe = Eng("pe", nc.tensor, es(nc.semaphore("s_pe")), True)
        self.act = Eng("act", nc.scalar, es(nc.semaphore("s_act")))
        self.dve = Eng("dve", nc.vector, es(nc.semaphore("s_dve")))
        self.pool = Eng("pool", nc.gpsimd, es(nc.semaphore("s_pool")))
        self.sp = Eng("sp", nc.sync, es(nc.semaphore("s_sp")))
        self.engs = [self.pe, self.act, self.dve, self.pool, self.sp]
        self.slots = [DmaSlot(es(nc.semaphore("s_dma%d" % i))) for i in range(n_dma_sems)]
        self.slots_sw = self.slots[:n_dma_sems // 2]
        self.slots_hw = self.slots[n_dma_sems // 2:]
        self.slot_i = {True: 0, False: 0}
        self.out_events = []
        self.scoped_dma = []

    def _resolve(self, ev):
        if ev[0] == "dma":
            return ev[1].sem, ev[2]
        e, idx = ev
        lo, hi = 0, len(e.sigs)
        while lo < hi:
            mid = (lo + hi) // 2
            if e.sigs[mid][0] >= idx:
                hi = mid
            else:
                lo = mid + 1
        if lo < len(e.sigs):
            return e.sem, e.sigs[lo][1]
        assert e.last is not None and e.n - 1 >= idx
        e.last.then_inc(e.sem, 1)
        e.nsig += 1
        e.sigs.append((e.n - 1, e.nsig))
        return e.sem, e.nsig

    def _wait(self, eng, evs):
        need = {}
        for ev in evs:
            if ev is None:
                continue
            sem, val = self._resolve(ev)
            k = id(sem)
            if eng.waited.get(k, 0) >= val:
                continue
            if k not in need or need[k][1] < val:
                need[k] = (sem, val)
        for k, (sem, val) in need.items():
            eng.h.wait_ge(sem, val)
            eng.waited[k] = val

    def _deps(self, eng, reads, writes, is_dma=False):
        evs = []
        for t in reads:
            if t.w is not None:
                if (not is_dma) and t.w[0] is eng and eng.is_pe:
                    continue
                evs.append(t.w)
        for t in writes:
            if t.w is not None:
                if not ((not is_dma) and t.w[0] is eng and eng.is_pe):
                    evs.append(t.w)
            for ev in t.r.values():
                if (not is_dma) and ev[0] is eng and eng.is_pe:
                    continue
                evs.append(ev)
        return evs

    def _record(self, ev, reads, writes):
        key = id(ev[0]) if ev[0] != "dma" else ("d", id(ev[1]))
        for t in reads:
            t.r[key] = ev
        for t in writes:
            t.w = ev
            t.r = {}

    def op(self, eng, fn, reads=(), writes=(), sig=None):
        self._wait(eng, self._deps(eng, reads, writes))
        inst = fn()
        eng.last = inst
        ev = (eng, eng.n)
        eng.n += 1
        if sig is None:
            sig = not eng.is_pe
        if sig:
            inst.then_inc(eng.sem, 1)
            eng.nsig += 1
            eng.sigs.append((eng.n - 1, eng.nsig))
        self._record(ev, reads, writes)
        return inst

    def dma(self, eng, out, in_, reads=(), writes=(), is_output=False):
        sw = eng is self.pool
        pool_ = self.slots_sw if sw else self.slots_hw
        slot = pool_[self.slot_i[sw]]
        self.slot_i[sw] = (self.slot_i[sw] + 1) % len(pool_)
        evs = self._deps(eng, reads, writes, is_dma=True)
        if slot.total:
            evs.append(("dma", slot, slot.total))
        self._wait(eng, evs)
        eng.h.dma_start(out=out, in_=in_).then_inc(slot.sem, 16)
        slot.total += 16
        ev = ("dma", slot, slot.total)
        self._record(ev, reads, writes)
        if any(t.scoped for t in reads) or any(t.scoped for t in writes):
            self.scoped_dma.append(ev)
        if is_output:
            self.out_events.append(ev)
        return ev

    def release(self):
        guard = dict(_GUARD[0] or {})
        for e in self.engs:
            if e.n:
                ev = (e, e.n - 1)
                if e.is_pe:
                    self._resolve(ev)
                guard[id(e)] = ev
        for ev in self.scoped_dma:
            guard[("d", id(ev[1]))] = ev
        self.scoped_dma = []
        _GUARD[0] = guard

    def barrier(self):
        evs = []
        for e in self.engs:
            if e.n:
                evs.append((e, e.n - 1))
        for s in self.slots:
            if s.total:
                evs.append(("dma", s, s.total))
        for e in self.engs:
            self._wait(e, evs)

    def finish(self):
        self.barrier()


class Prog:
    pass


def build_program(plan=None):
    _GUARD[0] = None
    nc = bass.Bass("TRN2", target_bir_lowering=False)

    def D(name, shape, kind="ExternalInput", dt=F32):
        return nc.dram_tensor(name, list(shape), dt, kind=kind).ap()

    g = Prog()
    g.nc = nc
    g.recording = plan is None
    g.wa_seq, g.wb_seq = ([], []) if plan is None else plan
    g.wa_pos = g.wb_pos = 0
    g.wa_issued = g.wb_issued = 0
    g.xin = D("xin", [NPASS, 128, 16, T])
    g.st_ret = D("st_ret", [NPASS, 4, 128, 256])
    g.st_hg = D("st_hg", [NPASS, 8, 128, 128])
    g.st_gla = D("st_gla", [NPASS, 4, 256, 512])
    g.cv_in = D("cv_in", [NPASS, 2, 128, NFT, 2])
    g.gam = D("gam", [128, 5, 16])
    g.convw = D("convw", [128, 2, NFT, 4])
    g.lbraw = D("lbraw", [128, 3, 8])
    g.hgn = D("hgn", [128, 1])
    g.wgk2 = D("wgk2", [16, 1024])
    g.bgk = D("bgk", [128, 8])
    g.glan = D("glan", [128, 4])
    g.w_in_e = D("w_in_e", [2048, 7168])
    g.w_out_e = D("w_out_e", [2048, 2048])
    g.w_in_o = D("w_in_o", [2048, 6160])
    g.w_out_o = D("w_out_o", [2048, 2048])
    g.w_up = D("w_up", [2, 2048, 2 * DFF])
    g.w_dn = D("w_dn", [2, DFF, 2048])
    g.c_rope = D("c_rope", [NPASS, 2, 128, T])
    g.c_ret = D("c_ret", [128, 4, 2, 128])
    g.c_ko = D("c_ko", [128, 4, 2])
    g.c_dec = D("c_dec", [128, 4, 2])
    g.c_mat = D("c_mat", [4, 128, 128])
    g.c_smask = D("c_smask", [128, T])
    EO = "ExternalOutput"
    g.yout = D("yout", [NPASS, 128, 16, T], EO)
    g.o_ret_p = D("o_ret_p", [4, 128, 256], EO)
    g.o_ret_s = D("o_ret_s", [NPASS, 4, 128, 256], EO)
    g.o_hg_p = D("o_hg_p", [8, 128, 128], EO)
    g.o_hg_s = D("o_hg_s", [NPASS, 8, 128, 128], EO)
    g.o_gla_p = D("o_gla_p", [4, 256, 512], EO)
    g.o_gla_s = D("o_gla_s", [NPASS, 4, 256, 512], EO)
    g.o_cv = D("o_cv", [NPASS, 2, 128, NFT, 4], EO)
    g.cr_ret = D("cr_ret", [4, 128, 256], "Internal")
    g.cr_hg = D("cr_hg", [8, 128, 128], "Internal")
    g.cr_gla = D("cr_gla", [4, 256, 512], "Internal")
    g.t_cr = {"ret": [Tile() for _ in range(4)], "hg": [Tile() for _ in range(8)],
              "gla": [Tile() for _ in range(4)]}

    with ExitStack() as st:
        es = st.enter_context
        fw = FW(nc, st)
        g.fw = fw

        g.uid = 0

        def SB(name, shape, dt=F32, stack=None):
            g.uid += 1
            return (stack or st).enter_context(nc.sbuf_tensor("%s_%d" % (name, g.uid), list(shape), dt))
        g.SB = SB

        g.H = SB("H", [128, 16, T]); g.tH = [Tile("H%d" % i) for i in range(16)]
        g.HN = SB("HN", [128, 16, T], BF16); g.tHN = [Tile("HN%d" % i) for i in range(16)]
        g.ZB = SB("ZB", [128, 8, T], BF16); g.tZB = [Tile("ZB%d" % i) for i in range(8)]
        g.WA = [SB("WA%d" % i, [128, 16, 256], BF16) for i in range(2)]; g.tWA = [Tile(), Tile()]
        g.WB = [SB("WB%d" % i, [128, 8, 256], BF16) for i in range(2)]; g.tWB = [Tile(), Tile()]
        g.wmap = {"w_in_e": g.w_in_e, "w_out_e": g.w_out_e, "w_in_o": g.w_in_o, "w_out_o": g.w_out_o,
                  ("w_up", 0): g.w_up[0], ("w_up", 1): g.w_up[1], ("w_dn", 0): g.w_dn[0], ("w_dn", 1): g.w_dn[1]}
        g.gamS = SB("gamS", [128, 5, 16]); g.tgam = Tile()
        g.convwS = SB("convwS", [128, 2, NFT, 4]); g.tconvw = Tile()
        g.cmatF = SB("cmatF", [128, 4, 128]); g.tcmat = Tile()
        g.identB = SB("identB", [128, 128], BF16)
        g.onesB = SB("onesB", [128, 128], BF16)
        g.cvcar = SB("cvcar", [128, 2, NFT, 2]); g.tcvcar = [Tile(), Tile()]
        g.lbS = SB("lbS", [128, 3, 8]); g.tlb = Tile()
        g.lb1 = SB("lb1", [128, 8]); g.lb2 = SB("lb2", [128, 8])
        g.hgnS = SB("hgnS", [128, 1])
        g.bgkS = SB("bgkS", [128, 8]); g.nbgk = SB("nbgk", [128, 8])
        g.glanS = SB("glanS", [128, 4])
        g.tmisc = Tile()
        g.kodec = SB("kodec", [128, 4, 4])
        g.PA = [es(nc.psum_tensor("PA%d" % i, [128, 1536], F32)) for i in range(2)]
        g.PB = [es(nc.psum_tensor("PB%d" % i, [128, 512], F32)) for i in range(2)]
        g.bank = [g.PA[0][:, 0:512], g.PA[0][:, 512:1024], g.PA[0][:, 1024:1536],
                  g.PA[1][:, 0:512], g.PA[1][:, 512:1024], g.PA[1][:, 1024:1536],
                  g.PB[0][:, :], g.PB[1][:, :]]
        g.tbank = [Tile("bank%d" % i) for i in range(8)]
        g.pa_i = 0

        load_consts(g)
        _GUARD[0] = {}
        for p in range(NPASS):
            run_pass(g, p)
        fw.finish()
    if plan is None:
        return build_program((g.wa_seq, g.wb_seq))
    return nc


def load_consts(g):
    fw, nc = g.fw, g.nc
    fw.dma(fw.sp, g.gamS[:], g.gam, writes=[g.tgam])
    fw.dma(fw.sp, g.convwS[:], g.convw, writes=[g.tconvw])
    fw.dma(fw.sp, g.cmatF[:], g.c_mat.rearrange("m p n -> p m n"), writes=[g.tcmat])
    fw.dma(fw.sp, g.lbS[:], g.lbraw, writes=[g.tlb])
    fw.dma(fw.sp, g.hgnS[:], g.hgn, writes=[g.tmisc])
    fw.dma(fw.sp, g.bgkS[:], g.bgk, writes=[g.tmisc])
    fw.dma(fw.sp, g.glanS[:], g.glan, writes=[g.tmisc])
    fw.dma(fw.sp, g.kodec[:, :, 0:2], g.c_ko, writes=[g.tmisc])
    fw.dma(fw.sp, g.kodec[:, :, 2:4], g.c_dec, writes=[g.tmisc])
    fw.op(fw.dve, lambda: nc.vector.tensor_copy(out=g.identB[:], in_=g.cmatF[:, 0, :]), reads=[g.tcmat], writes=[g.tmisc])
    fw.op(fw.dve, lambda: nc.vector.tensor_copy(out=g.onesB[:], in_=g.cmatF[:, 1, :]), reads=[g.tcmat], writes=[g.tmisc])
    fw.op(fw.act, lambda: nc.scalar.activation(out=g.lbS[:], in_=g.lbS[:], func=AF.Exp), reads=[g.tlb], writes=[g.tlb])
    fw.op(fw.dve, lambda: nc.vector.tensor_tensor(out=g.lb2[:], in0=g.lbS[:, 0, :], in1=g.lbS[:, 1, :], op=ALU.add), reads=[g.tlb], writes=[g.tmisc])
    fw.op(fw.dve, lambda: nc.vector.tensor_tensor(out=g.lb2[:], in0=g.lb2[:], in1=g.lbS[:, 2, :], op=ALU.add), reads=[g.tlb, g.tmisc], writes=[g.tmisc])
    fw.op(fw.dve, lambda: nc.vector.reciprocal(out=g.lb2[:], in_=g.lb2[:]), reads=[g.tmisc], writes=[g.tmisc])
    fw.op(fw.dve, lambda: nc.vector.tensor_tensor(out=g.lb1[:], in0=g.lbS[:, 0, :], in1=g.lb2[:], op=ALU.mult), reads=[g.tlb, g.tmisc], writes=[g.tmisc])
    fw.op(fw.dve, lambda: nc.vector.tensor_scalar(out=g.lb2[:], in0=g.lb1[:], scalar1=-1.0, scalar2=1.0, op0=ALU.mult, op1=ALU.add), reads=[g.tmisc], writes=[g.tmisc])
    fw.op(fw.dve, lambda: nc.vector.tensor_scalar(out=g.nbgk[:], in0=g.bgkS[:], scalar1=-1.0, scalar2=None, op0=ALU.mult), reads=[g.tmisc], writes=[g.tmisc])
    fw.op(fw.pool, lambda: nc.gpsimd.memset(g.cvcar[:], 0.0), writes=g.tcvcar)
    fw.barrier()


def pa_set(g):
    i = g.pa_i
    g.pa_i ^= 1
    return g.PA[i], [g.tbank[3 * i], g.tbank[3 * i + 1], g.tbank[3 * i + 2]]


def _wa_fetch(g, i):
    fw = g.fw
    wname, segs = g.wa_seq[i]
    Wv = g.wmap[wname].rearrange("(kc p) n -> p kc n", p=128)
    b = i % 2
    off = 0
    for (c0, ncol) in segs:
        fw.dma(fw.pool, g.WA[b][:, :, off:off + ncol], Wv[:, :, c0:c0 + ncol], writes=[g.tWA[b]])
        off += ncol
    g.wa_issued = i + 1


def gemm_fm(g, wname, blocks, on_tile):
    fw, nc = g.fw, g.nc
    for segs in blocks:
        i = g.wa_pos
        g.wa_pos += 1
        if g.recording:
            g.wa_seq.append((wname, segs))
        assert g.wa_seq[i] == (wname, segs)
        if g.wa_issued <= i:
            _wa_fetch(g, i)
        if i + 1 < len(g.wa_seq) and g.wa_issued <= i + 1 and not g.recording:
            _wa_fetch(g, i + 1)
        b = i % 2
        off = 0
        mts = []
        for (c0, ncol) in segs:
            m0 = 0
            while m0 < ncol:
                n = min(128, ncol - m0)
                mts.append((c0 + m0, off + m0, n))
                m0 += n
            off += ncol
        for (cg, m0, n) in mts:
            ps, tps = pa_set(g)
            for kc in range(16):
                for ti, (t0, tn) in enumerate(NTT):
                    fw.op(fw.pe, lambda kc=kc, t0=t0, tn=tn, m0=m0, n=n, b=b, ps=ps:
                          nc.tensor.matmul(ps[0:n, t0:t0 + tn], lhsT=g.WA[b][:, kc, m0:m0 + n],
                                           rhs=g.HN[:, kc, t0:t0 + tn], start=(kc == 0), stop=(kc == 15)),
                          reads=[g.tWA[b], g.tHN[kc]], writes=[tps[ti]], sig=(kc == 15 and ti == 2))
            on_tile(cg, n, ps, tps)


def _wb_fetch(g, i):
    fw = g.fw
    wname, row0, nk, c0 = g.wb_seq[i]
    Wv = g.wmap[wname][row0:row0 + nk * 128, :].rearrange("(kc p) n -> p kc n", p=128)
    b = i % 2
    fw.dma(fw.pool, g.WB[b][:, 0:nk, :], Wv[:, :, c0:c0 + 256], writes=[g.tWB[b]])
    g.wb_issued = i + 1


def gemm_acc(g, wname, row0, nk):
    fw, nc = g.fw, g.nc
    for blk in range(8):
        c0 = blk * 256
        i = g.wb_pos
        g.wb_pos += 1
        if g.recording:
            g.wb_seq.append((wname, row0, nk, c0))
        assert g.wb_seq[i] == (wname, row0, nk, c0)
        if g.wb_issued <= i:
            _wb_fetch(g, i)
        if i + 1 < len(g.wb_seq) and g.wb_issued <= i + 1 and not g.recording:
            _wb_fetch(g, i + 1)
        b = i % 2
        for mi in range(2):
            m = blk * 2 + mi
            ps, tps = pa_set(g)
            for kc in range(nk):
                for ti, (t0, tn) in enumerate(NTT):
                    fw.op(fw.pe, lambda kc=kc, t0=t0, tn=tn, mi=mi, b=b, ps=ps:
                          nc.tensor.matmul(ps[:, t0:t0 + tn], lhsT=g.WB[b][:, kc, mi * 128:(mi + 1) * 128],
                                           rhs=g.ZB[:, kc, t0:t0 + tn], start=(kc == 0), stop=(kc == nk - 1)),
                          reads=[g.tWB[b], g.tZB[kc]], writes=[tps[ti]], sig=(kc == nk - 1 and ti == 2))
            fw.op(fw.dve, lambda m=m, ps=ps: nc.vector.tensor_tensor(out=g.H[:, m, :], in0=g.H[:, m, :], in1=ps[:, 0:T], op=ALU.add),
                  reads=tps + [g.tH[m]], writes=[g.tH[m]])


def rmsnorm(g, gi, final_out=None):
    fw, nc = g.fw, g.nc
    with ExitStack() as ph:
        sq = [g.SB("rn_sq%d" % i, [128, T], BF16, ph) for i in range(3)]
        tsq = [Tile(), Tile(), Tile()]
        rstd = g.SB("rn_rstd", [128, T], F32, ph)
        trs = Tile()
        tmp = g.SB("rn_tmp", [128, T], F32, ph)
        ttmp = Tile()
        ps, tps = pa_set(g)
        for kc in range(16):
            i = kc % 3
            if i == 0:
                fw.op(fw.act, lambda kc=kc, i=i: nc.scalar.activation(out=sq[i][:], in_=g.H[:, kc, :], func=AF.Square),
                      reads=[g.tH[kc]], writes=[tsq[i]])
            else:
                eng = fw.dve if i == 1 else fw.pool
                fw.op(eng, lambda kc=kc, i=i, eng=eng: eng.h.tensor_tensor(out=sq[i][:], in0=g.H[:, kc, :], in1=g.H[:, kc, :], op=ALU.mult),
                      reads=[g.tH[kc]], writes=[tsq[i]])
            for ti, (t0, tn) in enumerate(NTT):
                fw.op(fw.pe, lambda kc=kc, i=i, t0=t0, tn=tn: nc.tensor.matmul(ps[:, t0:t0 + tn], lhsT=g.onesB[:], rhs=sq[i][:, t0:t0 + tn],
                                                                          start=(kc == 0), stop=(kc == 15)),
                      reads=[tsq[i], g.tmisc], writes=[tps[ti]], sig=(ti == 2))
        fw.op(fw.dve, lambda: nc.vector.tensor_scalar(out=rstd[:], in0=ps[:, 0:T], scalar1=1.0 / 2048.0, scalar2=EPS, op0=ALU.mult, op1=ALU.add),
              reads=tps, writes=[trs])
        fw.op(fw.act, lambda: nc.scalar.activation(out=rstd[:], in_=rstd[:], func=AF.Ln), reads=[trs], writes=[trs])
        fw.op(fw.act, lambda: nc.scalar.activation(out=rstd[:], in_=rstd[:], func=AF.Exp, scale=-0.5), reads=[trs], writes=[trs])
        for kc in range(16):
            dst, tdst = (g.HN, g.tHN) if final_out is None else (g.H, g.tH)
            if kc % 2 == 0:
                fw.op(fw.dve, lambda kc=kc, dst=dst: nc.vector.scalar_tensor_tensor(out=dst[:, kc, :], in0=g.H[:, kc, :], scalar=g.gamS[:, gi, kc:kc + 1],
                                                                                    in1=rstd[:], op0=ALU.mult, op1=ALU.mult),
                      reads=[g.tH[kc], trs, g.tgam], writes=[tdst[kc]])
            else:
                fw.op(fw.pool, lambda kc=kc: nc.gpsimd.tensor_tensor(out=tmp[:], in0=g.H[:, kc, :], in1=rstd[:], op=ALU.mult),
                      reads=[g.tH[kc], trs], writes=[ttmp])
                fw.op(fw.act, lambda kc=kc, dst=dst: nc.scalar.activation(out=dst[:, kc, :], in_=tmp[:], func=AF.Copy, scale=g.gamS[:, gi, kc:kc + 1]),
                      reads=[ttmp, g.tgam], writes=[tdst[kc]])
        fw.release()


def transposes_to_tok(g, src, tsrc, nf, dst, tdst, scale_ap=None):
    fw, nc = g.fw, g.nc
    for c, (t0, cc) in enumerate(CH):
        for f0 in range(0, nf, 4):
            fn = min(4, nf - f0)
            bi = 4 + (g.tr_i % 4)
            use_act = (g.tr_i % 2 == 0)
            g.tr_i += 1
            pb = g.bank[bi].bitcast(BF16)
            for f in range(fn):
                fw.op(fw.pe, lambda f=f, f0=f0, t0=t0, cc=cc, pb=pb: nc.tensor.transpose(pb[0:cc, f * 128:(f + 1) * 128], src[:, f0 + f, t0:t0 + cc], g.identB[:]),
                      reads=[tsrc[c], g.tmisc], writes=[g.tbank[bi]], sig=(f == fn - 1))
            o_ap = dst[0:cc, c, f0 * 128:(f0 + fn) * 128]
            i_ap = pb[0:cc, 0:fn * 128]
            if scale_ap is None:
                if use_act:
                    fw.op(fw.act, lambda o_ap=o_ap, i_ap=i_ap: nc.scalar.copy(out=o_ap, in_=i_ap), reads=[g.tbank[bi]], writes=[tdst[c]])
                else:
                    fw.op(fw.dve, lambda o_ap=o_ap, i_ap=i_ap: nc.vector.tensor_copy(out=o_ap, in_=i_ap), reads=[g.tbank[bi]], writes=[tdst[c]])
            else:
                sc = scale_ap(cc)
                if use_act:
                    fw.op(fw.act, lambda o_ap=o_ap, i_ap=i_ap, sc=sc: nc.scalar.activation(out=o_ap, in_=i_ap, func=AF.Copy, scale=sc),
                          reads=[g.tbank[bi], g.tmisc], writes=[tdst[c]])
                else:
                    fw.op(fw.dve, lambda o_ap=o_ap, i_ap=i_ap, sc=sc: nc.vector.tensor_scalar(out=o_ap, in0=i_ap, scalar1=sc, scalar2=None, op0=ALU.mult),
                          reads=[g.tbank[bi], g.tmisc], writes=[tdst[c]])


def recurrence(g, p, nk, nv, QIN, tQ, KIN, tK, KOT, tKO, VT, tV, OT, tOT, dec_ap, st_in, st_out_p, st_out_s, carry, tcarry, ph):
    fw, nc = g.fw, g.nc
    V = nv * 128
    LA = 4 if nk == 1 else 2
    NR = LA + 1
    S = g.SB("S", [128, nk, V], F32, ph); tS = Tile()
    Sb = [g.SB("Sb%d" % i, [128, nk, V], BF16, ph) for i in range(NR)]; tSb = [Tile() for _ in range(NR)]
    scm = g.SB("scm", [128, 9, 128], BF16, ph); tscm = [Tile() for _ in range(9)]

    def view(d):
        return d.rearrange("(kt p) v -> p kt v", p=128)

    if p == 0:
        fw.op(fw.pool, lambda: nc.gpsimd.memset(S[:], 0.0), writes=[tS])
        fw.op(fw.pool, lambda: nc.gpsimd.memset(Sb[0][:], 0.0), writes=[tSb[0]])
    else:
        fw.dma(fw.sp, S[:], view(carry), reads=[tcarry], writes=[tS])
        fw.op(fw.act, lambda: nc.scalar.copy(out=Sb[0][:], in_=S[:]), reads=[tS], writes=[tSb[0]])
    for c, (t0, cc) in enumerate(CH):
        bsc = c % 2
        psc = g.bank[bsc]
        for kt in range(nk):
            fw.op(fw.pe, lambda kt=kt, t0=t0, cc=cc, psc=psc: nc.tensor.matmul(psc[0:cc, 0:cc], lhsT=KIN[:, kt, t0:t0 + cc], rhs=QIN[:, kt, t0:t0 + cc],
                                                                         start=(kt == 0), stop=(kt == nk - 1)),
                  reads=[tK[c], tQ[c]], writes=[g.tbank[bsc]], sig=(kt == nk - 1))
        fw.op(fw.dve, lambda c=c, cc=cc, psc=psc: nc.vector.tensor_tensor(out=scm[0:cc, c, 0:cc], in0=psc[0:cc, 0:cc], in1=g.cmatF[0:cc, 2, 0:cc], op=ALU.mult),
              reads=[g.tbank[bsc], g.tcmat], writes=[tscm[c]])

    def scan_step(c):
        t0, cc = CH[c]
        if c == 8:
            if p == 0:
                fw.dma(fw.sp, view(carry), S[:], reads=[tS], writes=[tcarry])
            else:
                fw.dma(fw.sp, view(st_out_p), S[:], reads=[tS], is_output=True)
            fw.dma(fw.sp, S[:], view(st_in), writes=[tS])
            fw.op(fw.act, lambda: nc.scalar.copy(out=Sb[8 % NR][:], in_=S[:]), reads=[tS], writes=[tSb[8 % NR]])
        for kt in range(nk):
            bs = 4 + ((nk * c + kt) % 4)
            pS = g.bank[bs]
            fw.op(fw.pe, lambda kt=kt, c=c, cc=cc, pS=pS: nc.tensor.matmul(pS[:, 0:V], lhsT=KOT[0:cc, c, kt * 128:(kt + 1) * 128], rhs=VT[0:cc, c, 0:V],
                                                                     start=True, stop=True),
                  reads=[tKO[c], tV[c]], writes=[g.tbank[bs]], sig=True)
            fw.op(fw.dve, lambda kt=kt, c=c, pS=pS: nc.vector.scalar_tensor_tensor(out=S[:, kt, :], in0=S[:, kt, :], scalar=dec_ap(kt, c), in1=pS[:, 0:V],
                                                                               op0=ALU.mult, op1=ALU.add),
                  reads=[tS, g.tbank[bs], g.tmisc, g.tdec[c]], writes=[tS])
        if c < 7:
            r = (c + 1) % NR
            fw.op(fw.act, lambda r=r: nc.scalar.copy(out=Sb[r][:], in_=S[:]), reads=[tS], writes=[tSb[r]])
        if c == 8:
            fw.dma(fw.sp, view(st_out_s), S[:], reads=[tS], is_output=True)

    def out_step(c):
        t0, cc = CH[c]
        bo = c % 4
        po = g.bank[bo]
        r = c % NR
        for vt in range(nv):
            fw.op(fw.pe, lambda vt=vt, c=c, cc=cc, po=po: nc.tensor.matmul(po[:, vt * cc:(vt + 1) * cc], lhsT=VT[0:cc, c, vt * 128:(vt + 1) * 128],
                                                                     rhs=scm[0:cc, c, 0:cc], start=True, stop=False),
                  reads=[tV[c], tscm[c]], writes=[g.tbank[bo]])
            for kt in range(nk):
                fw.op(fw.pe, lambda vt=vt, kt=kt, t0=t0, cc=cc, po=po, r=r: nc.tensor.matmul(po[:, vt * cc:(vt + 1) * cc], lhsT=Sb[r][:, kt, vt * 128:(vt + 1) * 128],
                                                                                      rhs=QIN[:, kt, t0:t0 + cc], start=False, stop=(kt == nk - 1)),
                      reads=[tSb[r], tQ[c]], writes=[g.tbank[bo]], sig=(kt == nk - 1 and vt == nv - 1))
        fw.op(fw.act, lambda t0=t0, cc=cc, po=po: nc.scalar.copy(out=OT[:, :, t0:t0 + cc], in_=po[:, 0:nv * cc].rearrange("p (v c) -> p v c", v=nv)),
              reads=[g.tbank[bo]], writes=[tOT[c]])

    for c in range(LA):
        scan_step(c)
    for c in range(9):
        out_step(c)
        if c + LA < 9:
            scan_step(c + LA)


def head_norm_gate(g, OT, tOT, nv, SG, tSG, slot0, center, gn_ap, ph):
    fw, nc = g.fw, g.nc
    V = nv * 128
    sq = [g.SB("hn_sq%d" % i, [128, T], BF16, ph) for i in range(2)]; tsq = [Tile(), Tile()]
    rstd = g.SB("hn_rstd", [128, T], F32, ph); trs = Tile()
    mean = g.SB("hn_mean", [128, T], F32, ph) if center else None; tmn = Tile()
    tmp = g.SB("hn_tmp", [128, T], F32, ph); ttmp = Tile()
    ps2, tps2 = pa_set(g)
    for vt in range(nv):
        i = vt % 2
        fw.op(fw.act, lambda vt=vt, i=i: nc.scalar.activation(out=sq[i][:], in_=OT[:, vt, :], func=AF.Square), reads=list(tOT), writes=[tsq[i]])
        for ti, (t0, tn) in enumerate(NTT):
            fw.op(fw.pe, lambda vt=vt, i=i, t0=t0, tn=tn: nc.tensor.matmul(ps2[:, t0:t0 + tn], lhsT=g.onesB[:], rhs=sq[i][:, t0:t0 + tn],
                                                                      start=(vt == 0), stop=(vt == nv - 1)),
                  reads=[tsq[i], g.tmisc], writes=[tps2[ti]], sig=(ti == 2))
    if center:
        ps1, tps1 = pa_set(g)
        for vt in range(nv):
            for ti, (t0, tn) in enumerate(NTT):
                fw.op(fw.pe, lambda vt=vt, t0=t0, tn=tn: nc.tensor.matmul(ps1[:, t0:t0 + tn], lhsT=g.onesB[:], rhs=OT[:, vt, t0:t0 + tn],
                                                                     start=(vt == 0), stop=(vt == nv - 1)),
                      reads=list(tOT) + [g.tmisc], writes=[tps1[ti]], sig=(ti == 2 and vt == nv - 1))
        fw.op(fw.dve, lambda: nc.vector.tensor_scalar(out=mean[:], in0=ps1[:, 0:T], scalar1=1.0 / V, scalar2=None, op0=ALU.mult), reads=tps1, writes=[tmn])
        fw.op(fw.dve, lambda: nc.vector.tensor_tensor(out=tmp[:], in0=mean[:], in1=mean[:], op=ALU.mult), reads=[tmn], writes=[ttmp])
        fw.op(fw.dve, lambda: nc.vector.scalar_tensor_tensor(out=rstd[:], in0=ps2[:, 0:T], scalar=1.0 / V, in1=tmp[:], op0=ALU.mult, op1=ALU.subtract),
              reads=tps2 + [ttmp], writes=[trs])
        fw.op(fw.dve, lambda: nc.vector.tensor_scalar(out=rstd[:], in0=rstd[:], scalar1=EPS, scalar2=None, op0=ALU.add), reads=[trs], writes=[trs])
    else:
        fw.op(fw.dve, lambda: nc.vector.tensor_scalar(out=rstd[:], in0=ps2[:, 0:T], scalar1=1.0 / V, scalar2=EPS, op0=ALU.mult, op1=ALU.add), reads=tps2, writes=[trs])
    fw.op(fw.act, lambda: nc.scalar.activation(out=rstd[:], in_=rstd[:], func=AF.Ln), reads=[trs], writes=[trs])
    fw.op(fw.act, lambda: nc.scalar.activation(out=rstd[:], in_=rstd[:], func=AF.Exp, scale=-0.5), reads=[trs], writes=[trs])
    for vt in range(nv):
        if center:
            fw.op(fw.dve, lambda vt=vt: nc.vector.tensor_tensor(out=tmp[:], in0=OT[:, vt, :], in1=mean[:], op=ALU.subtract), reads=list(tOT) + [tmn], writes=[ttmp])
            fw.op(fw.dve, lambda: nc.vector.tensor_tensor(out=tmp[:], in0=tmp[:], in1=rstd[:], op=ALU.mult), reads=[ttmp, trs], writes=[ttmp])
        else:
            fw.op(fw.dve, lambda vt=vt: nc.vector.scalar_tensor_tensor(out=tmp[:], in0=OT[:, vt, :], scalar=gn_ap(vt), in1=rstd[:], op0=ALU.mult, op1=ALU.mult),
                  reads=list(tOT) + [trs, g.tmisc], writes=[ttmp])
        fw.op(fw.dve, lambda vt=vt: nc.vector.tensor_tensor(out=g.ZB[:, slot0 + vt, :], in0=tmp[:], in1=SG[:, vt, :], op=ALU.mult),
              reads=[ttmp, tSG], writes=[g.tZB[slot0 + vt]])


def smask_setup(g, ph):
    fw, nc = g.fw, g.nc
    sm = g.SB("smask", [128, T], F32, ph)
    tsm = Tile()
    fw.dma(fw.sp, sm[:], g.c_smask, writes=[tsm])
    return sm, tsm


def mixer_even(g, p):
    fw, nc = g.fw, g.nc
    WN = "w_in_e"
    g.tdec = [Tile() for _ in range(9)]
    for h in range(4):
        with ExitStack() as ph:
            SBp = lambda n, s, d=F32: g.SB(n, s, d, ph)
            QIN = SBp("QIN", [128, 1, T], BF16); tQ = [Tile() for _ in range(9)]
            KIN = SBp("KIN", [128, 1, T], BF16); tK = [Tile() for _ in range(9)]
            KOT = SBp("KOT", [128, 9, 128], BF16); tKO = [Tile() for _ in range(9)]
            VT = SBp("VT", [128, 9, 256], BF16); tV = [Tile() for _ in range(9)]
            OT = SBp("OT", [128, 2, T], BF16); tOT = [Tile() for _ in range(9)]
            SG = SBp("SG", [128, 2, T], BF16); tSG = Tile()
            with ExitStack() as ph1:
                SB1 = lambda n, s, d=F32: g.SB(n, s, d, ph1)
                cs = SB1("cs", [128, 2, T]); tcs = Tile()
                eqk = SB1("eqk", [128, 2, 128]); teqk = Tile()
                fw.dma(fw.sp, cs[:], g.c_rope[p].rearrange("m p t -> p m t"), writes=[tcs])
                fw.dma(fw.sp, eqk[:], g.c_ret[:, h, :, :], writes=[teqk])
                XF = SB1("XF", [128, T]); tXF = Tile()
                XR = SB1("XR", [128, T]); tXR = Tile()
                KRB = SB1("KRB", [128, 1, T], BF16); tKRB = [Tile() for _ in range(9)]
                VF = SB1("VF", [128, 2, T], BF16); tVF = [Tile() for _ in range(9)]

                def rot(which):
                    def on_tile(c0, n, ps, tps):
                        fw.op(fw.act, lambda: nc.scalar.copy(out=XF[:], in_=ps[:, 0:T]), reads=tps, writes=[tXF])
                        ps2, tps2 = pa_set(g)
                        for ti, (t0, tn) in enumerate(NTT):
                            fw.op(fw.pe, lambda t0=t0, tn=tn: nc.tensor.matmul(ps2[:, t0:t0 + tn], lhsT=g.cmatF[:, 3, :], rhs=XF[:, t0:t0 + tn], start=True, stop=True),
                                  reads=[tXF, g.tcmat], writes=[tps2[ti]], sig=(ti == 2))
                        fw.op(fw.dve, lambda: nc.vector.tensor_tensor(out=XR[:], in0=ps2[:, 0:T], in1=cs[:, 1, :], op=ALU.mult), reads=tps2 + [tcs], writes=[tXR])
                        fw.op(fw.dve, lambda: nc.vector.tensor_tensor(out=XF[:], in0=XF[:], in1=cs[:, 0, :], op=ALU.mult), reads=[tXF, tcs], writes=[tXF])
                        fw.op(fw.dve, lambda: nc.vector.tensor_tensor(out=XF[:], in0=XF[:], in1=XR[:], op=ALU.add), reads=[tXF, tXR], writes=[tXF])
                        dst, tdst = (QIN, tQ) if which == 0 else (KIN, tK)
                        for c, (t0, cc) in enumerate(CH):
                            fw.op(fw.dve, lambda t0=t0, cc=cc: nc.vector.tensor_tensor(out=dst[:, 0, t0:t0 + cc], in0=XF[:, t0:t0 + cc], in1=eqk[:, which, 0:cc], op=ALU.mult),
                                  reads=[tXF, teqk], writes=[tdst[c]])
                        if which == 1:
                            fw.op(fw.act, lambda: nc.scalar.copy(out=KRB[:, 0, :], in_=XF[:]), reads=[tXF], writes=tKRB)
                    return on_tile

                gemm_fm(g, WN, [[(h * 128, 128)]], rot(0))
                gemm_fm(g, WN, [[(512 + h * 128, 128)]], rot(1))

                def on_v(c0, n, ps, tps):
                    vt = (c0 - (1024 + h * 256)) // 128
                    fw.op(fw.act, lambda: nc.scalar.copy(out=VF[:, vt, :], in_=ps[:, 0:T]), reads=tps, writes=tVF)
                gemm_fm(g, WN, [[(1024 + h * 256, 256)]], on_v)
                g.tr_i = 0
                transposes_to_tok(g, KRB, tKRB, 1, KOT, tKO, scale_ap=lambda cc: g.kodec[0:cc, h, (0 if cc == 128 else 1):(1 if cc == 128 else 2)])
                transposes_to_tok(g, VF, tVF, 2, VT, tV)
                fw.release()
            with ExitStack() as ph3:
                dec_ap = lambda kt, c: g.kodec[:, h, (2 if CH[c][1] == 128 else 3):(3 if CH[c][1] == 128 else 4)]
                recurrence(g, p, 1, 2, QIN, tQ, KIN, tK, KOT, tKO, VT, tV, OT, tOT, dec_ap,
                           g.st_ret[p, h], g.o_ret_p[h], g.o_ret_s[p, h], g.cr_ret[h], g.t_cr["ret"][h], ph3)
                fw.release()

            def on_g(c0, n, ps, tps):
                vt = (c0 - (2048 + h * 256)) // 128
                fw.op(fw.act, lambda: nc.scalar.activation(out=SG[:, vt, :], in_=ps[:, 0:T], func=AF.Silu), reads=tps, writes=[tSG])
            gemm_fm(g, WN, [[(2048 + h * 256, 256)]], on_g)
            with ExitStack() as ph4:
                head_norm_gate(g, OT, tOT, 2, SG, tSG, 2 * h, True, None, ph4)
                fw.release()
    gemm_acc(g, "w_out_e", 0, 8)
    fw.release()
    for h in range(8):
        with ExitStack() as ph:
            SBp = lambda n, s, d=F32: g.SB(n, s, d, ph)
            dec = SBp("dec", [128, 9]); g.tdec = [Tile() for _ in range(9)]
            QIN = SBp("QIN", [128, 1, T], BF16); tQ = [Tile() for _ in range(9)]
            KIN = SBp("KIN", [128, 1, T], BF16); tK = [Tile() for _ in range(9)]
            KOT = SBp("KOT", [128, 9, 128], BF16); tKO = [Tile() for _ in range(9)]
            VT = SBp("VT", [128, 9, 128], BF16); tV = [Tile() for _ in range(9)]
            OT = SBp("OT", [128, 1, T], BF16); tOT = [Tile() for _ in range(9)]
            SG = SBp("SG", [128, 1, T], BF16); tSG = Tile()
            with ExitStack() as ph1:
                SB1 = lambda n, s, d=F32: g.SB(n, s, d, ph1)
                sm, tsm = smask_setup(g, ph1)
                QS = SB1("QS", [128, T]); tQS = Tile()
                FF = SB1("FF", [128, T]); tFF = Tile()
                BB = SB1("BB", [128, T]); tBB = Tile()
                EE = SB1("EE", [128, T]); tEE = Tile()
                KOF = SB1("KOF", [128, 1, T], BF16); tKOF = [Tile() for _ in range(9)]
                VF = SB1("VF", [128, 1, T], BF16); tVF = [Tile() for _ in range(9)]

                def on_q(c0, n, ps, tps):
                    fw.op(fw.act, lambda: nc.scalar.activation(out=QS[:], in_=ps[:, 0:T], func=AF.Silu), reads=tps, writes=[tQS])

                def on_f(c0, n, ps, tps):
                    fw.op(fw.act, lambda: nc.scalar.activation(out=FF[:], in_=ps[:, 0:T], func=AF.Sigmoid), reads=tps, writes=[tFF])
                    fw.op(fw.dve, lambda: nc.vector.tensor_scalar(out=FF[:], in0=FF[:], scalar1=g.lb2[:, h:h + 1], scalar2=g.lb1[:, h:h + 1], op0=ALU.mult, op1=ALU.add),
                          reads=[tFF, g.tmisc], writes=[tFF])
                    fw.op(fw.act, lambda: nc.scalar.activation(out=BB[:], in_=FF[:], func=AF.Ln), reads=[tFF], writes=[tBB])
                    fw.op(fw.dve, lambda: nc.vector.tensor_scalar(out=FF[:], in0=FF[:], scalar1=-1.0, scalar2=1.0, op0=ALU.mult, op1=ALU.add), reads=[tFF, tBB], writes=[tFF])
                    fw.op(fw.dve, lambda: nc.vector.tensor_tensor_scan(out=BB[:], data0=sm[:], data1=BB[:], initial=0.0, op0=ALU.mult, op1=ALU.add),
                          reads=[tsm, tBB], writes=[tBB])
                    fw.op(fw.act, lambda: nc.scalar.activation(out=EE[:], in_=BB[:], func=AF.Exp), reads=[tBB], writes=[tEE])
                    fw.op(fw.dve, lambda: nc.vector.scalar_tensor_tensor(out=QIN[:, 0, :], in0=QS[:], scalar=128.0 ** -0.5, in1=EE[:], op0=ALU.mult, op1=ALU.mult),
                          reads=[tQS, tEE], writes=tQ)
                    for c, (t0, cc) in enumerate(CH):
                        fw.op(fw.act, lambda c=c, t0=t0, cc=cc: nc.scalar.activation(out=dec[:, c:c + 1], in_=BB[:, t0 + cc - 1:t0 + cc], func=AF.Exp),
                              reads=[tBB], writes=[g.tdec[c]])
                    fw.op(fw.act, lambda: nc.scalar.activation(out=EE[:], in_=BB[:], func=AF.Exp, scale=-1.0), reads=[tBB] + tQ, writes=[tEE])
                    fw.op(fw.dve, lambda: nc.vector.scalar_tensor_tensor(out=EE[:], in0=EE[:], scalar=5.0e34, in1=FF[:], op0=ALU.min, op1=ALU.mult), reads=[tEE, tFF], writes=[tEE])
                    fw.op(fw.act, lambda: nc.scalar.copy(out=KIN[:, 0, :], in_=EE[:]), reads=[tEE], writes=tK)
                    for c, (t0, cc) in enumerate(CH):
                        fw.op(fw.dve, lambda c=c, t0=t0, cc=cc: nc.vector.tensor_scalar(out=KOF[:, 0, t0:t0 + cc], in0=EE[:, t0:t0 + cc], scalar1=dec[:, c:c + 1], scalar2=None, op0=ALU.mult),
                              reads=[tEE, g.tdec[c]], writes=[tKOF[c]])

                def on_v(c0, n, ps, tps):
                    fw.op(fw.act, lambda: nc.scalar.copy(out=VF[:, 0, :], in_=ps[:, 0:T]), reads=tps, writes=tVF)

                def on_g(c0, n, ps, tps):
                    fw.op(fw.act, lambda: nc.scalar.activation(out=SG[:, 0, :], in_=ps[:, 0:T], func=AF.Silu), reads=tps, writes=[tSG])

                gemm_fm(g, WN, [[(3072 + h * 128, 128)]], on_q)
                gemm_fm(g, WN, [[(4096 + h * 128, 128)]], on_f)
                gemm_fm(g, WN, [[(5120 + h * 128, 128)]], on_v)
                gemm_fm(g, WN, [[(6144 + h * 128, 128)]], on_g)
                g.tr_i = 0
                transposes_to_tok(g, KOF, tKOF, 1, KOT, tKO)
                transposes_to_tok(g, VF, tVF, 1, VT, tV)
                fw.release()
            with ExitStack() as ph3:
                dec_ap = lambda kt, c: dec[:, c:c + 1]
                recurrence(g, p, 1, 1, QIN, tQ, KIN, tK, KOT, tKO, VT, tV, OT, tOT, dec_ap,
                           g.st_hg[p, h], g.o_hg_p[h], g.o_hg_s[p, h], g.cr_hg[h], g.t_cr["hg"][h], ph3)
                fw.release()
            with ExitStack() as ph4:
                head_norm_gate(g, OT, tOT, 1, SG, tSG, h, False, lambda vt: g.hgnS[:, 0:1], ph4)
                fw.release()
    gemm_acc(g, "w_out_e", 1024, 8)
    fw.release()


def mixer_odd(g, p):
    fw, nc = g.fw, g.nc
    WN = "w_in_o"
    with ExitStack() as ph0:
        GK = g.SB("GK", [16, T], F32, ph0); tGK = Tile()
        wgk = g.SB("wgk", [16, 1024], F32, ph0); twgk = Tile()
        fw.dma(fw.sp, wgk[:], g.wgk2, writes=[twgk])

        def on_gk(c0, n, ps, tps):
            fw.op(fw.act, lambda: nc.scalar.copy(out=GK[:], in_=ps[0:16, 0:T]), reads=tps, writes=[tGK])
        gemm_fm(g, WN, [[(6144, 16)]], on_gk)
        for h in range(4):
            with ExitStack() as ph:
                OT = g.SB("OT", [128, 4, T], BF16, ph); tOT = [Tile() for _ in range(9)]
                with ExitStack() as phR:
                    SBr = lambda n, s, d=F32: g.SB(n, s, d, phR)
                    dec = SBr("dec", [128, 2, 9]); g.tdec = [Tile() for _ in range(9)]
                    QIN = SBr("QIN", [128, 2, T], BF16); tQ = [Tile() for _ in range(9)]
                    KIN = SBr("KIN", [128, 2, T], BF16); tK = [Tile() for _ in range(9)]
                    KOT = SBr("KOT", [128, 9, 256], BF16); tKO = [Tile() for _ in range(9)]
                    with ExitStack() as ph1:
                        SB1 = lambda n, s, d=F32: g.SB(n, s, d, ph1)
                        sm, tsm = smask_setup(g, ph1)
                        CS = SB1("CS", [128, 2, T]); tCS = Tile()
                        EE = SB1("EE", [128, T]); tEE = Tile()
                        KOF = SB1("KOF", [128, 2, T], BF16); tKOF = [Tile() for _ in range(9)]
                        for kt in range(2):
                            ps, tps = pa_set(g)
                            for ti, (t0, tn) in enumerate(NTT):
                                fw.op(fw.pe, lambda t0=t0, tn=tn, kt=kt, ps=ps: nc.tensor.matmul(ps[:, t0:t0 + tn], lhsT=wgk[0:16, h * 256 + kt * 128:h * 256 + (kt + 1) * 128], rhs=GK[0:16, t0:t0 + tn],
                                                                                             start=True, stop=True),
                                      reads=[tGK, twgk], writes=[tps[ti]], sig=(ti == 2))
                            j = h * 2 + kt
                            fw.op(fw.act, lambda kt=kt, j=j, ps=ps: nc.scalar.activation(out=CS[:, kt, :], in_=ps[:, 0:T], func=AF.Exp, scale=-1.0, bias=g.nbgk[:, j:j + 1]),
                                  reads=tps + [g.tmisc], writes=[tCS])
                            fw.op(fw.act, lambda kt=kt: nc.scalar.activation(out=CS[:, kt, :], in_=CS[:, kt, :], func=AF.Ln, bias=1.0), reads=[tCS], writes=[tCS])
                            fw.op(fw.dve, lambda kt=kt: nc.vector.tensor_tensor_scan(out=CS[:, kt, :], data0=sm[:], data1=CS[:, kt, :], initial=0.0, op0=ALU.mult, op1=ALU.add),
                                  reads=[tsm, tCS], writes=[tCS])
                            for c, (t0, cc) in enumerate(CH):
                                fw.op(fw.act, lambda kt=kt, c=c, t0=t0, cc=cc: nc.scalar.activation(out=dec[:, kt, c:c + 1], in_=CS[:, kt, t0 + cc - 1:t0 + cc], func=AF.Exp, scale=-1.0 / 16.0),
                                      reads=[tCS], writes=[g.tdec[c]])

                        def on_q(c0, n, ps, tps):
                            kt = (c0 - h * 256) // 128
                            fw.op(fw.act, lambda: nc.scalar.activation(out=EE[:], in_=CS[:, kt, :], func=AF.Exp, scale=-1.0 / 16.0), reads=[tCS], writes=[tEE])
                            fw.op(fw.dve, lambda: nc.vector.scalar_tensor_tensor(out=QIN[:, kt, :], in0=ps[:, 0:T], scalar=1.0 / 16.0, in1=EE[:], op0=ALU.mult, op1=ALU.mult),
                                  reads=tps + [tEE], writes=tQ)

                        def on_k(c0, n, ps, tps):
                            kt = (c0 - (1024 + h * 256)) // 128
                            fw.op(fw.act, lambda: nc.scalar.activation(out=EE[:], in_=CS[:, kt, :], func=AF.Exp, scale=1.0 / 16.0), reads=[tCS], writes=[tEE])
                            fw.op(fw.dve, lambda: nc.vector.tensor_tensor(out=EE[:], in0=EE[:], in1=ps[:, 0:T], op=ALU.mult), reads=tps + [tEE], writes=[tEE])
                            fw.op(fw.act, lambda: nc.scalar.copy(out=KIN[:, kt, :], in_=EE[:]), reads=[tEE], writes=tK)
                            for c, (t0, cc) in enumerate(CH):
                                fw.op(fw.dve, lambda c=c, t0=t0, cc=cc: nc.vector.tensor_scalar(out=KOF[:, kt, t0:t0 + cc], in0=EE[:, t0:t0 + cc], scalar1=dec[:, kt, c:c + 1], scalar2=None, op0=ALU.mult),
                                      reads=[tEE, g.tdec[c]], writes=[tKOF[c]])

                        gemm_fm(g, WN, [[(h * 256, 256)]], on_q)
                        gemm_fm(g, WN, [[(1024 + h * 256, 256)]], on_k)
                        g.tr_i = 0
                        transposes_to_tok(g, KOF, tKOF, 2, KOT, tKO)
                        fw.release()
                    VT = SBr("VT", [128, 9, 512], BF16); tV = [Tile() for _ in range(9)]
                    with ExitStack() as ph2:
                        VF = g.SB("VF", [128, 4, T], BF16, ph2); tVF = [Tile() for _ in range(9)]

                        def on_v(c0, n, ps, tps):
                            vt = (c0 - (2048 + h * 512)) // 128
                            fw.op(fw.act, lambda: nc.scalar.copy(out=VF[:, vt, :], in_=ps[:, 0:T]), reads=tps, writes=tVF)
                        gemm_fm(g, WN, [[(2048 + h * 512, 256)], [(2048 + h * 512 + 256, 256)]], on_v)
                        transposes_to_tok(g, VF, tVF, 4, VT, tV)
                        fw.release()
                    with ExitStack() as ph3:
                        dec_ap = lambda kt, c: dec[:, kt, c:c + 1]
                        recurrence(g, p, 2, 4, QIN, tQ, KIN, tK, KOT, tKO, VT, tV, OT, tOT, dec_ap,
                                   g.st_gla[p, h], g.o_gla_p[h], g.o_gla_s[p, h], g.cr_gla[h], g.t_cr["gla"][h], ph3)
                        fw.release()
                with ExitStack() as ph4:
                    SG = g.SB("SG", [128, 4, T], BF16, ph4); tSG = Tile()

                    def on_g(c0, n, ps, tps):
                        vt = (c0 - (4096 + h * 512)) // 128
                        fw.op(fw.act, lambda: nc.scalar.activation(out=SG[:, vt, :], in_=ps[:, 0:T], func=AF.Silu), reads=tps, writes=[tSG])
                    gemm_fm(g, WN, [[(4096 + h * 512, 256)], [(4096 + h * 512 + 256, 256)]], on_g)
                    head_norm_gate(g, OT, tOT, 4, SG, tSG, (h % 2) * 4, False, lambda vt: g.glanS[:, vt:vt + 1], ph4)
                    fw.release()
            if h % 2 == 1:
                gemm_acc(g, "w_out_o", (h - 1) * 512, 8)
                fw.release()
        fw.release()


def ffn(g, p, l):
    fw, nc = g.fw, g.nc
    with ExitStack() as ph:
        SBp = lambda n, s, d=F32: g.SB(n, s, d, ph)
        A = [SBp("A%d" % i, [128, 1092]) for i in range(2)]; tA = [Tile(), Tile()]
        C1 = [SBp("C1_%d" % i, [128, 1090]) for i in range(2)]; tC1 = [Tile(), Tile()]
        cvs = SBp("cvs", [128, NFT, 2]); tcvs = Tile()
        cvo = SBp("cvo", [128, NFT, 4]); tcvo = Tile()
        fw.dma(fw.sp, cvs[:], g.cv_in[p, l], writes=[tcvs])
        ft = 0
        while ft < NFT:
            nk = min(8, NFT - ft)
            for j in range(nk):
                f = ft + j
                i = f % 2
                cw = lambda k, f=f: g.convwS[:, l, f, k:k + 1]

                def on_tile(c0, n, ps, tps, f=f, i=i, j=j, cw=cw):
                    if c0 < DFF:
                        fw.op(fw.act, lambda: nc.scalar.copy(out=A[i][:, 2:1026], in_=ps[:, 0:1024]), reads=tps, writes=[tA[i]])
                        fw.op(fw.act, lambda: nc.scalar.copy(out=A[i][:, 1028:1092], in_=ps[:, 1024:1088]), reads=tps, writes=[tA[i]])
                        fw.op(fw.pool, lambda: nc.gpsimd.tensor_copy(out=A[i][:, 0:2], in_=g.cvcar[:, l, f, :]), reads=[g.tcvcar[l]], writes=[tA[i]])
                        fw.op(fw.pool, lambda: nc.gpsimd.tensor_copy(out=A[i][:, 1026:1028], in_=cvs[:, f, :]), reads=[tcvs], writes=[tA[i]])
                        fw.op(fw.pool, lambda: nc.gpsimd.tensor_copy(out=cvo[:, f, 0:2], in_=A[i][:, 1024:1026]), reads=[tA[i]], writes=[tcvo])
                        fw.op(fw.pool, lambda: nc.gpsimd.tensor_copy(out=cvo[:, f, 2:4], in_=A[i][:, 1090:1092]), reads=[tA[i]], writes=[tcvo])
                        fw.op(fw.pool, lambda: nc.gpsimd.tensor_copy(out=g.cvcar[:, l, f, :], in_=A[i][:, 1024:1026]), reads=[tA[i]], writes=[g.tcvcar[l]])
                        fw.op(fw.dve, lambda: nc.vector.tensor_scalar(out=C1[i][:], in0=A[i][:, 2:1092], scalar1=cw(2), scalar2=cw(3), op0=ALU.mult, op1=ALU.add),
                              reads=[tA[i], g.tconvw], writes=[tC1[i]])
                        fw.op(fw.dve, lambda: nc.vector.scalar_tensor_tensor(out=C1[i][:], in0=A[i][:, 1:1091], scalar=cw(1), in1=C1[i][:], op0=ALU.mult, op1=ALU.add),
                              reads=[tA[i], g.tconvw, tC1[i]], writes=[tC1[i]])
                        fw.op(fw.dve, lambda: nc.vector.scalar_tensor_tensor(out=C1[i][:], in0=A[i][:, 0:1090], scalar=cw(0), in1=C1[i][:], op0=ALU.mult, op1=ALU.add),
                              reads=[tA[i], g.tconvw, tC1[i]], writes=[tC1[i]])
                        fw.op(fw.act, lambda: nc.scalar.activation(out=C1[i][:], in_=C1[i][:], func=AF.Silu), reads=[tC1[i]], writes=[tC1[i]])
                    else:
                        fw.op(fw.dve, lambda: nc.vector.tensor_tensor(out=g.ZB[:, j, 0:1024], in0=C1[i][:, 0:1024], in1=ps[:, 0:1024], op=ALU.mult),
                              reads=tps + [tC1[i]], writes=[g.tZB[j]])
                        fw.op(fw.dve, lambda: nc.vector.tensor_tensor(out=g.ZB[:, j, 1024:1088], in0=C1[i][:, 1026:1090], in1=ps[:, 1024:1088], op=ALU.mult),
                              reads=tps + [tC1[i]], writes=[g.tZB[j]])
                gemm_fm(g, ("w_up", l), [[(f * 128, 128), (DFF + f * 128, 128)]], on_tile)
            gemm_acc(g, ("w_dn", l), ft * 128, nk)
            ft += nk
        fw.dma(fw.sp, g.o_cv[p, l], cvo[:], reads=[tcvo], is_output=True)
        fw.release()


def run_pass(g, p):
    fw, nc = g.fw, g.nc
    for kc in range(16):
        fw.dma(fw.sp if kc % 2 == 0 else fw.act, g.H[:, kc, :], g.xin[p, :, kc, :], writes=[g.tH[kc]])
    for l in range(2):
        rmsnorm(g, 2 * l)
        if l == 0:
            mixer_even(g, p)
        else:
            mixer_odd(g, p)
        rmsnorm(g, 2 * l + 1)
        ffn(g, p, l)
    rmsnorm(g, 4, final_out=True)
    for kc in range(16):
        fw.dma(fw.sp, g.yout[p, :, kc, :], g.H[:, kc, :], reads=[g.tH[kc]], is_output=True)
    fw.barrier()


_PROG = None


def _host_consts():
    f32 = np.float32
    half = 64
    inv = (f32(10000.0) ** (-(np.arange(half, dtype=f32) / f32(half)))).astype(f32)
    rope = np.zeros((NPASS, 2, 128, T), f32)
    for p in range(NPASS):
        pos = np.concatenate([p * 1024 + np.arange(1024), 1024 + np.arange(64)]).astype(f32)
        ang = (pos[:, None] * inv[None, :]).astype(f32)
        cs = np.cos(ang).astype(f32).T
        sn = np.sin(ang).astype(f32).T
        rope[p, 0] = np.concatenate([cs, cs], 0)
        rope[p, 1] = np.concatenate([sn, sn], 0)
    lg = np.log1p(-np.exp2(-5.0 - np.arange(4, dtype=np.float64)))
    tl = np.arange(128, dtype=np.float64)
    ret = np.zeros((128, 4, 2, 128), f32)
    ko = np.zeros((128, 4, 2), f32)
    dec = np.zeros((128, 4, 2), f32)
    for h in range(4):
        ret[:, h, 0, :] = np.exp(lg[h] * (tl + 1))[None, :]
        ret[:, h, 1, :] = (np.exp(-lg[h] * (tl + 1)) * 128.0 ** -0.5)[None, :]
        ko[:, h, 0] = np.exp(lg[h] * (127 - tl)) * 128.0 ** -0.5
        ko[:64, h, 1] = np.exp(lg[h] * (63 - tl[:64])) * 128.0 ** -0.5
        dec[:, h, 0] = np.exp(lg[h] * 128)
        dec[:, h, 1] = np.exp(lg[h] * 64)
    mat = np.zeros((4, 128, 128), f32)
    mat[0] = np.eye(128, dtype=f32)
    mat[1] = 1.0
    mat[2] = np.triu(np.ones((128, 128), f32))
    for m in range(64):
        mat[3][m + 64, m] = -1.0
        mat[3][m, m + 64] = 1.0
    sm = np.ones((128, T), f32)
    sm[:, 0:T:128] = 0.0
    return dict(c_rope=rope, c_ret=ret, c_ko=ko, c_dec=dec, c_mat=mat, c_smask=sm)


def kernel(x_prompt, x_sample, state_ret, state_hgrn, state_gla, cache_ffn_conv,
           norm_mix, norm_ffn, norm_final, w_in_even, w_out_even, hgrn_lb, hgrn_gnorm,
           w_in_odd, w_gk2, b_gk2, gla_gnorm, w_out_odd, ffn_w_up, ffn_conv_w, ffn_conv_b,
           ffn_w_down):
    global _PROG
    f32 = np.float32
    A = lambda a: np.ascontiguousarray(np.asarray(a, dtype=f32))
    x_prompt, x_sample = A(x_prompt), A(x_sample)
    state_ret, state_hgrn, state_gla, cache_ffn_conv = A(state_ret), A(state_hgrn), A(state_gla), A(cache_ffn_conv)
    if _PROG is None:
        _PROG = build_program()
    nc = _PROG
    consts = _host_consts()
    gam = np.stack([A(norm_mix)[0], A(norm_ffn)[0], A(norm_mix)[1], A(norm_ffn)[1], A(norm_final)], 0)
    gam = A(gam.reshape(5, 16, 128).transpose(2, 0, 1))
    cw = np.concatenate([A(ffn_conv_w), A(ffn_conv_b)[:, None, :]], 1)
    cw = A(cw.reshape(2, 4, NFT, 128).transpose(3, 0, 2, 1))
    shared = dict(
        gam=gam, convw=cw,
        lbraw=A(A(hgrn_lb).reshape(3, 8, 128).transpose(2, 0, 1)),
        hgn=A(A(hgrn_gnorm)[0][:, None]),
        wgk2=A(A(w_gk2)[0]),
        bgk=A(A(b_gk2)[0].reshape(8, 128).T),
        glan=A(A(gla_gnorm)[0].reshape(4, 128).T),
        w_in_e=A(w_in_even)[0], w_out_e=A(w_out_even)[0], w_in_o=A(w_in_odd)[0], w_out_o=A(w_out_odd)[0],
        w_up=A(ffn_w_up), w_dn=A(ffn_w_down),
    )
    shared.update(consts)
    USE = [0, 1, 4, 5]
    big = ("w_in_e", "w_out_e", "w_in_o", "w_out_o", "w_up", "w_dn")
    zshared = dict(shared)
    for k in big:
        zshared[k] = np.zeros_like(shared[k])
    zx = dict(xin=np.zeros((NPASS, 128, 16, T), f32), st_ret=np.zeros((NPASS, 4, 128, 256), f32),
              st_hg=np.zeros((NPASS, 8, 128, 128), f32), st_gla=np.zeros((NPASS, 4, 256, 512), f32),
              cv_in=np.zeros((NPASS, 2, 128, NFT, 2), f32))
    in_maps = []
    for core in range(8):
        if core not in USE:
            m = dict(zshared)
            m.update(zx)
            in_maps.append(m)
            continue
        b = USE.index(core)
        xin = np.zeros((NPASS, 128, 16, T), f32)
        st_r = np.zeros((NPASS, 4, 128, 256), f32)
        st_h = np.zeros((NPASS, 8, 128, 128), f32)
        st_g = np.zeros((NPASS, 4, 256, 512), f32)
        cv = np.zeros((NPASS, 2, 128, NFT, 2), f32)
        for p in range(NPASS):
            s = 2 * b + p
            X = np.concatenate([x_prompt[b, p * 1024:(p + 1) * 1024], x_sample[s]], 0)
            xin[p] = X.T.reshape(16, 128, T).transpose(1, 0, 2)
            st_r[p] = state_ret[0, s]
            st_h[p] = state_hgrn[0, s]
            st_g[p] = state_gla[0, s]
            cv[p] = cache_ffn_conv[:, s].reshape(2, 2, NFT, 128).transpose(0, 3, 2, 1)
        m = dict(shared)
        m.update(xin=xin, st_ret=st_r, st_hg=st_h, st_gla=st_g, cv_in=cv)
        in_maps.append(m)
    res = run_bass_kernel_spmd(nc, in_maps, core_ids=list(range(8)))
    R = res.results
    y_prompt = np.zeros((4, 2048, 2048), f32)
    y_sample = np.zeros((8, 64, 2048), f32)
    ret_p = np.zeros((1, 4, 4, 128, 256), f32); ret_s = np.zeros((1, 8, 4, 128, 256), f32)
    hg_p = np.zeros((1, 4, 8, 128, 128), f32); hg_s = np.zeros((1, 8, 8, 128, 128), f32)
    gla_p = np.zeros((1, 4, 4, 256, 512), f32); gla_s = np.zeros((1, 8, 4, 256, 512), f32)
    cv_p = np.zeros((2, 4, 2, DFF), f32); cv_s = np.zeros((2, 8, 2, DFF), f32)
    for b in range(4):
        r = R[USE[b]]
        yo = np.asarray(r["yout"])
        ocv = np.asarray(r["o_cv"])
        for p in range(NPASS):
            s = 2 * b + p
            Y = yo[p].transpose(2, 1, 0).reshape(T, 2048)
            y_prompt[b, p * 1024:(p + 1) * 1024] = Y[:1024]
            y_sample[s] = Y[1024:]
            ret_s[0, s] = np.asarray(r["o_ret_s"])[p]
            hg_s[0, s] = np.asarray(r["o_hg_s"])[p]
            gla_s[0, s] = np.asarray(r["o_gla_s"])[p]
            for l in range(2):
                cv_s[l, s] = ocv[p, l, :, :, 2:4].transpose(2, 1, 0).reshape(2, DFF)
        for l in range(2):
            cv_p[l, b] = ocv[1, l, :, :, 0:2].transpose(2, 1, 0).reshape(2, DFF)
        ret_p[0, b] = np.asarray(r["o_ret_p"])
        hg_p[0, b] = np.asarray(r["o_hg_p"])
        gla_p[0, b] = np.asarray(r["o_gla_p"])
    return (y_prompt, y_sample, ret_p, ret_s, hg_p, hg_s, gla_p, gla_s, cv_p, cv_s)
```

```python
import math
from contextlib import ExitStack

import numpy as np
import concourse.bass as bass
import concourse.mybir as mybir
from concourse.bass_utils import run_bass_kernel_spmd

F32 = mybir.dt.float32
BF16 = mybir.dt.bfloat16
ALU = mybir.AluOpType
AF = mybir.ActivationFunctionType

T = 1088
TP = 1024
NTT = [(0, 512), (512, 512), (1024, 64)]
CH = [(i * 128, 128) for i in range(8)] + [(1024, 64)]
NPASS = 2
DFF = 5504
NFT = 43
EPS = 1e-6


_GUARD = [None]


class Tile:
    __slots__ = ("name", "w", "r", "scoped")

    def __init__(self, name=""):
        self.name = name
        self.w = None
        self.scoped = _GUARD[0] is not None
        self.r = dict(_GUARD[0]) if _GUARD[0] else {}


class Eng:
    def __init__(self, name, h, sem, is_pe=False):
        self.name = name
        self.h = h
        self.sem = sem
        self.is_pe = is_pe
        self.n = 0
        self.nsig = 0
        self.sigs = []
        self.last = None
        self.waited = {}


class DmaSlot:
    def __init__(self, sem):
        self.sem = sem
        self.total = 0


class FW:
    def __init__(self, nc, stack, n_dma_sems=32):
        self.nc = nc
        es = stack.enter_context
        self.pe = Eng("pe", nc.tensor, es(nc.semaphore("s_pe")), True)
        self.act = Eng("act", nc.scalar, es(nc.semaphore("s_act")))
        self.dve = Eng("dve", nc.vector, es(nc.semaphore("s_dve")))
        self.pool = Eng("pool", nc.gpsimd, es(nc.semaphore("s_pool")))
        self.sp = Eng("sp", nc.sync, es(nc.semaphore("s_sp")))
        self.engs = [self.pe, self.act, self.dve, self.pool, self.sp]
        self.slots = [DmaSlot(es(nc.semaphore("s_dma%d" % i))) for i in range(n_dma_sems)]
        self.slots_sw = self.slots[:n_dma_sems // 2]
        self.slots_hw = self.slots[n_dma_sems // 2:]
        self.slot_i = {True: 0, False: 0}
        self.out_events = []
        self.scoped_dma = []

    def _resolve(self, ev):
        if ev[0] == "dma":
            return ev[1].sem, ev[2]
        e, idx = ev
        lo, hi = 0, len(e.sigs)
        while lo < hi:
            mid = (lo + hi) // 2
            if e.sigs[mid][0] >= idx:
                hi = mid
            else:
                lo = mid + 1
        if lo < len(e.sigs):
            return e.sem, e.sigs[lo][1]
        assert e.last is not None and e.n - 1 >= idx
        e.last.then_inc(e.sem, 1)
        e.nsig += 1
        e.sigs.append((e.n - 1, e.nsig))
        return e.sem, e.nsig

    def _wait(self, eng, evs):
        need = {}
        for ev in evs:
            if ev is None:
                continue
            sem, val = self._resolve(ev)
            k = id(sem)
            if eng.waited.get(k, 0) >= val:
                continue
            if k not in need or need[k][1] < val:
                need[k] = (sem, val)
        for k, (sem, val) in need.items():
            eng.h.wait_ge(sem, val)
            eng.waited[k] = val

    def _deps(self, eng, reads, writes, is_dma=False):
        evs = []
        for t in reads:
            if t.w is not None:
                if (not is_dma) and t.w[0] is eng and eng.is_pe:
                    continue
                evs.append(t.w)
        for t in writes:
            if t.w is not None:
                if not ((not is_dma) and t.w[0] is eng and eng.is_pe):
                    evs.append(t.w)
            for ev in t.r.values():
                if (not is_dma) and ev[0] is eng and eng.is_pe:
                    continue
                evs.append(ev)
        return evs

    def _record(self, ev, reads, writes):
        key = id(ev[0]) if ev[0] != "dma" else ("d", id(ev[1]))
        for t in reads:
            t.r[key] = ev
        for t in writes:
            t.w = ev
            t.r = {}

    def op(self, eng, fn, reads=(), writes=(), sig=None):
        self._wait(eng, self._deps(eng, reads, writes))
        inst = fn()
        eng.last = inst
        ev = (eng, eng.n)
        eng.n += 1
        if sig is None:
            sig = not eng.is_pe
        if sig:
            inst.then_inc(eng.sem, 1)
            eng.nsig += 1
            eng.sigs.append((eng.n - 1, eng.nsig))
        self._record(ev, reads, writes)
        return inst

    def dma(self, eng, out, in_, reads=(), writes=(), is_output=False):
        sw = eng is self.pool
        pool_ = self.slots_sw if sw else self.slots_hw
        slot = pool_[self.slot_i[sw]]
        self.slot_i[sw] = (self.slot_i[sw] + 1) % len(pool_)
        evs = self._deps(eng, reads, writes, is_dma=True)
        if slot.total:
            evs.append(("dma", slot, slot.total))
        self._wait(eng, evs)
        eng.h.dma_start(out=out, in_=in_).then_inc(slot.sem, 16)
        slot.total += 16
        ev = ("dma", slot, slot.total)
        self._record(ev, reads, writes)
        if any(t.scoped for t in reads) or any(t.scoped for t in writes):
            self.scoped_dma.append(ev)
        if is_output:
            self.out_events.append(ev)
        return ev

    def release(self):
        guard = dict(_GUARD[0] or {})
        for e in self.engs:
            if e.n:
                ev = (e, e.n - 1)
                if e.is_pe:
                    self._resolve(ev)
                guard[id(e)] = ev
        for ev in self.scoped_dma:
            guard[("d", id(ev[1]))] = ev
        self.scoped_dma = []
        _GUARD[0] = guard

    def barrier(self):
        evs = []
        for e in self.engs:
            if e.n:
                evs.append((e, e.n - 1))
        for s in self.slots:
            if s.total:
                evs.append(("dma", s, s.total))
        for e in self.engs:
            self._wait(e, evs)

    def finish(self):
        self.barrier()


class Prog:
    pass


def build_program(plan=None):
    _GUARD[0] = None
    nc = bass.Bass("TRN2", target_bir_lowering=False)

    def D(name, shape, kind="ExternalInput", dt=F32):
        return nc.dram_tensor(name, list(shape), dt, kind=kind).ap()

    g = Prog()
    g.nc = nc
    g.recording = plan is None
    g.wa_seq, g.wb_seq = ([], []) if plan is None else plan
    g.wa_pos = g.wb_pos = 0
    g.wa_issued = g.wb_issued = 0
    g.xin = D("xin", [NPASS, 128, 16, T])
    g.st_ret = D("st_ret", [NPASS, 4, 128, 256])
    g.st_hg = D("st_hg", [NPASS, 8, 128, 128])
    g.st_gla = D("st_gla", [NPASS, 4, 256, 512])
    g.cv_in = D("cv_in", [NPASS, 2, 128, NFT, 2])
    g.gam = D("gam", [128, 5, 16])
    g.convw = D("convw", [128, 2, NFT, 4])
    g.lbraw = D("lbraw", [128, 3, 8])
    g.hgn = D("hgn", [128, 1])
    g.wgk2 = D("wgk2", [16, 1024])
    g.bgk = D("bgk", [128, 8])
    g.glan = D("glan", [128, 4])
    g.w_in_e = D("w_in_e", [2048, 7168])
    g.w_out_e = D("w_out_e", [2048, 2048])
    g.w_in_o = D("w_in_o", [2048, 6160])
    g.w_out_o = D("w_out_o", [2048, 2048])
    g.w_up = D("w_up", [2, 2048, 2 * DFF])
    g.w_dn = D("w_dn", [2, DFF, 2048])
    g.c_rope = D("c_rope", [NPASS, 2, 128, T])
    g.c_ret = D("c_ret", [128, 4, 2, 128])
    g.c_ko = D("c_ko", [128, 4, 2])
    g.c_dec = D("c_dec", [128, 4, 2])
    g.c_mat = D("c_mat", [4, 128, 128])
    g.c_smask = D("c_smask", [128, T])
    EO = "ExternalOutput"
    g.yout = D("yout", [NPASS, 128, 16, T], EO)
    g.o_ret_p = D("o_ret_p", [4, 128, 256], EO)
    g.o_ret_s = D("o_ret_s", [NPASS, 4, 128, 256], EO)
    g.o_hg_p = D("o_hg_p", [8, 128, 128], EO)
    g.o_hg_s = D("o_hg_s", [NPASS, 8, 128, 128], EO)
    g.o_gla_p = D("o_gla_p", [4, 256, 512], EO)
    g.o_gla_s = D("o_gla_s", [NPASS, 4, 256, 512], EO)
    g.o_cv = D("o_cv", [NPASS, 2, 128, NFT, 4], EO)
    g.cr_ret = D("cr_ret", [4, 128, 256], "Internal")
    g.cr_hg = D("cr_hg", [8, 128, 128], "Internal")
    g.cr_gla = D("cr_gla", [4, 256, 512], "Internal")
    g.t_cr = {"ret": [Tile() for _ in range(4)], "hg": [Tile() for _ in range(8)],
              "gla": [Tile() for _ in range(4)]}

    with ExitStack() as st:
        es = st.enter_context
        fw = FW(nc, st)
        g.fw = fw

        g.uid = 0

        def SB(name, shape, dt=F32, stack=None):
            g.uid += 1
            return (stack or st).enter_context(nc.sbuf_tensor("%s_%d" % (name, g.uid), list(shape), dt))
        g.SB = SB

        g.H = SB("H", [128, 16, T]); g.tH = [Tile("H%d" % i) for i in range(16)]
        g.HN = SB("HN", [128, 16, T], BF16); g.tHN = [Tile("HN%d" % i) for i in range(16)]
        g.ZB = SB("ZB", [128, 8, T], BF16); g.tZB = [Tile("ZB%d" % i) for i in range(8)]
        g.WA = [SB("WA%d" % i, [128, 16, 256], BF16) for i in range(2)]; g.tWA = [Tile(), Tile()]
        g.WB = [SB("WB%d" % i, [128, 8, 256], BF16) for i in range(2)]; g.tWB = [Tile(), Tile()]
        g.wmap = {"w_in_e": g.w_in_e, "w_out_e": g.w_out_e, "w_in_o": g.w_in_o, "w_out_o": g.w_out_o,
                  ("w_up", 0): g.w_up[0], ("w_up", 1): g.w_up[1], ("w_dn", 0): g.w_dn[0], ("w_dn", 1): g.w_dn[1]}
        g.gamS = SB("gamS", [128, 5, 16]); g.tgam = Tile()
        g.convwS = SB("convwS", [128, 2, NFT, 4]); g.tconvw = Tile()
        g.cmatF = SB("cmatF", [128, 4, 128]); g.tcmat = Tile()
        g.identB = SB("identB", [128, 128], BF16)
        g.onesB = SB("onesB", [128, 128], BF16)
        g.cvcar = SB("cvcar", [128, 2, NFT, 2]); g.tcvcar = [Tile(), Tile()]
        g.lbS = SB("lbS", [128, 3, 8]); g.tlb = Tile()
        g.lb1 = SB("lb1", [128, 8]); g.lb2 = SB("lb2", [128, 8])
        g.hgnS = SB("hgnS", [128, 1])
        g.bgkS = SB("bgkS", [128, 8]); g.nbgk = SB("nbgk", [128, 8])
        g.glanS = SB("glanS", [128, 4])
        g.tmisc = Tile()
        g.kodec = SB("kodec", [128, 4, 4])
        g.PA = [es(nc.psum_tensor("PA%d" % i, [128, 1536], F32)) for i in range(2)]
        g.PB = [es(nc.psum_tensor("PB%d" % i, [128, 512], F32)) for i in range(2)]
        g.bank = [g.PA[0][:, 0:512], g.PA[0][:, 512:1024], g.PA[0][:, 1024:1536],
                  g.PA[1][:, 0:512], g.PA[1][:, 512:1024], g.PA[1][:, 1024:1536],
                  g.PB[0][:, :], g.PB[1][:, :]]
        g.tbank = [Tile("bank%d" % i) for i in range(8)]
        g.pa_i = 0

        load_consts(g)
        _GUARD[0] = {}
        for p in range(NPASS):
            run_pass(g, p)
        fw.finish()
    if plan is None:
        return build_program((g.wa_seq, g.wb_seq))
    return nc


def load_consts(g):
    fw, nc = g.fw, g.nc
    fw.dma(fw.sp, g.gamS[:], g.gam, writes=[g.tgam])
    fw.dma(fw.sp, g.convwS[:], g.convw, writes=[g.tconvw])
    fw.dma(fw.sp, g.cmatF[:], g.c_mat.rearrange("m p n -> p m n"), writes=[g.tcmat])
    fw.dma(fw.sp, g.lbS[:], g.lbraw, writes=[g.tlb])
    fw.dma(fw.sp, g.hgnS[:], g.hgn, writes=[g.tmisc])
    fw.dma(fw.sp, g.bgkS[:], g.bgk, writes=[g.tmisc])
    fw.dma(fw.sp, g.glanS[:], g.glan, writes=[g.tmisc])
    fw.dma(fw.sp, g.kodec[:, :, 0:2], g.c_ko, writes=[g.tmisc])
    fw.dma(fw.sp, g.kodec[:, :, 2:4], g.c_dec, writes=[g.tmisc])
    fw.op(fw.dve, lambda: nc.vector.tensor_copy(out=g.identB[:], in_=g.cmatF[:, 0, :]), reads=[g.tcmat], writes=[g.tmisc])
    fw.op(fw.dve, lambda: nc.vector.tensor_copy(out=g.onesB[:], in_=g.cmatF[:, 1, :]), reads=[g.tcmat], writes=[g.tmisc])
    fw.op(fw.act, lambda: nc.scalar.activation(out=g.lbS[:], in_=g.lbS[:], func=AF.Exp), reads=[g.tlb], writes=[g.tlb])
    fw.op(fw.dve, lambda: nc.vector.tensor_tensor(out=g.lb2[:], in0=g.lbS[:, 0, :], in1=g.lbS[:, 1, :], op=ALU.add), reads=[g.tlb], writes=[g.tmisc])
    fw.op(fw.dve, lambda: nc.vector.tensor_tensor(out=g.lb2[:], in0=g.lb2[:], in1=g.lbS[:, 2, :], op=ALU.add), reads=[g.tlb, g.tmisc], writes=[g.tmisc])
    fw.op(fw.dve, lambda: nc.vector.reciprocal(out=g.lb2[:], in_=g.lb2[:]), reads=[g.tmisc], writes=[g.tmisc])
    fw.op(fw.dve, lambda: nc.vector.tensor_tensor(out=g.lb1[:], in0=g.lbS[:, 0, :], in1=g.lb2[:], op=ALU.mult), reads=[g.tlb, g.tmisc], writes=[g.tmisc])
    fw.op(fw.dve, lambda: nc.vector.tensor_scalar(out=g.lb2[:], in0=g.lb1[:], scalar1=-1.0, scalar2=1.0, op0=ALU.mult, op1=ALU.add), reads=[g.tmisc], writes=[g.tmisc])
    fw.op(fw.dve, lambda: nc.vector.tensor_scalar(out=g.nbgk[:], in0=g.bgkS[:], scalar1=-1.0, scalar2=None, op0=ALU.mult), reads=[g.tmisc], writes=[g.tmisc])
    fw.op(fw.pool, lambda: nc.gpsimd.memset(g.cvcar[:], 0.0), writes=g.tcvcar)
    fw.barrier()


def pa_set(g):
    i = g.pa_i
    g.pa_i ^= 1
    return g.PA[i], [g.tbank[3 * i], g.tbank[3 * i + 1], g.tbank[3 * i + 2]]


def _wa_fetch(g, i):
    fw = g.fw
    wname, segs = g.wa_seq[i]
    Wv = g.wmap[wname].rearrange("(kc p) n -> p kc n", p=128)
    b = i % 2
    off = 0
    for (c0, ncol) in segs:
        fw.dma(fw.pool, g.WA[b][:, :, off:off + ncol], Wv[:, :, c0:c0 + ncol], writes=[g.tWA[b]])
        off += ncol
    g.wa_issued = i + 1


def gemm_fm(g, wname, blocks, on_tile):
    fw, nc = g.fw, g.nc
    for segs in blocks:
        i = g.wa_pos
        g.wa_pos += 1
        if g.recording:
            g.wa_seq.append((wname, segs))
        assert g.wa_seq[i] == (wname, segs)
        if g.wa_issued <= i:
            _wa_fetch(g, i)
        if i + 1 < len(g.wa_seq) and g.wa_issued <= i + 1 and not g.recording:
            _wa_fetch(g, i + 1)
        b = i % 2
        off = 0
        mts = []
        for (c0, ncol) in segs:
            m0 = 0
            while m0 < ncol:
                n = min(128, ncol - m0)
                mts.append((c0 + m0, off + m0, n))
                m0 += n
            off += ncol
        for (cg, m0, n) in mts:
            ps, tps = pa_set(g)
            for kc in range(16):
                for ti, (t0, tn) in enumerate(NTT):
                    fw.op(fw.pe, lambda kc=kc, t0=t0, tn=tn, m0=m0, n=n, b=b, ps=ps:
                          nc.tensor.matmul(ps[0:n, t0:t0 + tn], lhsT=g.WA[b][:, kc, m0:m0 + n],
                                           rhs=g.HN[:, kc, t0:t0 + tn], start=(kc == 0), stop=(kc == 15)),
                          reads=[g.tWA[b], g.tHN[kc]], writes=[tps[ti]], sig=(kc == 15 and ti == 2))
            on_tile(cg, n, ps, tps)


def _wb_fetch(g, i):
    fw = g.fw
    wname, row0, nk, c0 = g.wb_seq[i]
    Wv = g.wmap[wname][row0:row0 + nk * 128, :].rearrange("(kc p) n -> p kc n", p=128)
    b = i % 2
    fw.dma(fw.pool, g.WB[b][:, 0:nk, :], Wv[:, :, c0:c0 + 256], writes=[g.tWB[b]])
    g.wb_issued = i + 1


def gemm_acc(g, wname, row0, nk):
    fw, nc = g.fw, g.nc
    for blk in range(8):
        c0 = blk * 256
        i = g.wb_pos
        g.wb_pos += 1
        if g.recording:
            g.wb_seq.append((wname, row0, nk, c0))
        assert g.wb_seq[i] == (wname, row0, nk, c0)
        if g.wb_issued <= i:
            _wb_fetch(g, i)
        if i + 1 < len(g.wb_seq) and g.wb_issued <= i + 1 and not g.recording:
            _wb_fetch(g, i + 1)
        b = i % 2
        for mi in range(2):
            m = blk * 2 + mi
            ps, tps = pa_set(g)
            for kc in range(nk):
                for ti, (t0, tn) in enumerate(NTT):
                    fw.op(fw.pe, lambda kc=kc, t0=t0, tn=tn, mi=mi, b=b, ps=ps:
                          nc.tensor.matmul(ps[:, t0:t0 + tn], lhsT=g.WB[b][:, kc, mi * 128:(mi + 1) * 128],
                                           rhs=g.ZB[:, kc, t0:t0 + tn], start=(kc == 0), stop=(kc == nk - 1)),
                          reads=[g.tWB[b], g.tZB[kc]], writes=[tps[ti]], sig=(kc == nk - 1 and ti == 2))
            fw.op(fw.dve, lambda m=m, ps=ps: nc.vector.tensor_tensor(out=g.H[:, m, :], in0=g.H[:, m, :], in1=ps[:, 0:T], op=ALU.add),
                  reads=tps + [g.tH[m]], writes=[g.tH[m]])


def rmsnorm(g, gi, final_out=None):
    fw, nc = g.fw, g.nc
    with ExitStack() as ph:
        sq = [g.SB("rn_sq%d" % i, [128, T], BF16, ph) for i in range(3)]
        tsq = [Tile(), Tile(), Tile()]
        rstd = g.SB("rn_rstd", [128, T], F32, ph)
        trs = Tile()
        tmp = g.SB("rn_tmp", [128, T], F32, ph)
        ttmp = Tile()
        ps, tps = pa_set(g)
        for kc in range(16):
            i = kc % 3
            if i == 0:
                fw.op(fw.act, lambda kc=kc, i=i: nc.scalar.activation(out=sq[i][:], in_=g.H[:, kc, :], func=AF.Square),
                      reads=[g.tH[kc]], writes=[tsq[i]])
            else:
                eng = fw.dve if i == 1 else fw.pool
                fw.op(eng, lambda kc=kc, i=i, eng=eng: eng.h.tensor_tensor(out=sq[i][:], in0=g.H[:, kc, :], in1=g.H[:, kc, :], op=ALU.mult),
                      reads=[g.tH[kc]], writes=[tsq[i]])
            for ti, (t0, tn) in enumerate(NTT):
                fw.op(fw.pe, lambda kc=kc, i=i, t0=t0, tn=tn: nc.tensor.matmul(ps[:, t0:t0 + tn], lhsT=g.onesB[:], rhs=sq[i][:, t0:t0 + tn],
                                                                          start=(kc == 0), stop=(kc == 15)),
                      reads=[tsq[i], g.tmisc], writes=[tps[ti]], sig=(ti == 2))
        fw.op(fw.dve, lambda: nc.vector.tensor_scalar(out=rstd[:], in0=ps[:, 0:T], scalar1=1.0 / 2048.0, scalar2=EPS, op0=ALU.mult, op1=ALU.add),
              reads=tps, writes=[trs])
        fw.op(fw.act, lambda: nc.scalar.activation(out=rstd[:], in_=rstd[:], func=AF.Ln), reads=[trs], writes=[trs])
        fw.op(fw.act, lambda: nc.scalar.activation(out=rstd[:], in_=rstd[:], func=AF.Exp, scale=-0.5), reads=[trs], writes=[trs])
        for kc in range(16):
            dst, tdst = (g.HN, g.tHN) if final_out is None else (g.H, g.tH)
            if kc % 2 == 0:
                fw.op(fw.dve, lambda kc=kc, dst=dst: nc.vector.scalar_tensor_tensor(out=dst[:, kc, :], in0=g.H[:, kc, :], scalar=g.gamS[:, gi, kc:kc + 1],
                                                                                    in1=rstd[:], op0=ALU.mult, op1=ALU.mult),
                      reads=[g.tH[kc], trs, g.tgam], writes=[tdst[kc]])
            else:
                fw.op(fw.pool, lambda kc=kc: nc.gpsimd.tensor_tensor(out=tmp[:], in0=g.H[:, kc, :], in1=rstd[:], op=ALU.mult),
                      reads=[g.tH[kc], trs], writes=[ttmp])
                fw.op(fw.act, lambda kc=kc, dst=dst: nc.scalar.activation(out=dst[:, kc, :], in_=tmp[:], func=AF.Copy, scale=g.gamS[:, gi, kc:kc + 1]),
                      reads=[ttmp, g.tgam], writes=[tdst[kc]])
        fw.release()


def transposes_to_tok(g, src, tsrc, nf, dst, tdst, scale_ap=None):
    fw, nc = g.fw, g.nc
    for c, (t0, cc) in enumerate(CH):
        for f0 in range(0, nf, 4):
            fn = min(4, nf - f0)
            bi = 4 + (g.tr_i % 4)
            use_act = (g.tr_i % 2 == 0)
            g.tr_i += 1
            pb = g.bank[bi].bitcast(BF16)
            for f in range(fn):
                fw.op(fw.pe, lambda f=f, f0=f0, t0=t0, cc=cc, pb=pb: nc.tensor.transpose(pb[0:cc, f * 128:(f + 1) * 128], src[:, f0 + f, t0:t0 + cc], g.identB[:]),
                      reads=[tsrc[c], g.tmisc], writes=[g.tbank[bi]], sig=(f == fn - 1))
            o_ap = dst[0:cc, c, f0 * 128:(f0 + fn) * 128]
            i_ap = pb[0:cc, 0:fn * 128]
            if scale_ap is None:
                if use_act:
                    fw.op(fw.act, lambda o_ap=o_ap, i_ap=i_ap: nc.scalar.copy(out=o_ap, in_=i_ap), reads=[g.tbank[bi]], writes=[tdst[c]])
                else:
                    fw.op(fw.dve, lambda o_ap=o_ap, i_ap=i_ap: nc.vector.tensor_copy(out=o_ap, in_=i_ap), reads=[g.tbank[bi]], writes=[tdst[c]])
            else:
                sc = scale_ap(cc)
                if use_act:
                    fw.op(fw.act, lambda o_ap=o_ap, i_ap=i_ap, sc=sc: nc.scalar.activation(out=o_ap, in_=i_ap, func=AF.Copy, scale=sc),
                          reads=[g.tbank[bi], g.tmisc], writes=[tdst[c]])
                else:
                    fw.op(fw.dve, lambda o_ap=o_ap, i_ap=i_ap, sc=sc: nc.vector.tensor_scalar(out=o_ap, in0=i_ap, scalar1=sc, scalar2=None, op0=ALU.mult),
                          reads=[g.tbank[bi], g.tmisc], writes=[tdst[c]])


def recurrence(g, p, nk, nv, QIN, tQ, KIN, tK, KOT, tKO, VT, tV, OT, tOT, dec_ap, st_in, st_out_p, st_out_s, carry, tcarry, ph):
    fw, nc = g.fw, g.nc
    V = nv * 128
    LA = 4 if nk == 1 else 2
    NR = LA + 1
    S = g.SB("S", [128, nk, V], F32, ph); tS = Tile()
    Sb = [g.SB("Sb%d" % i, [128, nk, V], BF16, ph) for i in range(NR)]; tSb = [Tile() for _ in range(NR)]
    scm = g.SB("scm", [128, 9, 128], BF16, ph); tscm = [Tile() for _ in range(9)]
    S2 = g.SB("S2", [128, nk, V], F32, ph); tS2 = Tile()
    SbS = g.SB("SbS", [128, nk, V], BF16, ph); tSbS = Tile()

    def view(d):
        return d.rearrange("(kt p) v -> p kt v", p=128)

    fw.dma(fw.sp, S2[:], view(st_in), writes=[tS2])
    fw.op(fw.act, lambda: nc.scalar.copy(out=SbS[:], in_=S2[:]), reads=[tS2], writes=[tSbS])

    if p == 0:
        fw.op(fw.pool, lambda: nc.gpsimd.memset(S[:], 0.0), writes=[tS])
        fw.op(fw.pool, lambda: nc.gpsimd.memset(Sb[0][:], 0.0), writes=[tSb[0]])
    else:
        fw.dma(fw.sp, S[:], view(carry), reads=[tcarry], writes=[tS])
        fw.op(fw.act, lambda: nc.scalar.copy(out=Sb[0][:], in_=S[:]), reads=[tS], writes=[tSb[0]])
    for c, (t0, cc) in enumerate(CH):
        bsc = c % 2
        psc = g.bank[bsc]
        for kt in range(nk):
            fw.op(fw.pe, lambda kt=kt, t0=t0, cc=cc, psc=psc: nc.tensor.matmul(psc[0:cc, 0:cc], lhsT=KIN[:, kt, t0:t0 + cc], rhs=QIN[:, kt, t0:t0 + cc],
                                                                         start=(kt == 0), stop=(kt == nk - 1)),
                  reads=[tK[c], tQ[c]], writes=[g.tbank[bsc]], sig=(kt == nk - 1))
        fw.op(fw.dve, lambda c=c, cc=cc, psc=psc: nc.vector.tensor_tensor(out=scm[0:cc, c, 0:cc], in0=psc[0:cc, 0:cc], in1=g.cmatF[0:cc, 2, 0:cc], op=ALU.mult),
              reads=[g.tbank[bsc], g.tcmat], writes=[tscm[c]])

    def scan_step(c):
        t0, cc = CH[c]
        if c == 8:
            if p == 0:
                fw.dma(fw.sp, view(carry), S[:], reads=[tS], writes=[tcarry])
            else:
                fw.dma(fw.sp, view(st_out_p), S[:], reads=[tS], is_output=True)
        for kt in range(nk):
            bs = 4 + ((nk * c + kt) % 4)
            pS = g.bank[bs]
            fw.op(fw.pe, lambda kt=kt, c=c, cc=cc, pS=pS: nc.tensor.matmul(pS[:, 0:V], lhsT=KOT[0:cc, c, kt * 128:(kt + 1) * 128], rhs=VT[0:cc, c, 0:V],
                                                                     start=True, stop=True),
                  reads=[tKO[c], tV[c]], writes=[g.tbank[bs]], sig=True)
            SS, tSS = (S, tS) if c < 8 else (S2, tS2)
            fw.op(fw.dve, lambda kt=kt, c=c, pS=pS, SS=SS: nc.vector.scalar_tensor_tensor(out=SS[:, kt, :], in0=SS[:, kt, :], scalar=dec_ap(kt, c), in1=pS[:, 0:V],
                                                                                      op0=ALU.mult, op1=ALU.add),
                  reads=[tSS, g.tbank[bs], g.tmisc, g.tdec[c]], writes=[tSS])
        if c < 7:
            r = (c + 1) % NR
            fw.op(fw.act, lambda r=r: nc.scalar.copy(out=Sb[r][:], in_=S[:]), reads=[tS], writes=[tSb[r]])
        if c == 8:
            fw.dma(fw.sp, view(st_out_s), S2[:], reads=[tS2], is_output=True)

    def out_step(c):
        t0, cc = CH[c]
        bo = c % 4
        po = g.bank[bo]
        r = c % NR
        for vt in range(nv):
            fw.op(fw.pe, lambda vt=vt, c=c, cc=cc, po=po: nc.tensor.matmul(po[:, vt * cc:(vt + 1) * cc], lhsT=VT[0:cc, c, vt * 128:(vt + 1) * 128],
                                                                     rhs=scm[0:cc, c, 0:cc], start=True, stop=False),
                  reads=[tV[c], tscm[c]], writes=[g.tbank[bo]])
            for kt in range(nk):
                SBc, tSBc = (Sb[r], tSb[r]) if c < 8 else (SbS, tSbS)
                fw.op(fw.pe, lambda vt=vt, kt=kt, t0=t0, cc=cc, po=po, SBc=SBc: nc.tensor.matmul(po[:, vt * cc:(vt + 1) * cc], lhsT=SBc[:, kt, vt * 128:(vt + 1) * 128],
                                                                                          rhs=QIN[:, kt, t0:t0 + cc], start=False, stop=(kt == nk - 1)),
                      reads=[tSBc, tQ[c]], writes=[g.tbank[bo]], sig=(kt == nk - 1 and vt == nv - 1))
        fw.op(fw.act, lambda t0=t0, cc=cc, po=po: nc.scalar.copy(out=OT[:, :, t0:t0 + cc], in_=po[:, 0:nv * cc].rearrange("p (v c) -> p v c", v=nv)),
              reads=[g.tbank[bo]], writes=[tOT[c]])

    for c in range(LA):
        scan_step(c)
    for c in range(9):
        out_step(c)
        if c + LA < 9:
            scan_step(c + LA)


def head_norm_gate(g, OT, tOT, nv, SG, tSG, slot0, center, gn_ap, ph):
    fw, nc = g.fw, g.nc
    V = nv * 128
    sq = [g.SB("hn_sq%d" % i, [128, T], BF16, ph) for i in range(2)]; tsq = [Tile(), Tile()]
    rstd = g.SB("hn_rstd", [128, T], F32, ph); trs = Tile()
    mean = g.SB("hn_mean", [128, T], F32, ph) if center else None; tmn = Tile()
    tmp = g.SB("hn_tmp", [128, T], F32, ph); ttmp = Tile()
    ps2, tps2 = pa_set(g)
    for vt in range(nv):
        i = vt % 2
        fw.op(fw.act, lambda vt=vt, i=i: nc.scalar.activation(out=sq[i][:], in_=OT[:, vt, :], func=AF.Square), reads=list(tOT), writes=[tsq[i]])
        for ti, (t0, tn) in enumerate(NTT):
            fw.op(fw.pe, lambda vt=vt, i=i, t0=t0, tn=tn: nc.tensor.matmul(ps2[:, t0:t0 + tn], lhsT=g.onesB[:], rhs=sq[i][:, t0:t0 + tn],
                                                                      start=(vt == 0), stop=(vt == nv - 1)),
                  reads=[tsq[i], g.tmisc], writes=[tps2[ti]], sig=(ti == 2))
    if center:
        ps1, tps1 = pa_set(g)
        for vt in range(nv):
            for ti, (t0, tn) in enumerate(NTT):
                fw.op(fw.pe, lambda vt=vt, t0=t0, tn=tn: nc.tensor.matmul(ps1[:, t0:t0 + tn], lhsT=g.onesB[:], rhs=OT[:, vt, t0:t0 + tn],
                                                                     start=(vt == 0), stop=(vt == nv - 1)),
                      reads=list(tOT) + [g.tmisc], writes=[tps1[ti]], sig=(ti == 2 and vt == nv - 1))
        fw.op(fw.dve, lambda: nc.vector.tensor_scalar(out=mean[:], in0=ps1[:, 0:T], scalar1=1.0 / V, scalar2=None, op0=ALU.mult), reads=tps1, writes=[tmn])
        fw.op(fw.dve, lambda: nc.vector.tensor_tensor(out=tmp[:], in0=mean[:], in1=mean[:], op=ALU.mult), reads=[tmn], writes=[ttmp])
        fw.op(fw.dve, lambda: nc.vector.scalar_tensor_tensor(out=rstd[:], in0=ps2[:, 0:T], scalar=1.0 / V, in1=tmp[:], op0=ALU.mult, op1=ALU.subtract),
              reads=tps2 + [ttmp], writes=[trs])
        fw.op(fw.dve, lambda: nc.vector.tensor_scalar(out=rstd[:], in0=rstd[:], scalar1=EPS, scalar2=None, op0=ALU.add), reads=[trs], writes=[trs])
    else:
        fw.op(fw.dve, lambda: nc.vector.tensor_scalar(out=rstd[:], in0=ps2[:, 0:T], scalar1=1.0 / V, scalar2=EPS, op0=ALU.mult, op1=ALU.add), reads=tps2, writes=[trs])
    fw.op(fw.act, lambda: nc.scalar.activation(out=rstd[:], in_=rstd[:], func=AF.Ln), reads=[trs], writes=[trs])
    fw.op(fw.act, lambda: nc.scalar.activation(out=rstd[:], in_=rstd[:], func=AF.Exp, scale=-0.5), reads=[trs], writes=[trs])
    for vt in range(nv):
        if center:
            fw.op(fw.dve, lambda vt=vt: nc.vector.tensor_tensor(out=tmp[:], in0=OT[:, vt, :], in1=mean[:], op=ALU.subtract), reads=list(tOT) + [tmn], writes=[ttmp])
            fw.op(fw.dve, lambda: nc.vector.tensor_tensor(out=tmp[:], in0=tmp[:], in1=rstd[:], op=ALU.mult), reads=[ttmp, trs], writes=[ttmp])
        else:
            fw.op(fw.dve, lambda vt=vt: nc.vector.scalar_tensor_tensor(out=tmp[:], in0=OT[:, vt, :], scalar=gn_ap(vt), in1=rstd[:], op0=ALU.mult, op1=ALU.mult),
                  reads=list(tOT) + [trs, g.tmisc], writes=[ttmp])
        fw.op(fw.dve, lambda vt=vt: nc.vector.tensor_tensor(out=g.ZB[:, slot0 + vt, :], in0=tmp[:], in1=SG[:, vt, :], op=ALU.mult),
              reads=[ttmp, tSG], writes=[g.tZB[slot0 + vt]])


def smask_setup(g, ph):
    fw, nc = g.fw, g.nc
    sm = g.SB("smask", [128, T], F32, ph)
    tsm = Tile()
    fw.dma(fw.sp, sm[:], g.c_smask, writes=[tsm])
    return sm, tsm


def mixer_even(g, p):
    fw, nc = g.fw, g.nc
    WN = "w_in_e"
    g.tdec = [Tile() for _ in range(9)]
    for h in range(4):
        with ExitStack() as ph:
            SBp = lambda n, s, d=F32: g.SB(n, s, d, ph)
            QIN = SBp("QIN", [128, 1, T], BF16); tQ = [Tile() for _ in range(9)]
            KIN = SBp("KIN", [128, 1, T], BF16); tK = [Tile() for _ in range(9)]
            KOT = SBp("KOT", [128, 9, 128], BF16); tKO = [Tile() for _ in range(9)]
            VT = SBp("VT", [128, 9, 256], BF16); tV = [Tile() for _ in range(9)]
            OT = SBp("OT", [128, 2, T], BF16); tOT = [Tile() for _ in range(9)]
            SG = SBp("SG", [128, 2, T], BF16); tSG = Tile()
            with ExitStack() as ph1:
                SB1 = lambda n, s, d=F32: g.SB(n, s, d, ph1)
                cs = SB1("cs", [128, 2, T]); tcs = Tile()
                eqk = SB1("eqk", [128, 2, 128]); teqk = Tile()
                fw.dma(fw.sp, cs[:], g.c_rope[p].rearrange("m p t -> p m t"), writes=[tcs])
                fw.dma(fw.sp, eqk[:], g.c_ret[:, h, :, :], writes=[teqk])
                XF = SB1("XF", [128, T]); tXF = Tile()
                XR = SB1("XR", [128, T]); tXR = Tile()
                KRB = SB1("KRB", [128, 1, T], BF16); tKRB = [Tile() for _ in range(9)]
                VF = SB1("VF", [128, 2, T], BF16); tVF = [Tile() for _ in range(9)]

                def rot(which):
                    def on_tile(c0, n, ps, tps):
                        fw.op(fw.act, lambda: nc.scalar.copy(out=XF[:], in_=ps[:, 0:T]), reads=tps, writes=[tXF])
                        ps2, tps2 = pa_set(g)
                        for ti, (t0, tn) in enumerate(NTT):
                            fw.op(fw.pe, lambda t0=t0, tn=tn: nc.tensor.matmul(ps2[:, t0:t0 + tn], lhsT=g.cmatF[:, 3, :], rhs=XF[:, t0:t0 + tn], start=True, stop=True),
                                  reads=[tXF, g.tcmat], writes=[tps2[ti]], sig=(ti == 2))
                        fw.op(fw.dve, lambda: nc.vector.tensor_tensor(out=XR[:], in0=ps2[:, 0:T], in1=cs[:, 1, :], op=ALU.mult), reads=tps2 + [tcs], writes=[tXR])
                        fw.op(fw.dve, lambda: nc.vector.tensor_tensor(out=XF[:], in0=XF[:], in1=cs[:, 0, :], op=ALU.mult), reads=[tXF, tcs], writes=[tXF])
                        fw.op(fw.dve, lambda: nc.vector.tensor_tensor(out=XF[:], in0=XF[:], in1=XR[:], op=ALU.add), reads=[tXF, tXR], writes=[tXF])
                        dst, tdst = (QIN, tQ) if which == 0 else (KIN, tK)
                        for c, (t0, cc) in enumerate(CH):
                            fw.op(fw.dve, lambda t0=t0, cc=cc: nc.vector.tensor_tensor(out=dst[:, 0, t0:t0 + cc], in0=XF[:, t0:t0 + cc], in1=eqk[:, which, 0:cc], op=ALU.mult),
                                  reads=[tXF, teqk], writes=[tdst[c]])
                        if which == 1:
                            fw.op(fw.act, lambda: nc.scalar.copy(out=KRB[:, 0, :], in_=XF[:]), reads=[tXF], writes=tKRB)
                    return on_tile

                gemm_fm(g, WN, [[(h * 128, 128)]], rot(0))
                gemm_fm(g, WN, [[(512 + h * 128, 128)]], rot(1))

                def on_v(c0, n, ps, tps):
                    vt = (c0 - (1024 + h * 256)) // 128
                    fw.op(fw.act, lambda: nc.scalar.copy(out=VF[:, vt, :], in_=ps[:, 0:T]), reads=tps, writes=tVF)
                gemm_fm(g, WN, [[(1024 + h * 256, 256)]], on_v)
                g.tr_i = 0
                transposes_to_tok(g, KRB, tKRB, 1, KOT, tKO, scale_ap=lambda cc: g.kodec[0:cc, h, (0 if cc == 128 else 1):(1 if cc == 128 else 2)])
                transposes_to_tok(g, VF, tVF, 2, VT, tV)
                fw.release()
            with ExitStack() as ph3:
                dec_ap = lambda kt, c: g.kodec[:, h, (2 if CH[c][1] == 128 else 3):(3 if CH[c][1] == 128 else 4)]
                recurrence(g, p, 1, 2, QIN, tQ, KIN, tK, KOT, tKO, VT, tV, OT, tOT, dec_ap,
                           g.st_ret[p, h], g.o_ret_p[h], g.o_ret_s[p, h], g.cr_ret[h], g.t_cr["ret"][h], ph3)
                fw.release()

            def on_g(c0, n, ps, tps):
                vt = (c0 - (2048 + h * 256)) // 128
                fw.op(fw.act, lambda: nc.scalar.activation(out=SG[:, vt, :], in_=ps[:, 0:T], func=AF.Silu), reads=tps, writes=[tSG])
            gemm_fm(g, WN, [[(2048 + h * 256, 256)]], on_g)
            with ExitStack() as ph4:
                head_norm_gate(g, OT, tOT, 2, SG, tSG, 2 * h, True, None, ph4)
                fw.release()
    gemm_acc(g, "w_out_e", 0, 8)
    fw.release()
    for h in range(8):
        with ExitStack() as ph:
            SBp = lambda n, s, d=F32: g.SB(n, s, d, ph)
            dec = SBp("dec", [128, 9]); g.tdec = [Tile() for _ in range(9)]
            QIN = SBp("QIN", [128, 1, T], BF16); tQ = [Tile() for _ in range(9)]
            KIN = SBp("KIN", [128, 1, T], BF16); tK = [Tile() for _ in range(9)]
            KOT = SBp("KOT", [128, 9, 128], BF16); tKO = [Tile() for _ in range(9)]
            VT = SBp("VT", [128, 9, 128], BF16); tV = [Tile() for _ in range(9)]
            OT = SBp("OT", [128, 1, T], BF16); tOT = [Tile() for _ in range(9)]
            SG = SBp("SG", [128, 1, T], BF16); tSG = Tile()
            with ExitStack() as ph1:
                SB1 = lambda n, s, d=F32: g.SB(n, s, d, ph1)
                sm, tsm = smask_setup(g, ph1)
                QS = SB1("QS", [128, T]); tQS = Tile()
                FF = SB1("FF", [128, T]); tFF = Tile()
                BB = SB1("BB", [128, T]); tBB = Tile()
                EE = SB1("EE", [128, T]); tEE = Tile()
                KOF = SB1("KOF", [128, 1, T], BF16); tKOF = [Tile() for _ in range(9)]
                VF = SB1("VF", [128, 1, T], BF16); tVF = [Tile() for _ in range(9)]

                def on_q(c0, n, ps, tps):
                    fw.op(fw.act, lambda: nc.scalar.activation(out=QS[:], in_=ps[:, 0:T], func=AF.Silu), reads=tps, writes=[tQS])

                def on_f(c0, n, ps, tps):
                    fw.op(fw.act, lambda: nc.scalar.activation(out=FF[:], in_=ps[:, 0:T], func=AF.Sigmoid), reads=tps, writes=[tFF])
                    fw.op(fw.dve, lambda: nc.vector.tensor_scalar(out=FF[:], in0=FF[:], scalar1=g.lb2[:, h:h + 1], scalar2=g.lb1[:, h:h + 1], op0=ALU.mult, op1=ALU.add),
                          reads=[tFF, g.tmisc], writes=[tFF])
                    fw.op(fw.act, lambda: nc.scalar.activation(out=BB[:], in_=FF[:], func=AF.Ln), reads=[tFF], writes=[tBB])
                    fw.op(fw.dve, lambda: nc.vector.tensor_scalar(out=FF[:], in0=FF[:], scalar1=-1.0, scalar2=1.0, op0=ALU.mult, op1=ALU.add), reads=[tFF, tBB], writes=[tFF])
                    fw.op(fw.dve, lambda: nc.vector.tensor_tensor_scan(out=BB[:], data0=sm[:], data1=BB[:], initial=0.0, op0=ALU.mult, op1=ALU.add),
                          reads=[tsm, tBB], writes=[tBB])
                    fw.op(fw.act, lambda: nc.scalar.activation(out=EE[:], in_=BB[:], func=AF.Exp), reads=[tBB], writes=[tEE])
                    fw.op(fw.dve, lambda: nc.vector.scalar_tensor_tensor(out=QIN[:, 0, :], in0=QS[:], scalar=128.0 ** -0.5, in1=EE[:], op0=ALU.mult, op1=ALU.mult),
                          reads=[tQS, tEE], writes=tQ)
                    for c, (t0, cc) in enumerate(CH):
                        fw.op(fw.act, lambda c=c, t0=t0, cc=cc: nc.scalar.activation(out=dec[:, c:c + 1], in_=BB[:, t0 + cc - 1:t0 + cc], func=AF.Exp),
                              reads=[tBB], writes=[g.tdec[c]])
                    fw.op(fw.act, lambda: nc.scalar.activation(out=EE[:], in_=BB[:], func=AF.Exp, scale=-1.0), reads=[tBB] + tQ, writes=[tEE])
                    fw.op(fw.dve, lambda: nc.vector.scalar_tensor_tensor(out=EE[:], in0=EE[:], scalar=5.0e34, in1=FF[:], op0=ALU.min, op1=ALU.mult), reads=[tEE, tFF], writes=[tEE])
                    fw.op(fw.act, lambda: nc.scalar.copy(out=KIN[:, 0, :], in_=EE[:]), reads=[tEE], writes=tK)
                    for c, (t0, cc) in enumerate(CH):
                        fw.op(fw.dve, lambda c=c, t0=t0, cc=cc: nc.vector.tensor_scalar(out=KOF[:, 0, t0:t0 + cc], in0=EE[:, t0:t0 + cc], scalar1=dec[:, c:c + 1], scalar2=None, op0=ALU.mult),
                              reads=[tEE, g.tdec[c]], writes=[tKOF[c]])

                def on_v(c0, n, ps, tps):
                    fw.op(fw.act, lambda: nc.scalar.copy(out=VF[:, 0, :], in_=ps[:, 0:T]), reads=tps, writes=tVF)

                def on_g(c0, n, ps, tps):
                    fw.op(fw.act, lambda: nc.scalar.activation(out=SG[:, 0, :], in_=ps[:, 0:T], func=AF.Silu), reads=tps, writes=[tSG])

                gemm_fm(g, WN, [[(3072 + h * 128, 128)]], on_q)
                gemm_fm(g, WN, [[(4096 + h * 128, 128)]], on_f)
                gemm_fm(g, WN, [[(5120 + h * 128, 128)]], on_v)
                gemm_fm(g, WN, [[(6144 + h * 128, 128)]], on_g)
                g.tr_i = 0
                transposes_to_tok(g, KOF, tKOF, 1, KOT, tKO)
                transposes_to_tok(g, VF, tVF, 1, VT, tV)
                fw.release()
            with ExitStack() as ph3:
                dec_ap = lambda kt, c: dec[:, c:c + 1]
                recurrence(g, p, 1, 1, QIN, tQ, KIN, tK, KOT, tKO, VT, tV, OT, tOT, dec_ap,
                           g.st_hg[p, h], g.o_hg_p[h], g.o_hg_s[p, h], g.cr_hg[h], g.t_cr["hg"][h], ph3)
                fw.release()
            with ExitStack() as ph4:
                head_norm_gate(g, OT, tOT, 1, SG, tSG, h, False, lambda vt: g.hgnS[:, 0:1], ph4)
                fw.release()
    gemm_acc(g, "w_out_e", 1024, 8)
    fw.release()


def mixer_odd(g, p):
    fw, nc = g.fw, g.nc
    WN = "w_in_o"
    with ExitStack() as ph0:
        GK = g.SB("GK", [16, T], F32, ph0); tGK = Tile()
        wgk = g.SB("wgk", [16, 1024], F32, ph0); twgk = Tile()
        fw.dma(fw.sp, wgk[:], g.wgk2, writes=[twgk])

        def on_gk(c0, n, ps, tps):
            fw.op(fw.act, lambda: nc.scalar.copy(out=GK[:], in_=ps[0:16, 0:T]), reads=tps, writes=[tGK])
        gemm_fm(g, WN, [[(6144, 16)]], on_gk)
        for h in range(4):
            with ExitStack() as ph:
                OT = g.SB("OT", [128, 4, T], BF16, ph); tOT = [Tile() for _ in range(9)]
                with ExitStack() as phR:
                    SBr = lambda n, s, d=F32: g.SB(n, s, d, phR)
                    dec = SBr("dec", [128, 2, 9]); g.tdec = [Tile() for _ in range(9)]
                    QIN = SBr("QIN", [128, 2, T], BF16); tQ = [Tile() for _ in range(9)]
                    KIN = SBr("KIN", [128, 2, T], BF16); tK = [Tile() for _ in range(9)]
                    KOT = SBr("KOT", [128, 9, 256], BF16); tKO = [Tile() for _ in range(9)]
                    with ExitStack() as ph1:
                        SB1 = lambda n, s, d=F32: g.SB(n, s, d, ph1)
                        sm, tsm = smask_setup(g, ph1)
                        CS = SB1("CS", [128, 2, T]); tCS = Tile()
                        EE = SB1("EE", [128, T]); tEE = Tile()
                        KOF = SB1("KOF", [128, 2, T], BF16); tKOF = [Tile() for _ in range(9)]
                        for kt in range(2):
                            ps, tps = pa_set(g)
                            for ti, (t0, tn) in enumerate(NTT):
                                fw.op(fw.pe, lambda t0=t0, tn=tn, kt=kt, ps=ps: nc.tensor.matmul(ps[:, t0:t0 + tn], lhsT=wgk[0:16, h * 256 + kt * 128:h * 256 + (kt + 1) * 128], rhs=GK[0:16, t0:t0 + tn],
                                                                                             start=True, stop=True),
                                      reads=[tGK, twgk], writes=[tps[ti]], sig=(ti == 2))
                            j = h * 2 + kt
                            fw.op(fw.act, lambda kt=kt, j=j, ps=ps: nc.scalar.activation(out=CS[:, kt, :], in_=ps[:, 0:T], func=AF.Exp, scale=-1.0, bias=g.nbgk[:, j:j + 1]),
                                  reads=tps + [g.tmisc], writes=[tCS])
                            fw.op(fw.act, lambda kt=kt: nc.scalar.activation(out=CS[:, kt, :], in_=CS[:, kt, :], func=AF.Ln, bias=1.0), reads=[tCS], writes=[tCS])
                            fw.op(fw.dve, lambda kt=kt: nc.vector.tensor_tensor_scan(out=CS[:, kt, :], data0=sm[:], data1=CS[:, kt, :], initial=0.0, op0=ALU.mult, op1=ALU.add),
                                  reads=[tsm, tCS], writes=[tCS])
                            for c, (t0, cc) in enumerate(CH):
                                fw.op(fw.act, lambda kt=kt, c=c, t0=t0, cc=cc: nc.scalar.activation(out=dec[:, kt, c:c + 1], in_=CS[:, kt, t0 + cc - 1:t0 + cc], func=AF.Exp, scale=-1.0 / 16.0),
                                      reads=[tCS], writes=[g.tdec[c]])

                        def on_q(c0, n, ps, tps):
                            kt = (c0 - h * 256) // 128
                            fw.op(fw.act, lambda: nc.scalar.activation(out=EE[:], in_=CS[:, kt, :], func=AF.Exp, scale=-1.0 / 16.0), reads=[tCS], writes=[tEE])
                            fw.op(fw.dve, lambda: nc.vector.scalar_tensor_tensor(out=QIN[:, kt, :], in0=ps[:, 0:T], scalar=1.0 / 16.0, in1=EE[:], op0=ALU.mult, op1=ALU.mult),
                                  reads=tps + [tEE], writes=tQ)

                        def on_k(c0, n, ps, tps):
                            kt = (c0 - (1024 + h * 256)) // 128
                            fw.op(fw.act, lambda: nc.scalar.activation(out=EE[:], in_=CS[:, kt, :], func=AF.Exp, scale=1.0 / 16.0), reads=[tCS], writes=[tEE])
                            fw.op(fw.dve, lambda: nc.vector.tensor_tensor(out=EE[:], in0=EE[:], in1=ps[:, 0:T], op=ALU.mult), reads=tps + [tEE], writes=[tEE])
                            fw.op(fw.act, lambda: nc.scalar.copy(out=KIN[:, kt, :], in_=EE[:]), reads=[tEE], writes=tK)
                            for c, (t0, cc) in enumerate(CH):
                                fw.op(fw.dve, lambda c=c, t0=t0, cc=cc: nc.vector.tensor_scalar(out=KOF[:, kt, t0:t0 + cc], in0=EE[:, t0:t0 + cc], scalar1=dec[:, kt, c:c + 1], scalar2=None, op0=ALU.mult),
                                      reads=[tEE, g.tdec[c]], writes=[tKOF[c]])

                        gemm_fm(g, WN, [[(h * 256, 256)]], on_q)
                        gemm_fm(g, WN, [[(1024 + h * 256, 256)]], on_k)
                        g.tr_i = 0
                        transposes_to_tok(g, KOF, tKOF, 2, KOT, tKO)
                        fw.release()
                    VT = SBr("VT", [128, 9, 512], BF16); tV = [Tile() for _ in range(9)]
                    with ExitStack() as ph2:
                        VF = g.SB("VF", [128, 4, T], BF16, ph2); tVF = [Tile() for _ in range(9)]

                        def on_v(c0, n, ps, tps):
                            vt = (c0 - (2048 + h * 512)) // 128
                            fw.op(fw.act, lambda: nc.scalar.copy(out=VF[:, vt, :], in_=ps[:, 0:T]), reads=tps, writes=tVF)
                        gemm_fm(g, WN, [[(2048 + h * 512, 256)], [(2048 + h * 512 + 256, 256)]], on_v)
                        transposes_to_tok(g, VF, tVF, 4, VT, tV)
                        fw.release()
                    with ExitStack() as ph3:
                        dec_ap = lambda kt, c: dec[:, kt, c:c + 1]
                        recurrence(g, p, 2, 4, QIN, tQ, KIN, tK, KOT, tKO, VT, tV, OT, tOT, dec_ap,
                                   g.st_gla[p, h], g.o_gla_p[h], g.o_gla_s[p, h], g.cr_gla[h], g.t_cr["gla"][h], ph3)
                        fw.release()
                with ExitStack() as ph4:
                    SG = g.SB("SG", [128, 4, T], BF16, ph4); tSG = Tile()

                    def on_g(c0, n, ps, tps):
                        vt = (c0 - (4096 + h * 512)) // 128
                        fw.op(fw.act, lambda: nc.scalar.activation(out=SG[:, vt, :], in_=ps[:, 0:T], func=AF.Silu), reads=tps, writes=[tSG])
                    gemm_fm(g, WN, [[(4096 + h * 512, 256)], [(4096 + h * 512 + 256, 256)]], on_g)
                    head_norm_gate(g, OT, tOT, 4, SG, tSG, (h % 2) * 4, False, lambda vt: g.glanS[:, vt:vt + 1], ph4)
                    fw.release()
            if h % 2 == 1:
                gemm_acc(g, "w_out_o", (h - 1) * 512, 8)
                fw.release()
        fw.release()


def ffn(g, p, l):
    fw, nc = g.fw, g.nc
    with ExitStack() as ph:
        SBp = lambda n, s, d=F32: g.SB(n, s, d, ph)
        A = [SBp("A%d" % i, [128, 1092]) for i in range(2)]; tA = [Tile(), Tile()]
        C1 = [SBp("C1_%d" % i, [128, 1090]) for i in range(2)]; tC1 = [Tile(), Tile()]
        cvs = SBp("cvs", [128, NFT, 2]); tcvs = Tile()
        cvo = SBp("cvo", [128, NFT, 4]); tcvo = Tile()
        fw.dma(fw.sp, cvs[:], g.cv_in[p, l], writes=[tcvs])
        ft = 0
        while ft < NFT:
            nk = min(8, NFT - ft)
            for j in range(nk):
                f = ft + j
                i = f % 2
                cw = lambda k, f=f: g.convwS[:, l, f, k:k + 1]

                def on_tile(c0, n, ps, tps, f=f, i=i, j=j, cw=cw):
                    if c0 < DFF:
                        fw.op(fw.act, lambda: nc.scalar.copy(out=A[i][:, 2:1026], in_=ps[:, 0:1024]), reads=tps, writes=[tA[i]])
                        fw.op(fw.act, lambda: nc.scalar.copy(out=A[i][:, 1028:1092], in_=ps[:, 1024:1088]), reads=tps, writes=[tA[i]])
                        fw.op(fw.pool, lambda: nc.gpsimd.tensor_copy(out=A[i][:, 0:2], in_=g.cvcar[:, l, f, :]), reads=[g.tcvcar[l]], writes=[tA[i]])
                        fw.op(fw.pool, lambda: nc.gpsimd.tensor_copy(out=A[i][:, 1026:1028], in_=cvs[:, f, :]), reads=[tcvs], writes=[tA[i]])
                        fw.op(fw.pool, lambda: nc.gpsimd.tensor_copy(out=cvo[:, f, 0:2], in_=A[i][:, 1024:1026]), reads=[tA[i]], writes=[tcvo])
                        fw.op(fw.pool, lambda: nc.gpsimd.tensor_copy(out=cvo[:, f, 2:4], in_=A[i][:, 1090:1092]), reads=[tA[i]], writes=[tcvo])
                        fw.op(fw.pool, lambda: nc.gpsimd.tensor_copy(out=g.cvcar[:, l, f, :], in_=A[i][:, 1024:1026]), reads=[tA[i]], writes=[g.tcvcar[l]])
                        fw.op(fw.dve, lambda: nc.vector.tensor_scalar(out=C1[i][:], in0=A[i][:, 2:1092], scalar1=cw(2), scalar2=cw(3), op0=ALU.mult, op1=ALU.add),
                              reads=[tA[i], g.tconvw], writes=[tC1[i]])
                        fw.op(fw.dve, lambda: nc.vector.scalar_tensor_tensor(out=C1[i][:], in0=A[i][:, 1:1091], scalar=cw(1), in1=C1[i][:], op0=ALU.mult, op1=ALU.add),
                              reads=[tA[i], g.tconvw, tC1[i]], writes=[tC1[i]])
                        fw.op(fw.dve, lambda: nc.vector.scalar_tensor_tensor(out=C1[i][:], in0=A[i][:, 0:1090], scalar=cw(0), in1=C1[i][:], op0=ALU.mult, op1=ALU.add),
                              reads=[tA[i], g.tconvw, tC1[i]], writes=[tC1[i]])
                        fw.op(fw.act, lambda: nc.scalar.activation(out=C1[i][:], in_=C1[i][:], func=AF.Silu), reads=[tC1[i]], writes=[tC1[i]])
                    else:
                        fw.op(fw.dve, lambda: nc.vector.tensor_tensor(out=g.ZB[:, j, 0:1024], in0=C1[i][:, 0:1024], in1=ps[:, 0:1024], op=ALU.mult),
                              reads=tps + [tC1[i]], writes=[g.tZB[j]])
                        fw.op(fw.dve, lambda: nc.vector.tensor_tensor(out=g.ZB[:, j, 1024:1088], in0=C1[i][:, 1026:1090], in1=ps[:, 1024:1088], op=ALU.mult),
                              reads=tps + [tC1[i]], writes=[g.tZB[j]])
                gemm_fm(g, ("w_up", l), [[(f * 128, 128), (DFF + f * 128, 128)]], on_tile)
            gemm_acc(g, ("w_dn", l), ft * 128, nk)
            ft += nk
        fw.dma(fw.sp, g.o_cv[p, l], cvo[:], reads=[tcvo], is_output=True)
        fw.release()


def run_pass(g, p):
    fw, nc = g.fw, g.nc
    for kc in range(16):
        fw.dma(fw.sp if kc % 2 == 0 else fw.act, g.H[:, kc, :], g.xin[p, :, kc, :], writes=[g.tH[kc]])
    for l in range(2):
        rmsnorm(g, 2 * l)
        if l == 0:
            mixer_even(g, p)
        else:
            mixer_odd(g, p)
        rmsnorm(g, 2 * l + 1)
        ffn(g, p, l)
    rmsnorm(g, 4, final_out=True)
    for kc in range(16):
        fw.dma(fw.sp, g.yout[p, :, kc, :], g.H[:, kc, :], reads=[g.tH[kc]], is_output=True)
    fw.barrier()


_PROG = None


def _host_consts():
    f32 = np.float32
    half = 64
    inv = (f32(10000.0) ** (-(np.arange(half, dtype=f32) / f32(half)))).astype(f32)
    rope = np.zeros((NPASS, 2, 128, T), f32)
    for p in range(NPASS):
        pos = np.concatenate([p * 1024 + np.arange(1024), 1024 + np.arange(64)]).astype(f32)
        ang = (pos[:, None] * inv[None, :]).astype(f32)
        cs = np.cos(ang).astype(f32).T
        sn = np.sin(ang).astype(f32).T
        rope[p, 0] = np.concatenate([cs, cs], 0)
        rope[p, 1] = np.concatenate([sn, sn], 0)
    lg = np.log1p(-np.exp2(-5.0 - np.arange(4, dtype=np.float64)))
    tl = np.arange(128, dtype=np.float64)
    ret = np.zeros((128, 4, 2, 128), f32)
    ko = np.zeros((128, 4, 2), f32)
    dec = np.zeros((128, 4, 2), f32)
    for h in range(4):
        ret[:, h, 0, :] = np.exp(lg[h] * (tl + 1))[None, :]
        ret[:, h, 1, :] = (np.exp(-lg[h] * (tl + 1)) * 128.0 ** -0.5)[None, :]
        ko[:, h, 0] = np.exp(lg[h] * (127 - tl)) * 128.0 ** -0.5
        ko[:64, h, 1] = np.exp(lg[h] * (63 - tl[:64])) * 128.0 ** -0.5
        dec[:, h, 0] = np.exp(lg[h] * 128)
        dec[:, h, 1] = np.exp(lg[h] * 64)
    mat = np.zeros((4, 128, 128), f32)
    mat[0] = np.eye(128, dtype=f32)
    mat[1] = 1.0
    mat[2] = np.triu(np.ones((128, 128), f32))
    for m in range(64):
        mat[3][m + 64, m] = -1.0
        mat[3][m, m + 64] = 1.0
    sm = np.ones((128, T), f32)
    sm[:, 0:T:128] = 0.0
    return dict(c_rope=rope, c_ret=ret, c_ko=ko, c_dec=dec, c_mat=mat, c_smask=sm)


def kernel(x_prompt, x_sample, state_ret, state_hgrn, state_gla, cache_ffn_conv,
           norm_mix, norm_ffn, norm_final, w_in_even, w_out_even, hgrn_lb, hgrn_gnorm,
           w_in_odd, w_gk2, b_gk2, gla_gnorm, w_out_odd, ffn_w_up, ffn_conv_w, ffn_conv_b,
           ffn_w_down):
    global _PROG
    f32 = np.float32
    A = lambda a: np.ascontiguousarray(np.asarray(a, dtype=f32))
    x_prompt, x_sample = A(x_prompt), A(x_sample)
    state_ret, state_hgrn, state_gla, cache_ffn_conv = A(state_ret), A(state_hgrn), A(state_gla), A(cache_ffn_conv)
    if _PROG is None:
        _PROG = build_program()
    nc = _PROG
    consts = _host_consts()
    gam = np.stack([A(norm_mix)[0], A(norm_ffn)[0], A(norm_mix)[1], A(norm_ffn)[1], A(norm_final)], 0)
    gam = A(gam.reshape(5, 16, 128).transpose(2, 0, 1))
    cw = np.concatenate([A(ffn_conv_w), A(ffn_conv_b)[:, None, :]], 1)
    cw = A(cw.reshape(2, 4, NFT, 128).transpose(3, 0, 2, 1))
    shared = dict(
        gam=gam, convw=cw,
        lbraw=A(A(hgrn_lb).reshape(3, 8, 128).transpose(2, 0, 1)),
        hgn=A(A(hgrn_gnorm)[0][:, None]),
        wgk2=A(A(w_gk2)[0]),
        bgk=A(A(b_gk2)[0].reshape(8, 128).T),
        glan=A(A(gla_gnorm)[0].reshape(4, 128).T),
        w_in_e=A(w_in_even)[0], w_out_e=A(w_out_even)[0], w_in_o=A(w_in_odd)[0], w_out_o=A(w_out_odd)[0],
        w_up=A(ffn_w_up), w_dn=A(ffn_w_down),
    )
    shared.update(consts)
    USE = [0, 1, 4, 5]
    big = ("w_in_e", "w_out_e", "w_in_o", "w_out_o", "w_up", "w_dn")
    zshared = dict(shared)
    for k in big:
        zshared[k] = np.zeros_like(shared[k])
    zx = dict(xin=np.zeros((NPASS, 128, 16, T), f32), st_ret=np.zeros((NPASS, 4, 128, 256), f32),
              st_hg=np.zeros((NPASS, 8, 128, 128), f32), st_gla=np.zeros((NPASS, 4, 256, 512), f32),
              cv_in=np.zeros((NPASS, 2, 128, NFT, 2), f32))
    in_maps = []
    for core in range(8):
        if core not in USE:
            m = dict(zshared)
            m.update(zx)
            in_maps.append(m)
            continue
        b = USE.index(core)
        xin = np.zeros((NPASS, 128, 16, T), f32)
        st_r = np.zeros((NPASS, 4, 128, 256), f32)
        st_h = np.zeros((NPASS, 8, 128, 128), f32)
        st_g = np.zeros((NPASS, 4, 256, 512), f32)
        cv = np.zeros((NPASS, 2, 128, NFT, 2), f32)
        for p in range(NPASS):
            s = 2 * b + p
            X = np.concatenate([x_prompt[b, p * 1024:(p + 1) * 1024], x_sample[s]], 0)
            xin[p] = X.T.reshape(16, 128, T).transpose(1, 0, 2)
            st_r[p] = state_ret[0, s]
            st_h[p] = state_hgrn[0, s]
            st_g[p] = state_gla[0, s]
            cv[p] = cache_ffn_conv[:, s].reshape(2, 2, NFT, 128).transpose(0, 3, 2, 1)
        m = dict(shared)
        m.update(xin=xin, st_ret=st_r, st_hg=st_h, st_gla=st_g, cv_in=cv)
        in_maps.append(m)
    res = run_bass_kernel_spmd(nc, in_maps, core_ids=list(range(8)))
    R = res.results
    y_prompt = np.zeros((4, 2048, 2048), f32)
    y_sample = np.zeros((8, 64, 2048), f32)
    ret_p = np.zeros((1, 4, 4, 128, 256), f32); ret_s = np.zeros((1, 8, 4, 128, 256), f32)
    hg_p = np.zeros((1, 4, 8, 128, 128), f32); hg_s = np.zeros((1, 8, 8, 128, 128), f32)
    gla_p = np.zeros((1, 4, 4, 256, 512), f32); gla_s = np.zeros((1, 8, 4, 256, 512), f32)
    cv_p = np.zeros((2, 4, 2, DFF), f32); cv_s = np.zeros((2, 8, 2, DFF), f32)
    for b in range(4):
        r = R[USE[b]]
        yo = np.asarray(r["yout"])
        ocv = np.asarray(r["o_cv"])
        for p in range(NPASS):
            s = 2 * b + p
            Y = yo[p].transpose(2, 1, 0).reshape(T, 2048)
            y_prompt[b, p * 1024:(p + 1) * 1024] = Y[:1024]
            y_sample[s] = Y[1024:]
            ret_s[0, s] = np.asarray(r["o_ret_s"])[p]
            hg_s[0, s] = np.asarray(r["o_hg_s"])[p]
            gla_s[0, s] = np.asarray(r["o_gla_s"])[p]
            for l in range(2):
                cv_s[l, s] = ocv[p, l, :, :, 2:4].transpose(2, 1, 0).reshape(2, DFF)
        for l in range(2):
            cv_p[l, b] = ocv[1, l, :, :, 0:2].transpose(2, 1, 0).reshape(2, DFF)
        ret_p[0, b] = np.asarray(r["o_ret_p"])
        hg_p[0, b] = np.asarray(r["o_hg_p"])
        gla_p[0, b] = np.asarray(r["o_gla_p"])
    return (y_prompt, y_sample, ret_p, ret_s, hg_p, hg_s, gla_p, gla_s, cv_p, cv_s)
```
